# Optimizing a Trainium2 kernel written in Bass

```python
import jax
import jax.numpy as jnp
from jax import lax
import numpy as np

D_MODEL = 2048
BATCH = 4
SEQ = 8192
DEPTH = 1

CHUNK = 64
RET_HEADS = 4
RET_DK = 256
RET_DV = 256
GLA_HEADS = 4
GLA_DK = 128
GLA_DV = 256
GLA_GATE_RANK = 16
GLA_GATE_TAU = 16.0
ROPE_BASE = 10000.0
D_FF = 5632
CONV_WIDTH = 3
LN_EPS = 1e-5
DEEPNORM_ALPHA = (2.0 * DEPTH) ** 0.25
DEEPNORM_BETA = (8.0 * DEPTH) ** -0.25

RET_QK = RET_HEADS * RET_DK
RET_V = RET_HEADS * RET_DV
GLA_QK = GLA_HEADS * GLA_DK
GLA_V = GLA_HEADS * GLA_DV
MIX_WIDTH = RET_V + GLA_V
IN_SPLIT_SIZES = (RET_QK, RET_QK, RET_V, RET_V, GLA_QK, GLA_QK, GLA_V, GLA_V, GLA_GATE_RANK)
IN_WIDTH = 2 * RET_QK + 2 * RET_V + 2 * GLA_QK + 2 * GLA_V + GLA_GATE_RANK

kernel_name = 'hybrid_retention_gla_convffn_deepnorm'


def _layer_norm(x, g, b):
    xf = x.astype(jnp.float32)
    mu = jnp.mean(xf, axis=-1, keepdims=True)
    var = jnp.mean(jnp.square(xf - mu), axis=-1, keepdims=True)
    y = (xf - mu) * lax.rsqrt(var + LN_EPS) * g.astype(jnp.float32) + b.astype(jnp.float32)
    return y.astype(x.dtype)


def _head_norm(o, g, center):
    if center:
        o = o - jnp.mean(o, axis=-1, keepdims=True)
    o = o * lax.rsqrt(jnp.mean(jnp.square(o), axis=-1, keepdims=True) + LN_EPS)
    B, S, H, Dv = o.shape
    return o.reshape(B, S, H * Dv) * g.astype(jnp.float32)


def _rotary(x, pos):
    half = x.shape[-1] // 2
    inv_freq = ROPE_BASE ** (-jnp.arange(half, dtype=jnp.float32) / half)
    ang = pos[:, None] * inv_freq[None, :]
    cos = jnp.cos(ang)[None, :, None, :]
    sin = jnp.sin(ang)[None, :, None, :]
    x1, x2 = x[..., :half], x[..., half:]
    return jnp.concatenate([x1 * cos - x2 * sin, x1 * sin + x2 * cos], axis=-1)


def _to_chunks(t):
    B, S, H, D = t.shape
    return t.reshape(B, S // CHUNK, CHUNK, H, D).transpose(0, 3, 1, 2, 4)


def _from_chunks(t):
    B, H, N, C, D = t.shape
    return t.transpose(0, 2, 3, 1, 4).reshape(B, N * C, H, D)


def _chunk_scan(q_in, k_in, v, decay):
    B, H, N, C, Dk = q_in.shape
    Dv = v.shape[-1]

    def step(state, inp):
        q_n, k_n, v_n, d_n = inp
        o_n = jnp.einsum('bhcd,bhde->bhce', q_n, state)
        state = state * d_n[..., None] + jnp.einsum('bhcd,bhce->bhde', k_n, v_n)
        return state, o_n

    xs = (jnp.moveaxis(q_in, 2, 0), jnp.moveaxis(k_in, 2, 0),
          jnp.moveaxis(v, 2, 0), jnp.moveaxis(decay, 2, 0))
    init = jnp.zeros((B, H, Dk, Dv), q_in.dtype)
    _, o = lax.scan(step, init, xs)
    return jnp.moveaxis(o, 0, 2)


def _retention(q, k, v):
    B, S, H, Dk = q.shape
    N = S // CHUNK
    log_gamma = jnp.log1p(-jnp.exp2(-5.0 - jnp.arange(H, dtype=jnp.float32)))
    idx = jnp.arange(CHUNK, dtype=jnp.float32)
    intra_decay = jnp.exp(log_gamma[:, None, None] * jnp.abs(idx[:, None] - idx[None, :]))
    qc = _to_chunks(q)
    kc = _to_chunks(k) * (Dk ** -0.5)
    vc = _to_chunks(v)
    scores = jnp.einsum('bhncd,bhnmd->bhncm', qc, kc) * intra_decay[None, :, None]
    o_intra = jnp.einsum('bhncm,bhnme->bhnce', scores, vc)
    q_dec = jnp.exp(log_gamma[:, None] * (idx + 1.0))[None, :, None, :, None]
    k_dec = jnp.exp(log_gamma[:, None] * (CHUNK - 1.0 - idx))[None, :, None, :, None]
    state_dec = jnp.broadcast_to(jnp.exp(log_gamma * CHUNK)[None, :, None, None], (B, H, N, Dk))
    o_cross = _chunk_scan(qc * q_dec, kc * k_dec, vc, state_dec)
    return _from_chunks(o_intra + o_cross)


def _gla(q, k, v, log_a):
    Dk = q.shape[-1]
    qc = _to_chunks(q) * (Dk ** -0.5)
    kc = _to_chunks(k)
    vc = _to_chunks(v)
    b = jnp.cumsum(_to_chunks(log_a), axis=3)
    bc = b - b[:, :, :, CHUNK // 2:CHUNK // 2 + 1]
    e_pos = jnp.exp(bc)
    e_neg = jnp.exp(-bc)
    a_causal = jnp.einsum('bhncd,bhnmd->bhncm', qc * e_pos, kc * e_neg)
    a_anti = jnp.einsum('bhncd,bhnmd->bhncm', qc * e_neg, kc * e_pos)
    pos = jnp.arange(CHUNK)
    causal = pos[:, None] >= pos[None, :]
    scores = jnp.where(causal, a_causal, a_anti)
    o_intra = jnp.einsum('bhncm,bhnme->bhnce', scores, vc)
    b_last = b[:, :, :, -1:]
    o_cross = _chunk_scan(qc * jnp.exp(b), kc * jnp.exp(b_last - b), vc, jnp.exp(b_last[:, :, :, 0]))
    return _from_chunks(o_intra + o_cross)


def _mixer(x, w_in, w_gla_gate, b_gla_gate, g_ret, g_gla, w_out):
    B, S, _ = x.shape
    h = (x @ w_in).astype(jnp.float32)
    parts = []
    start = 0
    for size in IN_SPLIT_SIZES:
        parts.append(h[..., start:start + size])
        start += size
    q_r, k_r, v_r, z_r, q_g, k_g, v_g, z_g, a_g = parts
    pos = jnp.arange(S, dtype=jnp.float32)
    q_r = _rotary(q_r.reshape(B, S, RET_HEADS, RET_DK), pos)
    k_r = _rotary(k_r.reshape(B, S, RET_HEADS, RET_DK), pos)
    o_r = _retention(q_r, k_r, v_r.reshape(B, S, RET_HEADS, RET_DV))
    o_r = _head_norm(o_r, g_ret, True) * jax.nn.silu(z_r)
    log_a = jax.nn.log_sigmoid(a_g @ w_gla_gate.astype(jnp.float32)
                               + b_gla_gate.astype(jnp.float32)) / GLA_GATE_TAU
    o_g = _gla(q_g.reshape(B, S, GLA_HEADS, GLA_DK), k_g.reshape(B, S, GLA_HEADS, GLA_DK),
               v_g.reshape(B, S, GLA_HEADS, GLA_DV), log_a.reshape(B, S, GLA_HEADS, GLA_DK))
    o_g = _head_norm(o_g, g_gla, False) * jax.nn.silu(z_g)
    o = jnp.concatenate([o_r, o_g], axis=-1).astype(x.dtype)
    return o @ w_out


def _conv_ffn(x, w_up, w_conv, b_conv, w_down):
    u = x @ w_up
    u = lax.conv_general_dilated(u, w_conv[:, None, :], window_strides=(1,),
                                 padding=[(CONV_WIDTH - 1, 0)],
                                 dimension_numbers=('NWC', 'WIO', 'NWC'),
                                 feature_group_count=u.shape[-1]) + b_conv
    val, gate = jnp.split(u, 2, axis=-1)
    return (jax.nn.silu(gate) * val) @ w_down


def setup_inputs(seed: int = 0) -> dict:
    key = jax.random.key(seed)
    ks = jax.random.split(key, 16)
    nrm = jax.random.normal
    f32 = jnp.float32
    return {
        'x': nrm(ks[0], (BATCH, SEQ, D_MODEL), f32),
        'w_in': nrm(ks[1], (DEPTH, D_MODEL, IN_WIDTH), f32) * D_MODEL ** -0.5,
        'w_gla_gate': nrm(ks[2], (DEPTH, GLA_GATE_RANK, GLA_QK), f32) * GLA_GATE_RANK ** -0.5,
        'b_gla_gate': 0.1 * nrm(ks[3], (DEPTH, GLA_QK), f32),
        'g_ret': 1.0 + 0.02 * nrm(ks[4], (DEPTH, RET_V), f32),
        'g_gla': 1.0 + 0.02 * nrm(ks[5], (DEPTH, GLA_V), f32),
        'w_out': nrm(ks[6], (DEPTH, MIX_WIDTH, D_MODEL), f32) * MIX_WIDTH ** -0.5 * DEEPNORM_BETA,
        'ln1_g': 1.0 + 0.02 * nrm(ks[7], (DEPTH, D_MODEL), f32),
        'ln1_b': 0.02 * nrm(ks[8], (DEPTH, D_MODEL), f32),
        'w_up': nrm(ks[9], (DEPTH, D_MODEL, 2 * D_FF), f32) * D_MODEL ** -0.5,
        'w_conv': nrm(ks[10], (DEPTH, CONV_WIDTH, 2 * D_FF), f32) * CONV_WIDTH ** -0.5,
        'b_conv': 0.02 * nrm(ks[11], (DEPTH, 2 * D_FF), f32),
        'w_down': nrm(ks[12], (DEPTH, D_FF, D_MODEL), f32) * D_FF ** -0.5 * DEEPNORM_BETA,
        'ln2_g': 1.0 + 0.02 * nrm(ks[13], (DEPTH, D_MODEL), f32),
        'ln2_b': 0.02 * nrm(ks[14], (DEPTH, D_MODEL), f32),
    }


def reference(x, w_in, w_gla_gate, b_gla_gate, g_ret, g_gla, w_out, ln1_g, ln1_b,
              w_up, w_conv, b_conv, w_down, ln2_g, ln2_b):
    for layer in range(DEPTH):
        mix = _mixer(x, w_in[layer], w_gla_gate[layer], b_gla_gate[layer],
                     g_ret[layer], g_gla[layer], w_out[layer])
        x = _layer_norm(DEEPNORM_ALPHA * x + mix, ln1_g[layer], ln1_b[layer])
        ffn = _conv_ffn(x, w_up[layer], w_conv[layer], b_conv[layer], w_down[layer])
        x = _layer_norm(DEEPNORM_ALPHA * x + ffn, ln2_g[layer], ln2_b[layer])
    return x
```

```python
from contextlib import ExitStack

import numpy as np
import concourse.bass as bass
import concourse.mybir as mybir
from concourse.bass_utils import run_bass_kernel_spmd

F32 = mybir.dt.float32
BF16 = mybir.dt.bfloat16
AF = mybir.ActivationFunctionType
ALU = mybir.AluOpType

P = 128
D = 2048
T = 512
NB = 4
DFF = 5632
NFF = DFF // P
ALPHA = 2.0 ** 0.25
EPS = 1e-5
NSLOT = 6
WINDOW = 6
QR, KR, VR, ZR, QG, KG, VG, ZG, AG = 0, 1024, 2048, 3072, 4096, 4608, 5120, 6144, 7168


class Buf:
    __slots__ = ("name", "w", "r", "semval", "const")

    def __init__(self, name, const=False):
        self.name = name
        self.w = None
        self.r = []
        self.semval = 0
        self.const = const


class Op:
    __slots__ = ("eng", "fn", "deps", "signal", "ticket", "dbuf", "dval", "idx")


class Sched:
    ENGS = ("pe", "act", "dve", "pool", "sp")

    def __init__(self):
        self.q = {e: [] for e in self.ENGS}
        self.dma_since_barrier = []
        self.pending_barrier = {e: [] for e in self.ENGS}
        self.dma_bufs = []

    def op(self, eng, fn, reads=(), writes=(), dbuf=None):
        o = Op()
        o.eng = eng
        o.fn = fn
        o.signal = False
        o.ticket = 0
        o.dbuf = dbuf
        o.dval = 0
        o.idx = len(self.q[eng])
        if dbuf is not None:
            if dbuf.semval == 0 and dbuf not in self.dma_bufs:
                self.dma_bufs.append(dbuf)
            dbuf.semval += 16
            o.dval = dbuf.semval
        deps = []
        for b in reads:
            if b.w is not None:
                deps.append(b.w)
        for b in writes:
            if b.w is not None:
                deps.append(b.w)
            deps.extend(b.r)
        if self.pending_barrier[eng]:
            deps.extend(self.pending_barrier[eng])
            self.pending_barrier[eng] = []
        o.deps = []
        seen = set()
        for d in deps:
            if d is o or id(d) in seen:
                continue
            seen.add(id(d))
            if d.dbuf is None and d.eng == eng:
                if eng == "pe":
                    continue
                if o.idx - d.idx > WINDOW:
                    continue
            if d.dbuf is None:
                d.signal = True
            o.deps.append(d)
        for b in reads:
            if not b.const:
                b.r.append(o)
        for b in writes:
            b.w = o
            b.r = []
        self.q[eng].append(o)
        if dbuf is not None:
            self.dma_since_barrier.append(o)
        return o

    def barrier(self):
        lasts = []
        for e in ("pe", "act", "dve", "pool"):
            for o in reversed(self.q[e]):
                if o.dbuf is None:
                    lasts.append(o)
                    break
        lasts.extend(self.dma_since_barrier)
        self.dma_since_barrier = []
        for e in ("act", "dve", "pool", "sp"):
            self.pending_barrier[e] = self.pending_barrier[e] + list(lasts)

    def emit(self, nc, final_waits):
        for e in self.ENGS:
            c = 0
            for o in self.q[e]:
                if o.dbuf is None and o.signal:
                    c += 1
                    o.ticket = c
        with ExitStack() as es:
            esem = {e: es.enter_context(nc.semaphore("s_" + e)) for e in ("pe", "act", "dve", "pool", "sp")}
            dsem = {}
            for b in self.dma_bufs:
                dsem[id(b)] = es.enter_context(nc.semaphore("d_" + b.name))
            block = es.enter_context(nc.Block())

            def run(ename, eng):
                seen = {}
                for o in self.q[ename]:
                    need = {}
                    for d in o.deps:
                        if d.dbuf is not None:
                            sem, val, key = dsem[id(d.dbuf)], d.dval, id(d.dbuf)
                        else:
                            sem, val, key = esem[d.eng], d.ticket, d.eng
                        if key not in need or need[key][1] < val:
                            need[key] = (sem, val)
                    for key, (sem, val) in need.items():
                        if seen.get(key, 0) < val:
                            eng.wait_ge(sem, val)
                            seen[key] = val
                    ins = o.fn(eng)
                    if o.dbuf is not None:
                        ins.then_inc(dsem[id(o.dbuf)], 16)
                    elif o.signal:
                        ins.then_inc(esem[ename], 1)
                if ename == "sp":
                    for b in final_waits:
                        if seen.get(id(b), 0) < b.semval:
                            eng.wait_ge(dsem[id(b)], b.semval)

            @block.tensor
            def _(e):
                run("pe", e)

            @block.scalar
            def _(e):
                run("act", e)

            @block.vector
            def _(e):
                run("dve", e)

            @block.gpsimd
            def _(e):
                run("pool", e)

            @block.sync
            def _(e):
                run("sp", e)


def _fm_piece(W, col0):
    K = W.shape[0] // P
    return W[:, col0:col0 + P].reshape(K, P, P).transpose(1, 0, 2).reshape(P, K * P)


def _tm_piece(W, kq, col0):
    return W[kq * 512:(kq + 1) * 512, col0:col0 + 512].reshape(4, P, 512).transpose(1, 0, 2).reshape(P, 2048)


def piece_plan():
    pl = []
    for hp in range(2):
        for h in (2 * hp, 2 * hp + 1):
            for c in range(2):
                pl.append(("qr", "fm_in", QR + h * 256 + c * P))
        for h in (2 * hp, 2 * hp + 1):
            for c in range(2):
                pl.append(("kr", "fm_in", KR + h * 256 + c * P))
        for kq in range(4):
            pl.append(("vr", "tm_in", (kq, VR + hp * 512)))
        for kq in range(4):
            pl.append(("zr", "tm_in", (kq, ZR + hp * 512)))
    for hp in range(2):
        for h in (2 * hp, 2 * hp + 1):
            pl.append(("qg", "fm_in", QG + h * P))
        for h in (2 * hp, 2 * hp + 1):
            pl.append(("kg", "fm_in", KG + h * P))
        for kq in range(4):
            pl.append(("vg", "tm_in", (kq, VG + hp * 512)))
        for kq in range(4):
            pl.append(("zg", "tm_in", (kq, ZG + hp * 512)))
    for cg in range(4):
        for kq in range(4):
            pl.append(("wo", "tm_out", (kq, cg * 512)))
    for j in range(NFF):
        pl.append(("uv", "fm_up", j * P))
        pl.append(("ug", "fm_up", DFF + j * P))
    for cg in range(4):
        for kq in range(NFF // 4):
            pl.append(("wd", "tm_down", (kq, cg * 512)))
    return pl


PLAN = piece_plan()
NPIECE = len(PLAN)
STATE_TAGS = ("kr", "vr", "kg", "vg")


def build_pieces(w_in, w_out, w_up, w_down):
    out = np.empty((NPIECE, P, 2048), np.float32)
    for i, (tag, kind, a) in enumerate(PLAN):
        if kind == "fm_in":
            out[i] = _fm_piece(w_in, a)
        elif kind == "tm_in":
            out[i] = _tm_piece(w_in, a[0], a[1])
        elif kind == "tm_out":
            out[i] = _tm_piece(w_out, a[0], a[1])
        elif kind == "fm_up":
            out[i] = _fm_piece(w_up, a)
        else:
            out[i] = _tm_piece(w_down, a[0], a[1])
    return out


def host_consts():
    gam = 1.0 - 2.0 ** (-5.0 - np.arange(4, dtype=np.float64))
    idx = np.arange(P)
    c = {}
    c["ident"] = np.eye(P, dtype=np.float32)
    mr = np.zeros((P, 4, P), np.float64)
    for h in range(4):
        dist = np.abs(idx[None, :] - idx[:, None])
        ok = (idx[:, None] // 64) <= (idx[None, :] // 64)
        mr[:, h, :] = np.where(ok, gam[h] ** dist / 16.0, 0.0)
    c["maskR"] = mr.astype(np.float32).reshape(P, 4 * P)
    mc = (idx[:, None] <= idx[None, :]).astype(np.float32)
    ma = ((idx[:, None] > idx[None, :]) & ((idx[:, None] // 64) == (idx[None, :] // 64))).astype(np.float32)
    c["maskG"] = np.concatenate([mc, ma], axis=1)
    c["Lmat"] = np.where(idx[:, None] <= idx[None, :], -1.0 / 16.0, 0.0).astype(np.float32)
    gq = np.zeros((P, 4, P), np.float64)
    for h in range(4):
        gq[:, h, :] = (gam[h] ** (idx + 1.0))[None, :]
    c["gq"] = gq.astype(np.float32).reshape(P, 4 * P)
    gk = np.zeros((P, 4), np.float64)
    for h in range(4):
        gk[:, h] = gam[h] ** (127.0 - idx) / 16.0
    c["gk"] = gk.astype(np.float32)
    c["g128"] = [float(g ** 128.0) for g in gam]
    return c


def rope_tables(pos):
    half = 128
    inv_freq = (np.float32(10000.0) ** (-(np.arange(half, dtype=np.float32)) / np.float32(half))).astype(np.float32)
    ang = (pos.astype(np.float32)[None, :] * inv_freq[:, None]).astype(np.float32)
    a64 = ang.astype(np.float64)
    return np.cos(a64).astype(np.float32), np.sin(a64).astype(np.float32)


def build_program(modes, dbg=None):
    NT = len(modes)
    NTOK = NT * T
    n_out_tiles = sum(1 for m in modes if m == "full")
    first_full = modes.index("full")
    C = host_consts()
    g128 = C["g128"]

    nc = bass.Bass("TRN2", target_bir_lowering=False)
    x_d = nc.dram_tensor("x", [NTOK, D], F32, kind="ExternalInput").ap()
    cos_d = nc.dram_tensor("cos", [P, NTOK], F32, kind="ExternalInput").ap()
    sin_d = nc.dram_tensor("sin", [P, NTOK], F32, kind="ExternalInput").ap()
    wp_d = nc.dram_tensor("wp", [NPIECE, P, 2048], F32, kind="ExternalInput").ap()
    agw_d = nc.dram_tensor("agw", [P, 256], F32, kind="ExternalInput").ap()
    cst_d = nc.dram_tensor("cst", [P, 1796], F32, kind="ExternalInput").ap()
    rep_d = nc.dram_tensor("rep", [5, P, 2048], F32, kind="ExternalInput").ap()
    wconv_d = nc.dram_tensor("wconv", [P, 88 * 4], F32, kind="ExternalInput").ap()
    wg_d = nc.dram_tensor("wg", [16, 512], F32, kind="ExternalInput").ap()
    bg_d = nc.dram_tensor("bg", [1, 640], F32, kind="ExternalInput").ap()
    flag_d = nc.dram_tensor("flag", [P, 1], F32, kind="ExternalInput").ap()
    out_d = nc.dram_tensor("out", [n_out_tiles * T, D], F32, kind="ExternalOutput").ap()
    wb_d = nc.dram_tensor("wbf", [NPIECE, P, 2048], BF16, kind="Internal").ap()
    dbg_d = None
    if dbg is not None:
        dbg_d = nc.dram_tensor("dbg", list(dbg), F32, kind="ExternalOutput").ap()

    S = Sched()
    es = ExitStack()

    def sb(name, shape, dt):
        return es.enter_context(nc.sbuf_tensor("s_" + name, shape, dt))

    cst = sb("cst", [P, 1796], F32)
    ident_f = cst[:, 0:128]
    maskR = cst[:, 128:640]
    maskG = cst[:, 640:896]
    Lmat = cst[:, 896:1024]
    gq = cst[:, 1024:1536]
    gk = cst[:, 1536:1540]
    neghalf = cst[:, 1540:1796]
    ident_b = sb("identb", [P, P], BF16)
    rep = sb("rep", [P, 2048], F32)
    rep2 = sb("rep2", [P, 2048], F32)
    wconv = sb("wconv", [P, 88 * 4], F32)
    wg = sb("wg", [16, 512], F32)
    bg = sb("bg", [1, 640], F32)
    agw_f = sb("agwf", [P, 256], F32)
    agw = sb("agw", [P, 256], BF16)
    flag = sb("flag", [P, 1], F32)
    Sr = sb("Sr", [P, 4 * 512], F32)
    Srb = sb("Srb", [P, 4 * 512], BF16)
    Sg = sb("Sg", [P, 4 * 256], F32)
    Sgb = sb("Sgb", [P, 4 * 256], BF16)
    carry = sb("carry", [P, 88 * 2], F32)
    Y = sb("Y", [P, NB * D], F32)
    wring = sb("wring", [P, NSLOT * 2048], BF16)
    small = sb("small", [P, 64], F32)
    UB = 97 * 1024
    U = sb("U", [P, UB // 2], BF16)

    class Carver:
        def __init__(self):
            self.off = 0

        def take(self, nbytes, dt):
            a = self.off
            self.off += (nbytes + 31) // 32 * 32
            assert self.off <= UB, (self.off, UB)
            ap = U[:, a // 2:(a + nbytes) // 2]
            return ap if dt == BF16 else ap.bitcast(F32)

    cv = Carver()
    x1T = cv.take(16 * 512 * 2, BF16)
    actT = cv.take(NFF * 512 * 2, BF16)
    ue = [cv.take(514 * 4, F32) for _ in range(4)]
    acc = [cv.take(512 * 4, F32) for _ in range(4)]
    sgb = [cv.take(512 * 4, F32) for _ in range(2)]
    bd_end = cv.off
    cv = Carver()
    xT = cv.take(16 * 512 * 2, BF16)
    cosb = cv.take(512 * 4, F32)
    sinb = cv.take(512 * 4, F32)
    qkb = [cv.take(512 * 2, BF16) for _ in range(12)]
    rt = [cv.take(512 * 4, F32) for _ in range(3)]
    vpair = cv.take(NB * 512 * 2, BF16)
    gz = cv.take(NB * 512 * 4, F32)
    spb = cv.take(NB * 512 * 4, F32)
    agT = cv.take(512 * 4, F32)
    Eb = [cv.take(512 * 4, F32) for _ in range(4)]
    e1 = cv.take(512 * 4, F32)
    sTb = [cv.take(256 * 2, BF16) for _ in range(2)]
    ktok = [cv.take(256 * 2, BF16) for _ in range(2)]
    onb = [cv.take(256 * 4, F32) for _ in range(2)]
    S1 = [cv.take(256 * 4, F32) for _ in range(2)]
    opair = cv.take(NB * 512 * 2, BF16)
    cv.off = max(cv.off, bd_end)
    oT = cv.take(16 * 512 * 2, BF16)

    ps = [es.enter_context(nc.psum_tensor("ps%d" % i, [P, 512], F32)) for i in range(8)]

    def B(name, const=False):
        return Buf(name, const)

    b_cst = B("cst", True)
    b_rep = B("rep")
    b_rep2 = B("rep2")
    b_ps = [B("ps%d" % i) for i in range(8)]
    b_slot = [B("slot%d" % i) for i in range(NSLOT)]
    b_Y = [B("Y%d" % i) for i in range(NB)]
    b_xT = B("xT")
    b_x1T = B("x1T")
    b_oT = [B("oT%d" % i) for i in range(16)]
    b_actT = [B("actT%d" % i) for i in range(NFF)]
    b_rope = B("rope")
    b_qk = [B("qk%d" % i) for i in range(12)]
    b_rt = [B("rt%d" % i) for i in range(3)]
    b_vp = [B("vp%d" % i) for i in range(NB)]
    b_gz = [B("gz%d" % i) for i in range(NB)]
    b_sp = [B("sp%d" % i) for i in range(NB)]
    b_agT = B("agT")
    b_E = [B("E%d" % i) for i in range(4)]
    b_e1 = B("e1")
    b_sT = [B("sT%d" % i) for i in range(2)]
    b_kt = [B("kt%d" % i) for i in range(2)]
    b_on = [B("on%d" % i) for i in range(2)]
    b_S1 = [B("S1%d" % i) for i in range(2)]
    b_op = [B("op%d" % i) for i in range(NB)]
    b_Sr = [B("Sr%d" % i) for i in range(4)]
    b_Srb = [B("Srb%d" % i) for i in range(4)]
    b_Sg = [B("Sg%d" % i) for i in range(4)]
    b_Sgb = [B("Sgb%d" % i) for i in range(4)]
    b_carry = B("carry")
    b_ue = [B("ue%d" % i) for i in range(4)]
    b_acc = [B("acc%d" % i) for i in range(4)]
    b_sg = [B("sg%d" % i) for i in range(2)]
    b_small = [B("small%d" % i) for i in range(8)]
    b_wb = B("wbf")
    b_dbg = B("dbg")

    psn = [0]

    def bank():
        i = psn[0] % 8
        psn[0] += 1
        return ps[i], b_ps[i]

    def banks4():
        while psn[0] % 4 != 0:
            psn[0] += 1
        return [bank() for _ in range(4)]

    def mm(out, lhsT, rhs, start, stop, reads, writes):
        S.op("pe", lambda e: e.matmul(out, lhsT, rhs, start=start, stop=stop), reads, writes)

    def tp(out, in_, ident, reads, writes):
        S.op("pe", lambda e: e.transpose(out, in_, ident), reads + [b_cst], writes)

    def act(out, in_, func, reads, writes, scale=1.0, bias=0.0):
        S.op("act", lambda e: e.activation(out, in_, func, bias=bias, scale=scale), reads, writes)

    def tt(eng, out, a, b, op, reads, writes):
        S.op(eng, lambda e: e.tensor_tensor(out, a, b, op), reads, writes)

    def ts(eng, out, a, s1, s2, op0, op1, reads, writes):
        if s2 is None:
            S.op(eng, lambda e: e.tensor_scalar(out, a, s1, None, op0), reads, writes)
        else:
            S.op(eng, lambda e: e.tensor_scalar(out, a, s1, s2, op0, op1), reads, writes)

    def stt(out, a, s, b, op0, op1, reads, writes):
        S.op("dve", lambda e: e.scalar_tensor_tensor(out, a, s, b, op0, op1), reads, writes)

    def cp(eng, out, in_, reads, writes):
        S.op(eng, lambda e: e.tensor_copy(out, in_), reads, writes)

    def dma(eng, out, in_, reads, writes, dbuf):
        return S.op(eng, lambda e: e.dma_start(out=out, in_=in_), reads, writes, dbuf)

    b_cl = B("cload")
    for (dst, src) in ((cst[:, :], cst_d), (wconv[:, :], wconv_d), (wg[:, :], wg_d), (bg[:, :], bg_d),
                       (agw_f[:, :], agw_d), (flag[:, :], flag_d)):
        dma("sp", dst, src, [], [b_cst], b_cl)
    S.op("dve", lambda e: e.tensor_copy(ident_b[:, :], ident_f), [b_cst], [b_cst])
    S.op("dve", lambda e: e.tensor_copy(agw[:, :], agw_f[:, :]), [b_cst], [b_cst])
    for t_, n_ in ((Sr, 2048), (Sg, 1024), (carry, 176)):
        S.op("pool", (lambda t_: (lambda e: e.memset(t_[:, :], 0.0)))(t_), [], [b_cst])
    for t_ in (Srb, Sgb):
        S.op("pool", (lambda t_: (lambda e: e.memset(t_[:, :], 0.0)))(t_), [], [b_cst])
    for i in range(NPIECE):
        S.op("pool", (lambda i: (lambda e: e.dma_start(out=wb_d[i], in_=wp_d[i])))(i), [], [], b_wb)
    S.op("sp", lambda e: e.dma_start(out=flag[:, :], in_=flag_d), [b_wb, b_cst], [b_cst], b_cl)
    S.barrier()

    stream = []
    for ti, m in enumerate(modes):
        for pi, (tag, kind, a) in enumerate(PLAN):
            if m == "state" and tag not in STATE_TAGS:
                continue
            if m == "halo" and tag == "wd":
                continue
            stream.append((ti, pi, tag))
    wpos = {"issued": 0, "used": 0}

    def issue_loads(upto):
        while wpos["issued"] < min(upto, len(stream)):
            k = wpos["issued"]
            _, pi, _ = stream[k]
            sl = k % NSLOT
            dma("sp", wring[:, sl * 2048:(sl + 1) * 2048], wb_d[pi], [], [b_slot[sl]], b_slot[sl])
            wpos["issued"] += 1

    def next_piece(ti, tag):
        k = wpos["used"]
        assert stream[k][0] == ti and stream[k][2] == tag, (stream[k], ti, tag)
        issue_loads(k + NSLOT)
        wpos["used"] += 1
        sl = k % NSLOT
        return wring[:, sl * 2048:(sl + 1) * 2048], b_slot[sl]

    def fm_group(ti, tag, src, b_src, M=P):
        w, bw = next_piece(ti, tag)
        pb, bb = bank()
        for k in range(16):
            mm(pb[0:M, :], w[:, k * P:k * P + M], src[:, k * 512:(k + 1) * 512], k == 0, k == 15,
               [bw, b_src], [bb])
        return pb, bb

    def tm_group(ti, tag, nk, lhs_fn):
        bks = banks4()
        for kq in range(nk // 4):
            w, bw = next_piece(ti, tag)
            for kk in range(4):
                k = kq * 4 + kk
                for blk in range(NB):
                    l, bl = lhs_fn(k, blk)
                    mm(bks[blk][0][:, :], l, w[:, kk * 512:(kk + 1) * 512], k == 0, k == nk - 1,
                       [bw, bl], [bks[blk][1]])
        return bks

    def x3(ap, a):
        return ap.rearrange("p (a b) -> p a b", a=a)

    out_tile = [0]

    def transposes_to(dstT, b_dst, nblk=NB):
        d3 = x3(dstT, 16)
        for blk in range(nblk):
            for j in range(4):
                pb, bb = bank()
                for c in range(4):
                    fc = 4 * j + c
                    tp(pb[:, c * P:(c + 1) * P], Y[:, blk * D + fc * P: blk * D + (fc + 1) * P], ident_f,
                       [b_Y[blk]], [bb])
                eng_copy = "act" if (j % 2 == 0) else "dve"
                o_ap = d3[:, 4 * j:4 * j + 4, blk * P:(blk + 1) * P]
                i_ap = x3(pb[:, :], 4)
                if eng_copy == "act":
                    act(o_ap, i_ap, AF.Copy, [bb], [b_dst])
                else:
                    S.op("dve", (lambda o_ap, i_ap: (lambda e: e.tensor_copy(o_ap, i_ap)))(o_ap, i_ap), [bb], [b_dst])

    def layer_norm(blk, si):
        st = small[:, 0:24]
        mv = small[:, 24:26]
        rs = small[:, 26:27]
        yb = Y[:, blk * D:(blk + 1) * D]
        for c in range(4):
            S.op("dve", (lambda c: (lambda e: e.bn_stats(st[:, c * 6:(c + 1) * 6], yb[:, c * 512:(c + 1) * 512])))(c),
                 [b_Y[blk]], [b_small[0]])
        S.op("dve", lambda e: e.bn_aggr(mv, st), [b_small[0]], [b_small[1]])
        ts("pool", rs, mv[:, 1:2], EPS, None, ALU.add, None, [b_small[1]], [b_small[2]])
        tt("pool", rs, rs, neghalf[:, 0:1], ALU.pow, [b_small[2], b_cst], [b_small[2]])
        ts("dve", yb, yb, mv[:, 0:1], rs, ALU.subtract, ALU.mult, [b_Y[blk], b_small[1], b_small[2]], [b_Y[blk]])
        tt("dve", yb, yb, rep[:, :], ALU.mult, [b_Y[blk], b_rep], [b_Y[blk]])
        tt("pool", yb, yb, rep2[:, :], ALU.add, [b_Y[blk], b_rep2], [b_Y[blk]])

    def head_norm(pb, bb, hl, blk, center, si):
        st = small[:, 32 + si * 16: 32 + si * 16 + 6]
        mv = small[:, 32 + si * 16 + 6: 32 + si * 16 + 8]
        rs = small[:, 32 + si * 16 + 8: 32 + si * 16 + 9]
        m2 = small[:, 32 + si * 16 + 9: 32 + si * 16 + 10]
        bs = b_small[3 + si]
        S.op("dve", lambda e: e.bn_stats(st, pb[:, 0:256]), [bb], [bs])
        S.op("dve", lambda e: e.bn_aggr(mv, st), [bs], [bs])
        if center:
            ts("pool", rs, mv[:, 1:2], EPS, None, ALU.add, None, [bs], [bs])
        else:
            tt("pool", m2, mv[:, 0:1], mv[:, 0:1], ALU.mult, [bs], [bs])
            ts("pool", rs, m2, EPS, mv[:, 1:2], ALU.add, ALU.add, [bs], [bs])
        tt("pool", rs, rs, neghalf[:, 0:1], ALU.pow, [bs, b_cst], [bs])
        on = onb[si]
        if center:
            ts("dve", on, pb[:, 0:256], mv[:, 0:1], rs, ALU.subtract, ALU.mult, [bb, bs], [b_on[si]])
        else:
            ts("dve", on, pb[:, 0:256], rs, None, ALU.mult, None, [bb, bs], [b_on[si]])
        g3 = x3(gz, NB)
        o3 = x3(opair, NB)
        tt("pool", o3[:, blk, hl * 256:(hl + 1) * 256], on, g3[:, blk, hl * 256:(hl + 1) * 256], ALU.mult,
           [b_on[si], b_gz[blk]], [b_op[blk]])

    def vz_groups(ti, vtag, ztag, gcol, full):
        xT3 = x3(xT, 16)

        def lhs(k, blk):
            return xT3[:, k, blk * P:(blk + 1) * P], b_xT
        bks = tm_group(ti, vtag, 16, lhs)
        v3 = x3(vpair, NB)
        for blk in range(NB):
            if blk % 2 == 0:
                act(v3[:, blk, :], bks[blk][0][:, :], AF.Copy, [bks[blk][1]], [b_vp[blk]])
            else:
                cp("dve", v3[:, blk, :], bks[blk][0][:, :], [bks[blk][1]], [b_vp[blk]])
        if full:
            bks = tm_group(ti, ztag, 16, lhs)
            g3 = x3(gz, NB)
            for blk in range(NB):
                act(g3[:, blk, :], bks[blk][0][:, :], AF.Silu, [bks[blk][1]], [b_gz[blk]])
                tt("pool", g3[:, blk, :], g3[:, blk, :], rep[:, gcol:gcol + 512], ALU.mult,
                   [b_gz[blk], b_rep], [b_gz[blk]])

    def opair_to_oT(fc0):
        o3 = x3(opair, NB)
        oT3 = x3(oT, 16)
        for c in range(4):
            pb, bb = bank()
            pbb = pb[:, :].bitcast(BF16)
            for blk in range(NB):
                tp(pbb[:, blk * P:(blk + 1) * P], o3[:, blk, c * P:(c + 1) * P], ident_b[:, :], [b_op[blk]], [bb])
            if c % 2 == 0:
                act(oT3[:, fc0 + c, :], pbb[:, 0:512], AF.Copy, [bb], [b_oT[fc0 + c]])
            else:
                S.op("dve", (lambda c, pbb: (lambda e: e.tensor_copy(oT3[:, fc0 + c, :], pbb[:, 0:512])))(c, pbb),
                     [bb], [b_oT[fc0 + c]])

    def do_tile(ti, mode):
        full = mode != "state"
        tok0 = ti * T
        S.barrier()
        Y3 = x3(Y, NB)
        for blk in range(NB):
            dma("sp", Y[:, blk * D:(blk + 1) * D], x_d[tok0 + blk * P: tok0 + (blk + 1) * P, :], [], [b_Y[blk]], b_Y[blk])
        dma("sp", cosb, cos_d[:, tok0:tok0 + T], [], [b_rope], b_rope)
        dma("sp", sinb, sin_d[:, tok0:tok0 + T], [], [b_rope], b_rope)
        dma("sp", rep[:, :], rep_d[0], [], [b_rep], b_rep)
        transposes_to(xT, b_xT)

        pb, bb = bank()
        for k in range(16):
            mm(pb[0:16, :], agw[:, k * 16:(k + 1) * 16], xT[:, k * 512:(k + 1) * 512], k == 0, k == 15,
               [b_cst, b_xT], [bb])
        act(agT[0:16, :], pb[0:16, :], AF.Copy, [bb], [b_agT])
        sp3 = x3(spb, NB)
        for blk in range(NB):
            pb, bb = bank()
            mm(pb[:, :], agT[0:16, blk * P:(blk + 1) * P], wg[:, :], True, False, [b_agT, b_cst], [bb])
            mm(pb[:, :], bg[0:1, 512:640], bg[0:1, 0:512], False, True, [b_cst], [bb])
            act(e1, pb[:, :], AF.Exp, [bb], [b_e1], scale=-1.0)
            act(sp3[:, blk, :], e1, AF.Ln, [b_e1], [b_sp[blk]], bias=1.0)

        for hp in range(2):
            qb = {}
            for which in (("qr",) if full else ()) + ("kr",):
                if which == "qr":
                    pass
                for hl in range(2):
                    h = 2 * hp + hl
                    pb0, bb0 = fm_group(ti, which, xT, b_xT)
                    pb1, bb1 = fm_group(ti, which, xT, b_xT)
                    x1_, x2_ = pb0[:, :], pb1[:, :]
                    if which == "qr":
                        i0 = hl * 4
                        q0, q1, qd0, qd1 = qkb[i0], qkb[i0 + 1], qkb[i0 + 2], qkb[i0 + 3]
                        bq = b_qk[i0:i0 + 4]
                        gqb = gq[:, h * P:(h + 1) * P].unsqueeze(1).broadcast_to([P, NB, P])
                        tt("dve", rt[0], x1_, cosb, ALU.mult, [bb0, b_rope], [b_rt[0]])
                        tt("dve", rt[1], x2_, sinb, ALU.mult, [bb1, b_rope], [b_rt[1]])
                        tt("dve", rt[2], rt[0], rt[1], ALU.subtract, [b_rt[0], b_rt[1]], [b_rt[2]])
                        act(q0, rt[2], AF.Copy, [b_rt[2]], [bq[0]])
                        tt("dve", x3(qd0, NB), x3(rt[2], NB), gqb, ALU.mult, [b_rt[2], b_cst], [bq[2]])
                        tt("dve", rt[0], x1_, sinb, ALU.mult, [bb0, b_rope], [b_rt[0]])
                        tt("dve", rt[1], x2_, cosb, ALU.mult, [bb1, b_rope], [b_rt[1]])
                        tt("dve", rt[2], rt[0], rt[1], ALU.add, [b_rt[0], b_rt[1]], [b_rt[2]])
                        act(q1, rt[2], AF.Copy, [b_rt[2]], [bq[1]])
                        tt("dve", x3(qd1, NB), x3(rt[2], NB), gqb, ALU.mult, [b_rt[2], b_cst], [bq[3]])
                    else:
                        i0 = 8 + hl * 2
                        k0, k1 = qkb[i0], qkb[i0 + 1]
                        tt("dve", rt[0], x1_, cosb, ALU.mult, [bb0, b_rope], [b_rt[0]])
                        tt("dve", rt[1], x2_, sinb, ALU.mult, [bb1, b_rope], [b_rt[1]])
                        tt("dve", k0, rt[0], rt[1], ALU.subtract, [b_rt[0], b_rt[1]], [b_qk[i0]])
                        tt("dve", rt[0], x1_, sinb, ALU.mult, [bb0, b_rope], [b_rt[0]])
                        tt("dve", rt[1], x2_, cosb, ALU.mult, [bb1, b_rope], [b_rt[1]])
                        tt("dve", k1, rt[0], rt[1], ALU.add, [b_rt[0], b_rt[1]], [b_qk[i0 + 1]])
            vz_groups(ti, "vr", "zr", hp * 512, full)
            v3 = x3(vpair, NB)
            for blk in range(NB):
                for hl in range(2):
                    h = 2 * hp + hl
                    k0, k1 = qkb[8 + hl * 2], qkb[9 + hl * 2]
                    bk = [b_qk[8 + hl * 2], b_qk[9 + hl * 2]]
                    vv = v3[:, blk, hl * 256:(hl + 1) * 256]
                    cs = slice(blk * P, (blk + 1) * P)
                    if full:
                        q0, q1, qd0, qd1 = qkb[hl * 4:hl * 4 + 4]
                        bq = b_qk[hl * 4:hl * 4 + 4]
                        pbs, bbs = bank()
                        mm(pbs[:, 0:P], k0[:, cs], q0[:, cs], True, False, [bk[0], bq[0]], [bbs])
                        mm(pbs[:, 0:P], k1[:, cs], q1[:, cs], False, True, [bk[1], bq[1]], [bbs])
                        tt("dve", sTb[hl][:, 0:P], pbs[:, 0:P], maskR[:, h * P:(h + 1) * P], ALU.mult,
                           [bbs, b_cst], [b_sT[hl]])
                    pbk, bbk = bank()
                    pkb = pbk[:, :].bitcast(BF16)
                    tp(pkb[:, 0:P], k0[:, cs], ident_b[:, :], [bk[0]], [bbk])
                    tp(pkb[:, P:2 * P], k1[:, cs], ident_b[:, :], [bk[1]], [bbk])
                    act(ktok[hl], pkb[:, 0:256], AF.Identity, [bbk, b_cst], [b_kt[hl]], scale=gk[:, h:h + 1])
                    if full:
                        pbo, bbo = bank()
                        mm(pbo[:, 0:256], sTb[hl][:, 0:P], vv, True, False, [b_sT[hl], b_vp[blk]], [bbo])
                        mm(pbo[:, 0:256], qd0[:, cs], Srb[:, h * 512:h * 512 + 256], False, False,
                           [bq[2], b_Srb[h]], [bbo])
                        mm(pbo[:, 0:256], qd1[:, cs], Srb[:, h * 512 + 256:h * 512 + 512], False, True,
                           [bq[3], b_Srb[h]], [bbo])
                    pbd, bbd = bank()
                    mm(pbd[:, 0:256], ktok[hl][:, 0:P], vv, True, True, [b_kt[hl], b_vp[blk]], [bbd])
                    mm(pbd[:, 256:512], ktok[hl][:, P:2 * P], vv, True, True, [b_kt[hl], b_vp[blk]], [bbd])
                    sr = Sr[:, h * 512:(h + 1) * 512]
                    stt(sr, sr, g128[h], pbd[:, :], ALU.mult, ALU.add, [b_Sr[h], bbd], [b_Sr[h]])
                    act(Srb[:, h * 512:(h + 1) * 512], sr, AF.Copy, [b_Sr[h]], [b_Srb[h]])
                    if full:
                        head_norm(pbo, bbo, hl, blk, True, hl)
            if full:
                opair_to_oT(hp * 4)

        for hp in range(2):
            qps = {}
            for which in (("qg",) if full else ()) + ("kg",):
                for hl in range(2):
                    qps[(which, hl)] = fm_group(ti, which, xT, b_xT)
            for hl in range(2):
                h = 2 * hp + hl
                pbB, bbB = bank()
                for blk in range(NB):
                    mm(pbB[:, blk * P:(blk + 1) * P], sp3[:, blk, h * P:(h + 1) * P], Lmat, True, True,
                       [b_sp[blk], b_cst], [bbB])
                Ep, En = Eb[hl * 2], Eb[hl * 2 + 1]
                act(Ep, pbB[:, :], AF.Exp, [bbB], [b_E[hl * 2]])
                act(En, pbB[:, :], AF.Exp, [bbB], [b_E[hl * 2 + 1]], scale=-1.0)
                i0 = hl * 4
                qP, qN, kP, kN = qkb[i0], qkb[i0 + 1], qkb[i0 + 2], qkb[i0 + 3]
                sc = 128.0 ** -0.5
                if full:
                    pq, bq_ = qps[("qg", hl)]
                    stt(qP, pq[:, :], sc, Ep, ALU.mult, ALU.mult, [bq_, b_E[hl * 2]], [b_qk[i0]])
                    stt(qN, pq[:, :], sc, En, ALU.mult, ALU.mult, [bq_, b_E[hl * 2 + 1]], [b_qk[i0 + 1]])
                pk, bk_ = qps[("kg", hl)]
                if full:
                    tt("dve", kP, pk[:, :], Ep, ALU.mult, [bk_, b_E[hl * 2]], [b_qk[i0 + 2]])
                tt("dve", kN, pk[:, :], En, ALU.mult, [bk_, b_E[hl * 2 + 1]], [b_qk[i0 + 3]])
            vz_groups(ti, "vg", "zg", 1024 + hp * 512, full)
            v3 = x3(vpair, NB)
            for blk in range(NB):
                for hl in range(2):
                    h = 2 * hp + hl
                    i0 = hl * 4
                    qP, qN, kP, kN = qkb[i0], qkb[i0 + 1], qkb[i0 + 2], qkb[i0 + 3]
                    bqP, bqN, bkP, bkN = b_qk[i0:i0 + 4]
                    Ep = Eb[hl * 2]
                    vv = v3[:, blk, hl * 256:(hl + 1) * 256]
                    cs = slice(blk * P, (blk + 1) * P)
                    if full:
                        pbs, bbs = bank()
                        mm(pbs[:, 0:P], kN[:, cs], qP[:, cs], True, True, [bkN, bqP], [bbs])
                        mm(pbs[:, P:2 * P], kP[:, cs], qN[:, cs], True, True, [bkP, bqN], [bbs])
                        tt("dve", sTb[hl], pbs[:, 0:256], maskG, ALU.mult, [bbs, b_cst], [b_sT[hl]])
                    pbk, bbk = bank()
                    pkb = pbk[:, :].bitcast(BF16)
                    tp(pkb[:, 0:P], kN[:, cs], ident_b[:, :], [bkN], [bbk])
                    act(ktok[hl][:, 0:P], pkb[:, 0:P], AF.Copy, [bbk], [b_kt[hl]])
                    sg_ = Sg[:, h * 256:(h + 1) * 256]
                    e127 = Ep[:, blk * P + 127: blk * P + 128]
                    if full:
                        pbo, bbo = bank()
                        mm(pbo[:, 0:256], sTb[hl][:, 0:P], vv, True, False, [b_sT[hl], b_vp[blk]], [bbo])
                        mm(pbo[:, 0:256], sTb[hl][:, P:2 * P], vv, False, False, [b_sT[hl], b_vp[blk]], [bbo])
                        mm(pbo[:, 0:256], qP[:, cs], Sgb[:, h * 256:(h + 1) * 256], False, True,
                           [bqP, b_Sgb[h]], [bbo])
                    pbd, bbd = bank()
                    mm(pbd[:, 0:256], ktok[hl][:, 0:P], vv, True, True, [b_kt[hl], b_vp[blk]], [bbd])
                    ts("dve", S1[hl], sg_, e127, None, ALU.mult, None, [b_Sg[h], b_E[hl * 2]], [b_S1[hl]])
                    stt(sg_, pbd[:, 0:256], e127, S1[hl], ALU.mult, ALU.add, [bbd, b_E[hl * 2], b_S1[hl], b_Sg[h]],
                        [b_Sg[h]])
                    act(Sgb[:, h * 256:(h + 1) * 256], sg_, AF.Copy, [b_Sg[h]], [b_Sgb[h]])
                    if full:
                        head_norm(pbo, bbo, hl, blk, False, hl)
            if full:
                opair_to_oT(8 + hp * 4)
        if not full:
            return

        dma("sp", rep[:, :], rep_d[1], [], [b_rep], b_rep)
        dma("sp", rep2[:, :], rep_d[2], [], [b_rep2], b_rep2)
        oT3 = x3(oT, 16)
        for cg in range(4):
            def lhs(k, blk):
                return oT3[:, k, blk * P:(blk + 1) * P], b_oT[k]
            bks = tm_group(ti, "wo", 16, lhs)
            for blk in range(NB):
                yb = Y[:, blk * D + cg * 512: blk * D + (cg + 1) * 512]
                stt(yb, yb, ALPHA, bks[blk][0][:, :], ALU.mult, ALU.add, [b_Y[blk], bks[blk][1]], [b_Y[blk]])
        S.barrier()
        for blk in range(NB):
            layer_norm(blk, 0)
        transposes_to(x1T, b_x1T)

        if ti == first_full:
            ts("pool", carry[:, :], carry[:, :], flag[:, 0:1], None, ALU.mult, None, [b_carry, b_cst], [b_carry])
        a3 = x3(actT, NFF)
        wc3 = x3(wconv[:, :], 88)
        c3 = x3(carry[:, :], 88)
        for j in range(NFF):
            res = []
            for half_, tag in ((0, "uv"), (1, "ug")):
                ch = j + half_ * NFF
                pb, bb = fm_group(ti, tag, x1T, b_x1T)
                wi = (2 * j + half_) % 4
                u, bu = ue[wi], b_ue[wi]
                a_, ba = acc[wi], b_acc[wi]
                S.op("pool", (lambda u, ch: (lambda e: e.tensor_copy(u[:, 0:2], c3[:, ch, :])))(u, ch),
                     [b_carry], [bu])
                act(u[:, 2:514], pb[:, :], AF.Copy, [bb], [bu])
                S.op("pool", (lambda u, ch: (lambda e: e.tensor_copy(c3[:, ch, :], u[:, 512:514])))(u, ch),
                     [bu], [b_carry])
                act(a_, pb[:, :], AF.Identity, [bb, b_cst], [ba], scale=wc3[:, ch, 2:3], bias=wc3[:, ch, 3:4])
                stt(a_, u[:, 1:513], wc3[:, ch, 1:2], a_, ALU.mult, ALU.add, [bu, ba, b_cst], [ba])
                stt(a_, u[:, 0:512], wc3[:, ch, 0:1], a_, ALU.mult, ALU.add, [bu, ba, b_cst], [ba])
                res.append((a_, ba))
            if mode == "full":
                s_, bs_ = sgb[j % 2], b_sg[j % 2]
                act(s_, res[1][0], AF.Silu, [res[1][1]], [bs_])
                tt("dve", a3[:, j, :], s_, res[0][0], ALU.mult, [bs_, res[0][1]], [b_actT[j]])
        if mode != "full":
            return

        dma("sp", rep[:, :], rep_d[3], [], [b_rep], b_rep)
        dma("sp", rep2[:, :], rep_d[4], [], [b_rep2], b_rep2)
        for cg in range(4):
            def lhs(k, blk):
                return a3[:, k, blk * P:(blk + 1) * P], b_actT[k]
            bks = tm_group(ti, "wd", NFF, lhs)
            for blk in range(NB):
                yb = Y[:, blk * D + cg * 512: blk * D + (cg + 1) * 512]
                stt(yb, yb, ALPHA, bks[blk][0][:, :], ALU.mult, ALU.add, [b_Y[blk], bks[blk][1]], [b_Y[blk]])
        ot = out_tile[0]
        out_tile[0] += 1
        for blk in range(NB):
            layer_norm(blk, 0)
            dma("sp", out_d[ot * T + blk * P: ot * T + (blk + 1) * P, :], Y[:, blk * D:(blk + 1) * D],
                [b_Y[blk]], [], b_Y[blk])

    for ti, m in enumerate(modes):
        do_tile(ti, m)

    S.emit(nc, final_waits=b_Y)
    es.close()
    return nc


def make_shared_inputs(w_in, w_gla_gate, b_gla_gate, g_ret, g_gla, w_out, ln1_g, ln1_b,
                       w_up, w_conv, b_conv, w_down, ln2_g, ln2_b):
    C = host_consts()
    w_in = np.asarray(w_in)[0]
    sh = {}
    sh["wp"] = build_pieces(w_in, np.asarray(w_out)[0], np.asarray(w_up)[0], np.asarray(w_down)[0])
    sh["agw"] = np.ascontiguousarray(
        w_in[:, AG:AG + 16].reshape(16, P, 16).transpose(1, 0, 2).reshape(P, 256))
    cst = np.zeros((P, 1796), np.float32)
    cst[:, 0:128] = C["ident"]
    cst[:, 128:640] = C["maskR"]
    cst[:, 640:896] = C["maskG"]
    cst[:, 896:1024] = C["Lmat"]
    cst[:, 1024:1536] = C["gq"]
    cst[:, 1536:1540] = C["gk"]
    cst[:, 1540:1796] = -0.5
    sh["cst"] = cst
    rep = np.empty((5, P, 2048), np.float32)
    rep[0] = np.concatenate([np.asarray(g_ret)[0], np.asarray(g_gla)[0]])[None, :]
    rep[1] = np.asarray(ln1_g)[0][None, :]
    rep[2] = np.asarray(ln1_b)[0][None, :]
    rep[3] = np.asarray(ln2_g)[0][None, :]
    rep[4] = np.asarray(ln2_b)[0][None, :]
    sh["rep"] = rep
    wc = np.concatenate([np.asarray(w_conv)[0], np.asarray(b_conv)], axis=0)
    sh["wconv"] = np.ascontiguousarray(wc.reshape(4, 88, P).transpose(2, 1, 0).reshape(P, 88 * 4))
    sh["wg"] = np.ascontiguousarray(np.asarray(w_gla_gate)[0])
    bg = np.ones((1, 640), np.float32)
    bg[0, 0:512] = np.asarray(b_gla_gate)[0]
    sh["bg"] = bg
    return sh


_CACHE = {}


def kernel(x, w_in, w_gla_gate, b_gla_gate, g_ret, g_gla, w_out, ln1_g, ln1_b,
           w_up, w_conv, b_conv, w_down, ln2_g, ln2_b):
    x = np.asarray(x)
    Bn, Sn, _ = x.shape
    half = Sn // 2
    npre = half // T
    nmain = half // T
    modes = ["state"] * (npre - 1) + ["halo"] + ["full"] * nmain
    sh = make_shared_inputs(w_in, w_gla_gate, b_gla_gate, g_ret, g_gla, w_out, ln1_g, ln1_b,
                            w_up, w_conv, b_conv, w_down, ln2_g, ln2_b)
    key = tuple(modes)
    if key not in _CACHE:
        _CACHE[key] = build_program(modes)
    nc = _CACHE[key]
    in_maps = []
    for c in range(2 * Bn):
        b, hf = c // 2, c % 2
        if hf == 0:
            xs = np.concatenate([np.zeros((half, D), np.float32), x[b, :half]], axis=0)
            pos = np.concatenate([np.zeros(half), np.arange(half)])
        else:
            xs = x[b]
            pos = np.arange(Sn)
        cs, sn = rope_tables(pos)
        m = dict(sh)
        m["x"] = np.ascontiguousarray(xs)
        m["cos"] = cs
        m["sin"] = sn
        m["flag"] = np.full((P, 1), float(hf), np.float32)
        in_maps.append(m)
    res = run_bass_kernel_spmd(nc, in_maps, core_ids=list(range(2 * Bn)))
    out = np.empty((Bn, Sn, D), np.float32)
    for c in range(2 * Bn):
        b, hf = c // 2, c % 2
        out[b, hf * half:(hf + 1) * half] = res.results[c]["out"]
    return out
```

```python
from contextlib import ExitStack

import numpy as np
import concourse.bass as bass
import concourse.mybir as mybir
from concourse.bass_utils import run_bass_kernel_spmd

F32 = mybir.dt.float32
BF16 = mybir.dt.bfloat16
AF = mybir.ActivationFunctionType
ALU = mybir.AluOpType

P = 128
D = 2048
T = 512
NB = 4
DFF = 5632
NFF = DFF // P
ALPHA = 2.0 ** 0.25
EPS = 1e-5
NSLOT = 6
WINDOW = 6
QR, KR, VR, ZR, QG, KG, VG, ZG, AG = 0, 1024, 2048, 3072, 4096, 4608, 5120, 6144, 7168


class Buf:
    __slots__ = ("name", "w", "r", "semval", "const")

    def __init__(self, name, const=False):
        self.name = name
        self.w = None
        self.r = []
        self.semval = 0
        self.const = const


class Op:
    __slots__ = ("eng", "fn", "deps", "signal", "ticket", "dbuf", "dval", "idx")


class Sched:
    ENGS = ("pe", "act", "dve", "pool", "sp")

    def __init__(self):
        self.q = {e: [] for e in self.ENGS}
        self.dma_since_barrier = []
        self.pending_barrier = {e: [] for e in self.ENGS}
        self.dma_bufs = []

    def op(self, eng, fn, reads=(), writes=(), dbuf=None, extra=(), track=True):
        o = Op()
        o.eng = eng
        o.fn = fn
        o.signal = False
        o.ticket = 0
        o.dbuf = dbuf
        o.dval = 0
        o.idx = len(self.q[eng])
        if dbuf is not None:
            if dbuf.semval == 0 and dbuf not in self.dma_bufs:
                self.dma_bufs.append(dbuf)
            dbuf.semval += 16
            o.dval = dbuf.semval
        deps = []
        for b in reads:
            if b.w is not None:
                deps.append(b.w)
        for b in writes:
            if b.w is not None:
                deps.append(b.w)
            deps.extend(b.r)
        deps.extend(extra)
        if self.pending_barrier[eng]:
            deps.extend(self.pending_barrier[eng])
            self.pending_barrier[eng] = []
        o.deps = []
        seen = set()
        for d in deps:
            if d is o or id(d) in seen:
                continue
            seen.add(id(d))
            if d.dbuf is None and d.eng == eng:
                if eng == "pe":
                    continue
                if o.idx - d.idx > WINDOW:
                    continue
            if d.dbuf is None:
                d.signal = True
            o.deps.append(d)
        for b in reads:
            if not b.const:
                b.r.append(o)
        for b in writes:
            b.w = o
            b.r = []
        self.q[eng].append(o)
        if dbuf is not None and track:
            self.dma_since_barrier.append(o)
        return o

    def barrier(self):
        lasts = []
        for e in ("pe", "act", "dve", "pool"):
            for o in reversed(self.q[e]):
                if o.dbuf is None:
                    lasts.append(o)
                    break
        lasts.extend(self.dma_since_barrier)
        self.dma_since_barrier = []
        for e in ("act", "dve", "pool", "sp"):
            self.pending_barrier[e] = self.pending_barrier[e] + list(lasts)

    def emit(self, nc, final_waits):
        for e in self.ENGS:
            c = 0
            for o in self.q[e]:
                if o.dbuf is None and o.signal:
                    c += 1
                    o.ticket = c
        with ExitStack() as es:
            esem = {e: es.enter_context(nc.semaphore("s_" + e)) for e in ("pe", "act", "dve", "pool", "sp")}
            dsem = {}
            for b in self.dma_bufs:
                dsem[id(b)] = es.enter_context(nc.semaphore("d_" + b.name))
            block = es.enter_context(nc.Block())

            def run(ename, eng):
                seen = {}
                for o in self.q[ename]:
                    need = {}
                    for d in o.deps:
                        if d.dbuf is not None:
                            sem, val, key = dsem[id(d.dbuf)], d.dval, id(d.dbuf)
                        else:
                            sem, val, key = esem[d.eng], d.ticket, d.eng
                        if key not in need or need[key][1] < val:
                            need[key] = (sem, val)
                    for key, (sem, val) in need.items():
                        if seen.get(key, 0) < val:
                            eng.wait_ge(sem, val)
                            seen[key] = val
                    ins = o.fn(eng)
                    if o.dbuf is not None:
                        ins.then_inc(dsem[id(o.dbuf)], 16)
                    elif o.signal:
                        ins.then_inc(esem[ename], 1)
                if ename == "sp":
                    for b in final_waits:
                        if seen.get(id(b), 0) < b.semval:
                            eng.wait_ge(dsem[id(b)], b.semval)

            @block.tensor
            def _(e):
                run("pe", e)

            @block.scalar
            def _(e):
                run("act", e)

            @block.vector
            def _(e):
                run("dve", e)

            @block.gpsimd
            def _(e):
                run("pool", e)

            @block.sync
            def _(e):
                run("sp", e)


def _fm_piece(W, col0):
    K = W.shape[0] // P
    return W[:, col0:col0 + P].reshape(K, P, P).transpose(1, 0, 2).reshape(P, K * P)


def _tm_piece(W, kq, col0):
    return W[kq * 512:(kq + 1) * 512, col0:col0 + 512].reshape(4, P, 512).transpose(1, 0, 2).reshape(P, 2048)


def piece_plan():
    pl = []
    for hp in range(2):
        for h in (2 * hp, 2 * hp + 1):
            for c in range(2):
                pl.append(("qr", "fm_in", QR + h * 256 + c * P))
        for h in (2 * hp, 2 * hp + 1):
            for c in range(2):
                pl.append(("kr", "fm_in", KR + h * 256 + c * P))
        for kq in range(4):
            pl.append(("vr", "tm_in", (kq, VR + hp * 512)))
        for kq in range(4):
            pl.append(("zr", "tm_in", (kq, ZR + hp * 512)))
    for hp in range(2):
        for h in (2 * hp, 2 * hp + 1):
            pl.append(("qg", "fm_in", QG + h * P))
        for h in (2 * hp, 2 * hp + 1):
            pl.append(("kg", "fm_in", KG + h * P))
        for kq in range(4):
            pl.append(("vg", "tm_in", (kq, VG + hp * 512)))
        for kq in range(4):
            pl.append(("zg", "tm_in", (kq, ZG + hp * 512)))
    for cg in range(4):
        for kq in range(4):
            pl.append(("wo", "tm_out", (kq, cg * 512)))
    for j in range(NFF):
        pl.append(("uv", "fm_up", j * P))
        pl.append(("ug", "fm_up", DFF + j * P))
    for cg in range(4):
        for kq in range(NFF // 4):
            pl.append(("wd", "tm_down", (kq, cg * 512)))
    return pl


PLAN = piece_plan()
NPIECE = len(PLAN)
STATE_TAGS = ("kr", "vr", "kg", "vg")


def build_pieces(w_in, w_out, w_up, w_down):
    out = np.empty((NPIECE, P, 2048), np.float32)
    for i, (tag, kind, a) in enumerate(PLAN):
        if kind == "fm_in":
            out[i] = _fm_piece(w_in, a)
        elif kind == "tm_in":
            out[i] = _tm_piece(w_in, a[0], a[1])
        elif kind == "tm_out":
            out[i] = _tm_piece(w_out, a[0], a[1])
        elif kind == "fm_up":
            out[i] = _fm_piece(w_up, a)
        else:
            out[i] = _tm_piece(w_down, a[0], a[1])
    return out


def host_consts():
    gam = 1.0 - 2.0 ** (-5.0 - np.arange(4, dtype=np.float64))
    idx = np.arange(P)
    c = {}
    c["ident"] = np.eye(P, dtype=np.float32)
    mr = np.zeros((P, 4, P), np.float64)
    for h in range(4):
        dist = np.abs(idx[None, :] - idx[:, None])
        ok = (idx[:, None] // 64) <= (idx[None, :] // 64)
        mr[:, h, :] = np.where(ok, gam[h] ** dist / 16.0, 0.0)
    c["maskR"] = mr.astype(np.float32).reshape(P, 4 * P)
    mc = (idx[:, None] <= idx[None, :]).astype(np.float32)
    ma = ((idx[:, None] > idx[None, :]) & ((idx[:, None] // 64) == (idx[None, :] // 64))).astype(np.float32)
    c["maskG"] = np.concatenate([mc, ma], axis=1)
    c["Lmat"] = np.where(idx[:, None] <= idx[None, :], -1.0 / 16.0, 0.0).astype(np.float32)
    gq = np.zeros((P, 4, P), np.float64)
    for h in range(4):
        gq[:, h, :] = (gam[h] ** (idx + 1.0))[None, :]
    c["gq"] = gq.astype(np.float32).reshape(P, 4 * P)
    gk = np.zeros((P, 4), np.float64)
    for h in range(4):
        gk[:, h] = gam[h] ** (127.0 - idx) / 16.0
    c["gk"] = gk.astype(np.float32)
    c["g128"] = [float(g ** 128.0) for g in gam]
    return c


def rope_tables(pos):
    half = 128
    inv_freq = (np.float32(10000.0) ** (-(np.arange(half, dtype=np.float32)) / np.float32(half))).astype(np.float32)
    ang = (pos.astype(np.float32)[None, :] * inv_freq[:, None]).astype(np.float32)
    a64 = ang.astype(np.float64)
    return np.cos(a64).astype(np.float32), np.sin(a64).astype(np.float32)


def build_program(modes, dbg=None):
    NT = len(modes)
    NTOK = NT * T
    n_out_tiles = sum(1 for m in modes if m == "full")
    first_full = modes.index("full")
    C = host_consts()
    g128 = C["g128"]

    nc = bass.Bass("TRN2", target_bir_lowering=False)
    x_d = nc.dram_tensor("x", [NTOK, D], F32, kind="ExternalInput").ap()
    cos_d = nc.dram_tensor("cos", [P, NTOK], F32, kind="ExternalInput").ap()
    sin_d = nc.dram_tensor("sin", [P, NTOK], F32, kind="ExternalInput").ap()
    wp_d = nc.dram_tensor("wp", [NPIECE, P, 2048], F32, kind="ExternalInput").ap()
    agw_d = nc.dram_tensor("agw", [P, 256], F32, kind="ExternalInput").ap()
    cst_d = nc.dram_tensor("cst", [P, 1796], F32, kind="ExternalInput").ap()
    rep_d = nc.dram_tensor("rep", [5, P, 2048], F32, kind="ExternalInput").ap()
    wconv_d = nc.dram_tensor("wconv", [P, 88 * 4], F32, kind="ExternalInput").ap()
    wg_d = nc.dram_tensor("wg", [16, 512], F32, kind="ExternalInput").ap()
    bg_d = nc.dram_tensor("bg", [1, 640], F32, kind="ExternalInput").ap()
    flag_d = nc.dram_tensor("flag", [P, 1], F32, kind="ExternalInput").ap()
    lncol_d = nc.dram_tensor("lncol", [P, 32], F32, kind="ExternalInput").ap()
    out_d = nc.dram_tensor("out", [n_out_tiles * T, D], F32, kind="ExternalOutput").ap()
    wb_d = nc.dram_tensor("wbf", [NPIECE, P, 2048], BF16, kind="Internal").ap()
    dbg_d = None
    if dbg is not None:
        dbg_d = nc.dram_tensor("dbg", list(dbg), F32, kind="ExternalOutput").ap()

    S = Sched()
    es = ExitStack()

    def sb(name, shape, dt):
        return es.enter_context(nc.sbuf_tensor("s_" + name, shape, dt))

    cst = sb("cst", [P, 1796], F32)
    ident_f = cst[:, 0:128]
    maskR = cst[:, 128:640]
    maskG = cst[:, 640:896]
    Lmat = cst[:, 896:1024]
    gq = cst[:, 1024:1536]
    gk = cst[:, 1536:1540]
    neghalf = cst[:, 1540:1796]
    ident_b = sb("identb", [P, P], BF16)
    rep = sb("rep", [P, 2048], F32)
    rep2 = sb("rep2", [P, 2048], F32)
    wconv = sb("wconv", [P, 88 * 4], F32)
    wg = sb("wg", [16, 512], F32)
    bg = sb("bg", [1, 640], F32)
    agw_f = sb("agwf", [P, 256], F32)
    agw = sb("agw", [P, 256], BF16)
    flag = sb("flag", [P, 1], F32)
    lncol = sb("lncol", [P, 32], F32)
    Sr = sb("Sr", [P, 4 * 512], F32)
    Srb = sb("Srb", [P, 4 * 512], BF16)
    Sg = sb("Sg", [P, 4 * 256], F32)
    Sgb = sb("Sgb", [P, 4 * 256], BF16)
    carry = sb("carry", [P, 88 * 2], F32)
    Y = sb("Y", [P, NB * D], F32)
    wring = sb("wring", [P, NSLOT * 2048], BF16)
    small = sb("small", [P, 64], F32)
    UB = 99 * 1024
    U = sb("U", [P, UB // 2], BF16)

    class Carver:
        def __init__(self):
            self.off = 0

        def take(self, nbytes, dt):
            a = self.off
            self.off += (nbytes + 31) // 32 * 32
            assert self.off <= UB, (self.off, UB)
            ap = U[:, a // 2:(a + nbytes) // 2]
            return ap if dt == BF16 else ap.bitcast(F32)

    cv = Carver()
    x1T = cv.take(16 * 512 * 2, BF16)
    actT = cv.take(NFF * 512 * 2, BF16)
    ue = [cv.take(514 * 4, F32) for _ in range(4)]
    acc = [cv.take(512 * 4, F32) for _ in range(4)]
    sgb = [cv.take(512 * 4, F32) for _ in range(2)]
    bd_end = cv.off
    cv = Carver()
    xT = cv.take(16 * 512 * 2, BF16)
    cosb = cv.take(512 * 4, F32)
    sinb = cv.take(512 * 4, F32)
    qkb = [cv.take(512 * 2, BF16) for _ in range(12)]
    rt = [cv.take(512 * 4, F32) for _ in range(3)]
    vpair = cv.take(NB * 512 * 2, BF16)
    gz = cv.take(NB * 512 * 4, F32)
    spb = cv.take(NB * 512 * 4, F32)
    agT = cv.take(512 * 4, F32)
    Eb = [cv.take(512 * 4, F32) for _ in range(4)]
    e1 = cv.take(512 * 4, F32)
    sTb = [cv.take(256 * 2, BF16) for _ in range(4)]
    ktok = [cv.take(256 * 2, BF16) for _ in range(4)]
    onb = [cv.take(256 * 4, F32) for _ in range(2)]
    S1 = [cv.take(256 * 4, F32) for _ in range(2)]
    opair = cv.take(NB * 512 * 2, BF16)
    cv.off = max(cv.off, bd_end)
    oT = cv.take(16 * 512 * 2, BF16)

    ps = [es.enter_context(nc.psum_tensor("ps%d" % i, [P, 512], F32)) for i in range(8)]

    def B(name, const=False):
        return Buf(name, const)

    b_cst = B("cst", True)
    b_rep = B("rep")
    b_rep2 = B("rep2")
    b_ps = [B("ps%d" % i) for i in range(8)]
    b_slot = [B("slot%d" % i) for i in range(NSLOT)]
    b_Y = [B("Y%d" % i) for i in range(NB)]
    b_xT = B("xT")
    b_x1T = B("x1T")
    b_oT = [B("oT%d" % i) for i in range(16)]
    b_actT = [B("actT%d" % i) for i in range(NFF)]
    b_rope = B("rope")
    b_qk = [B("qk%d" % i) for i in range(12)]
    b_rt = [B("rt%d" % i) for i in range(3)]
    b_vp = [B("vp%d" % i) for i in range(NB)]
    b_gz = [B("gz%d" % i) for i in range(NB)]
    b_sp = [B("sp%d" % i) for i in range(NB)]
    b_agT = B("agT")
    b_E = [B("E%d" % i) for i in range(4)]
    b_e1 = B("e1")
    b_sT = [B("sT%d" % i) for i in range(4)]
    b_kt = [B("kt%d" % i) for i in range(4)]
    b_on = [B("on%d" % i) for i in range(2)]
    b_S1 = [B("S1%d" % i) for i in range(2)]
    b_op = [B("op%d" % i) for i in range(NB)]
    b_Sr = [B("Sr%d" % i) for i in range(4)]
    b_Srb = [B("Srb%d" % i) for i in range(4)]
    b_Sg = [B("Sg%d" % i) for i in range(4)]
    b_Sgb = [B("Sgb%d" % i) for i in range(4)]
    b_carry = B("carry")
    b_ue = [B("ue%d" % i) for i in range(4)]
    b_acc = [B("acc%d" % i) for i in range(4)]
    b_sg = [B("sg%d" % i) for i in range(2)]
    b_small = [B("small%d" % i) for i in range(8)]
    b_wb = B("wbf")
    b_xb = [B("xb%d" % i) for i in range(NB)]
    b_dbg = B("dbg")

    psn = [0]

    def bank():
        i = psn[0] % 8
        psn[0] += 1
        return ps[i], b_ps[i]

    def banks4():
        while psn[0] % 4 != 0:
            psn[0] += 1
        return [bank() for _ in range(4)]

    def mm(out, lhsT, rhs, start, stop, reads, writes):
        S.op("pe", lambda e: e.matmul(out, lhsT, rhs, start=start, stop=stop), reads, writes)

    def tp(out, in_, ident, reads, writes):
        S.op("pe", lambda e: e.transpose(out, in_, ident), reads + [b_cst], writes)

    def act(out, in_, func, reads, writes, scale=1.0, bias=0.0):
        S.op("act", lambda e: e.activation(out, in_, func, bias=bias, scale=scale), reads, writes)

    def tt(eng, out, a, b, op, reads, writes):
        S.op(eng, lambda e: e.tensor_tensor(out, a, b, op), reads, writes)

    def ts(eng, out, a, s1, s2, op0, op1, reads, writes):
        if s2 is None:
            S.op(eng, lambda e: e.tensor_scalar(out, a, s1, None, op0), reads, writes)
        else:
            S.op(eng, lambda e: e.tensor_scalar(out, a, s1, s2, op0, op1), reads, writes)

    def stt(out, a, s, b, op0, op1, reads, writes):
        S.op("dve", lambda e: e.scalar_tensor_tensor(out, a, s, b, op0, op1), reads, writes)

    def cp(eng, out, in_, reads, writes):
        S.op(eng, lambda e: e.tensor_copy(out, in_), reads, writes)

    def dma(eng, out, in_, reads, writes, dbuf):
        return S.op(eng, lambda e: e.dma_start(out=out, in_=in_), reads, writes, dbuf)

    b_cl = B("cload")
    for (dst, src) in ((cst[:, :], cst_d), (wconv[:, :], wconv_d), (wg[:, :], wg_d), (bg[:, :], bg_d),
                       (agw_f[:, :], agw_d), (flag[:, :], flag_d), (lncol[:, :], lncol_d)):
        dma("sp", dst, src, [], [b_cst], b_cl)
    S.op("dve", lambda e: e.tensor_copy(ident_b[:, :], ident_f), [b_cst], [b_cst])
    S.op("dve", lambda e: e.tensor_copy(agw[:, :], agw_f[:, :]), [b_cst], [b_cst])
    for t_, n_ in ((Sr, 2048), (Sg, 1024), (carry, 176)):
        S.op("pool", (lambda t_: (lambda e: e.memset(t_[:, :], 0.0)))(t_), [], [b_cst])
    for t_ in (Srb, Sgb):
        S.op("pool", (lambda t_: (lambda e: e.memset(t_[:, :], 0.0)))(t_), [], [b_cst])
    b_wb2 = B("wbf2")
    last_cast = {}
    order = [i for i in range(NPIECE) if PLAN[i][0] in STATE_TAGS] + [i for i in range(NPIECE) if PLAN[i][0] not in STATE_TAGS]
    for i in order:
        grp = 0 if PLAN[i][0] in STATE_TAGS else 1
        last_cast[grp] = S.op("pool", (lambda i: (lambda e: e.dma_start(out=wb_d[i], in_=wp_d[i])))(i), [], [],
                              b_wb if grp == 0 else b_wb2, track=False)
    S.barrier()

    stream = []
    for ti, m in enumerate(modes):
        for pi, (tag, kind, a) in enumerate(PLAN):
            if m == "state" and tag not in STATE_TAGS:
                continue
            if m == "halo" and tag == "wd":
                continue
            stream.append((ti, pi, tag))
    wpos = {"issued": 0, "used": 0}

    def issue_loads(upto):
        while wpos["issued"] < min(upto, len(stream)):
            k = wpos["issued"]
            _, pi, _ = stream[k]
            sl = k % NSLOT
            grp = 0 if PLAN[pi][0] in STATE_TAGS else 1
            S.op("sp", (lambda sl, pi: (lambda e: e.dma_start(out=wring[:, sl * 2048:(sl + 1) * 2048], in_=wb_d[pi])))(sl, pi),
                 [], [b_slot[sl]], b_slot[sl], extra=[last_cast[grp]])
            wpos["issued"] += 1

    def next_piece(ti, tag):
        k = wpos["used"]
        assert stream[k][0] == ti and stream[k][2] == tag, (stream[k], ti, tag)
        issue_loads(k + NSLOT)
        wpos["used"] += 1
        sl = k % NSLOT
        return wring[:, sl * 2048:(sl + 1) * 2048], b_slot[sl]

    def fm_group(ti, tag, src, b_src, M=P, c0=0, c1=512):
        w, bw = next_piece(ti, tag)
        pb, bb = bank()
        for k in range(16):
            mm(pb[0:M, 0:c1 - c0], w[:, k * P:k * P + M], src[:, k * 512 + c0:k * 512 + c1], k == 0, k == 15,
               [bw, b_src], [bb])
        return pb, bb

    def tm_group(ti, tag, nk, lhs_fn, blks=range(NB)):
        bks = banks4()
        for kq in range(nk // 4):
            w, bw = next_piece(ti, tag)
            for kk in range(4):
                k = kq * 4 + kk
                for blk in blks:
                    l, bl = lhs_fn(k, blk)
                    mm(bks[blk][0][:, :], l, w[:, kk * 512:(kk + 1) * 512], k == 0, k == nk - 1,
                       [bw, bl], [bks[blk][1]])
        return bks

    def x3(ap, a):
        return ap.rearrange("p (a b) -> p a b", a=a)

    out_tile = [0]

    def transposes_to(dstT, b_dst, blks=range(NB)):
        d3 = x3(dstT, 16)
        for blk in blks:
            for j in range(4):
                pb, bb = bank()
                for c in range(4):
                    fc = 4 * j + c
                    tp(pb[:, c * P:(c + 1) * P], Y[:, blk * D + fc * P: blk * D + (fc + 1) * P], ident_f,
                       [b_Y[blk]], [bb])
                eng_copy = "act" if (j % 2 == 0) else "dve"
                o_ap = d3[:, 4 * j:4 * j + 4, blk * P:(blk + 1) * P]
                i_ap = x3(pb[:, :], 4)
                if eng_copy == "act":
                    act(o_ap, i_ap, AF.Copy, [bb], [b_dst])
                else:
                    S.op("dve", (lambda o_ap, i_ap: (lambda e: e.tensor_copy(o_ap, i_ap)))(o_ap, i_ap), [bb], [b_dst])

    def ln_stats(blk):
        st = small[:, 0:24]
        mv = small[:, 24:26]
        rs = small[:, 26:27]
        yb = Y[:, blk * D:(blk + 1) * D]
        for c in range(4):
            S.op("dve", (lambda c: (lambda e: e.bn_stats(st[:, c * 6:(c + 1) * 6], yb[:, c * 512:(c + 1) * 512])))(c),
                 [b_Y[blk]], [b_small[0]])
        S.op("dve", lambda e: e.bn_aggr(mv, st), [b_small[0]], [b_small[1]])
        ts("pool", rs, mv[:, 1:2], EPS, None, ALU.add, None, [b_small[1]], [b_small[2]])
        tt("pool", rs, rs, neghalf[:, 0:1], ALU.pow, [b_small[2], b_cst], [b_small[2]])
        return yb, mv, rs

    xbf = x3(oT, NB)

    def prefetch_x(ti):
        if ti >= NT:
            return
        tok0 = ti * T
        for blk in range(NB):
            S.op("pool", (lambda blk, tok0: (lambda e: e.dma_start(
                out=xbf[:, blk, :], in_=x_d[tok0 + blk * P: tok0 + (blk + 1) * P, :])))(blk, tok0),
                [], [b_oT[4 * blk + i] for i in range(4)], b_xb[blk])

    def transposes_x():
        d3 = x3(xT, 16)
        for blk in range(NB):
            for j in range(2):
                pb, bb = bank()
                pbb = pb[:, :].bitcast(BF16)
                for c in range(8):
                    fc = 8 * j + c
                    tp(pbb[:, c * P:(c + 1) * P], xbf[:, blk, fc * P:(fc + 1) * P], ident_b[:, :],
                       [b_oT[4 * blk + fc // 4]], [bb])
                act(d3[:, 8 * j:8 * j + 8, blk * P:(blk + 1) * P], x3(pbb[:, 0:1024], 8), AF.Copy, [bb], [b_xT])

    def transposes_x1(blks):
        d3 = x3(x1T, 16)
        for blk in blks:
            for j in range(4):
                pb, bb = bank()
                for c in range(4):
                    fc = 4 * j + c
                    tp(pb[:, c * P:(c + 1) * P], Y[:, blk * D + fc * P: blk * D + (fc + 1) * P], ident_f,
                       [b_Y[blk]], [bb])
                for c in range(4):
                    fc = 4 * j + c
                    act(d3[:, fc, blk * P:(blk + 1) * P], pb[:, c * P:(c + 1) * P], AF.Identity, [bb, b_cst], [b_x1T],
                        scale=lncol[:, fc:fc + 1], bias=lncol[:, 16 + fc:17 + fc])

    def layer_norm1(blk):
        yb, mv, rs = ln_stats(blk)
        ts("dve", yb, yb, mv[:, 0:1], rs, ALU.subtract, ALU.mult, [b_Y[blk], b_small[1], b_small[2]], [b_Y[blk]])

    def layer_norm2(blk):
        yb, mv, rs = ln_stats(blk)
        stt(yb, yb, mv[:, 0:1], rep[:, :], ALU.subtract, ALU.mult, [b_Y[blk], b_small[1], b_rep], [b_Y[blk]])
        stt(yb, yb, rs, rep2[:, :], ALU.mult, ALU.add, [b_Y[blk], b_small[2], b_rep2], [b_Y[blk]])

    def layer_norm(blk, si):
        st = small[:, 0:24]
        mv = small[:, 24:26]
        rs = small[:, 26:27]
        yb = Y[:, blk * D:(blk + 1) * D]
        for c in range(4):
            S.op("dve", (lambda c: (lambda e: e.bn_stats(st[:, c * 6:(c + 1) * 6], yb[:, c * 512:(c + 1) * 512])))(c),
                 [b_Y[blk]], [b_small[0]])
        S.op("dve", lambda e: e.bn_aggr(mv, st), [b_small[0]], [b_small[1]])
        ts("pool", rs, mv[:, 1:2], EPS, None, ALU.add, None, [b_small[1]], [b_small[2]])
        tt("pool", rs, rs, neghalf[:, 0:1], ALU.pow, [b_small[2], b_cst], [b_small[2]])
        ts("dve", yb, yb, mv[:, 0:1], rs, ALU.subtract, ALU.mult, [b_Y[blk], b_small[1], b_small[2]], [b_Y[blk]])
        tt("dve", yb, yb, rep[:, :], ALU.mult, [b_Y[blk], b_rep], [b_Y[blk]])
        tt("pool", yb, yb, rep2[:, :], ALU.add, [b_Y[blk], b_rep2], [b_Y[blk]])

    def head_norm(pb, bb, hl, blk, center, si):
        st = small[:, 32 + si * 16: 32 + si * 16 + 6]
        mv = small[:, 32 + si * 16 + 6: 32 + si * 16 + 8]
        rs = small[:, 32 + si * 16 + 8: 32 + si * 16 + 9]
        m2 = small[:, 32 + si * 16 + 9: 32 + si * 16 + 10]
        bs = b_small[3 + si]
        S.op("dve", lambda e: e.bn_stats(st, pb[:, 0:256]), [bb], [bs])
        S.op("dve", lambda e: e.bn_aggr(mv, st), [bs], [bs])
        if center:
            ts("pool", rs, mv[:, 1:2], EPS, None, ALU.add, None, [bs], [bs])
        else:
            tt("pool", m2, mv[:, 0:1], mv[:, 0:1], ALU.mult, [bs], [bs])
            ts("pool", rs, m2, EPS, mv[:, 1:2], ALU.add, ALU.add, [bs], [bs])
        tt("pool", rs, rs, neghalf[:, 0:1], ALU.pow, [bs, b_cst], [bs])
        on = onb[si]
        if center:
            ts("dve", on, pb[:, 0:256], mv[:, 0:1], rs, ALU.subtract, ALU.mult, [bb, bs], [b_on[si]])
        else:
            ts("dve", on, pb[:, 0:256], rs, None, ALU.mult, None, [bb, bs], [b_on[si]])
        g3 = x3(gz, NB)
        o3 = x3(opair, NB)
        tt("pool", o3[:, blk, hl * 256:(hl + 1) * 256], on, g3[:, blk, hl * 256:(hl + 1) * 256], ALU.mult,
           [b_on[si], b_gz[blk]], [b_op[blk]])

    def vz_groups(ti, vtag, ztag, gcol, full):
        xT3 = x3(xT, 16)

        def lhs(k, blk):
            return xT3[:, k, blk * P:(blk + 1) * P], b_xT
        bks = tm_group(ti, vtag, 16, lhs)
        v3 = x3(vpair, NB)
        for blk in range(NB):
            if blk % 2 == 0:
                act(v3[:, blk, :], bks[blk][0][:, :], AF.Copy, [bks[blk][1]], [b_vp[blk]])
            else:
                cp("dve", v3[:, blk, :], bks[blk][0][:, :], [bks[blk][1]], [b_vp[blk]])
        if full:
            bks = tm_group(ti, ztag, 16, lhs)
            g3 = x3(gz, NB)
            for blk in range(NB):
                act(g3[:, blk, :], bks[blk][0][:, :], AF.Silu, [bks[blk][1]], [b_gz[blk]])
                tt("pool", g3[:, blk, :], g3[:, blk, :], rep[:, gcol:gcol + 512], ALU.mult,
                   [b_gz[blk], b_rep], [b_gz[blk]])

    def opair_to_oT(fc0):
        o3 = x3(opair, NB)
        oT3 = x3(oT, 16)
        for c in range(4):
            pb, bb = bank()
            pbb = pb[:, :].bitcast(BF16)
            for blk in range(NB):
                tp(pbb[:, blk * P:(blk + 1) * P], o3[:, blk, c * P:(c + 1) * P], ident_b[:, :], [b_op[blk]], [bb])
            if c % 2 == 0:
                act(oT3[:, fc0 + c, :], pbb[:, 0:512], AF.Copy, [bb], [b_oT[fc0 + c]])
            else:
                S.op("dve", (lambda c, pbb: (lambda e: e.tensor_copy(oT3[:, fc0 + c, :], pbb[:, 0:512])))(c, pbb),
                     [bb], [b_oT[fc0 + c]])

    def do_tile(ti, mode):
        full = mode != "state"
        tok0 = ti * T
        if ti == 0 or modes[ti - 1] != "full":
            S.barrier()
        dma("sp", cosb, cos_d[:, tok0:tok0 + T], [], [b_rope], b_rope)
        dma("sp", sinb, sin_d[:, tok0:tok0 + T], [], [b_rope], b_rope)
        transposes_x()
        if not full:
            prefetch_x(ti + 1)

        pb, bb = bank()
        for k in range(16):
            mm(pb[0:16, :], agw[:, k * 16:(k + 1) * 16], xT[:, k * 512:(k + 1) * 512], k == 0, k == 15,
               [b_cst, b_xT], [bb])
        act(agT[0:16, :], pb[0:16, :], AF.Copy, [bb], [b_agT])
        sp3 = x3(spb, NB)
        for blk in range(NB):
            pb, bb = bank()
            mm(pb[:, :], agT[0:16, blk * P:(blk + 1) * P], wg[:, :], True, False, [b_agT, b_cst], [bb])
            mm(pb[:, :], bg[0:1, 512:640], bg[0:1, 0:512], False, True, [b_cst], [bb])
            act(e1, pb[:, :], AF.Exp, [bb], [b_e1], scale=-1.0)
            act(sp3[:, blk, :], e1, AF.Ln, [b_e1], [b_sp[blk]], bias=1.0)

        for hp in range(2):
            qb = {}
            for which in (("qr",) if full else ()) + ("kr",):
                if which == "qr":
                    pass
                for hl in range(2):
                    h = 2 * hp + hl
                    pb0, bb0 = fm_group(ti, which, xT, b_xT)
                    pb1, bb1 = fm_group(ti, which, xT, b_xT)
                    x1_, x2_ = pb0[:, :], pb1[:, :]
                    if which == "qr":
                        i0 = hl * 4
                        q0, q1, qd0, qd1 = qkb[i0], qkb[i0 + 1], qkb[i0 + 2], qkb[i0 + 3]
                        bq = b_qk[i0:i0 + 4]
                        gqb = gq[:, h * P:(h + 1) * P].unsqueeze(1).broadcast_to([P, NB, P])
                        tt("dve", rt[0], x1_, cosb, ALU.mult, [bb0, b_rope], [b_rt[0]])
                        tt("dve", rt[1], x2_, sinb, ALU.mult, [bb1, b_rope], [b_rt[1]])
                        tt("dve", rt[2], rt[0], rt[1], ALU.subtract, [b_rt[0], b_rt[1]], [b_rt[2]])
                        act(q0, rt[2], AF.Copy, [b_rt[2]], [bq[0]])
                        tt("dve", x3(qd0, NB), x3(rt[2], NB), gqb, ALU.mult, [b_rt[2], b_cst], [bq[2]])
                        tt("dve", rt[0], x1_, sinb, ALU.mult, [bb0, b_rope], [b_rt[0]])
                        tt("dve", rt[1], x2_, cosb, ALU.mult, [bb1, b_rope], [b_rt[1]])
                        tt("dve", rt[2], rt[0], rt[1], ALU.add, [b_rt[0], b_rt[1]], [b_rt[2]])
                        act(q1, rt[2], AF.Copy, [b_rt[2]], [bq[1]])
                        tt("dve", x3(qd1, NB), x3(rt[2], NB), gqb, ALU.mult, [b_rt[2], b_cst], [bq[3]])
                    else:
                        i0 = 8 + hl * 2
                        k0, k1 = qkb[i0], qkb[i0 + 1]
                        tt("dve", rt[0], x1_, cosb, ALU.mult, [bb0, b_rope], [b_rt[0]])
                        tt("dve", rt[1], x2_, sinb, ALU.mult, [bb1, b_rope], [b_rt[1]])
                        tt("dve", k0, rt[0], rt[1], ALU.subtract, [b_rt[0], b_rt[1]], [b_qk[i0]])
                        tt("dve", rt[0], x1_, sinb, ALU.mult, [bb0, b_rope], [b_rt[0]])
                        tt("dve", rt[1], x2_, cosb, ALU.mult, [bb1, b_rope], [b_rt[1]])
                        tt("dve", k1, rt[0], rt[1], ALU.add, [b_rt[0], b_rt[1]], [b_qk[i0 + 1]])
            if full and hp == 0:
                dma("sp", rep[:, :], rep_d[0], [], [b_rep], b_rep)
            vz_groups(ti, "vr", "zr", hp * 512, full)
            v3 = x3(vpair, NB)

            def ret_s1(blk, hl):
                h = 2 * hp + hl
                k0, k1 = qkb[8 + hl * 2], qkb[9 + hl * 2]
                bk = [b_qk[8 + hl * 2], b_qk[9 + hl * 2]]
                cs = slice(blk * P, (blk + 1) * P)
                bi = hl * 2 + blk % 2
                if full:
                    q0, q1 = qkb[hl * 4], qkb[hl * 4 + 1]
                    bq = b_qk[hl * 4:hl * 4 + 4]
                    pbs, bbs = bank()
                    mm(pbs[:, 0:P], k0[:, cs], q0[:, cs], True, False, [bk[0], bq[0]], [bbs])
                    mm(pbs[:, 0:P], k1[:, cs], q1[:, cs], False, True, [bk[1], bq[1]], [bbs])
                    tt("dve", sTb[bi][:, 0:P], pbs[:, 0:P], maskR[:, h * P:(h + 1) * P], ALU.mult,
                       [bbs, b_cst], [b_sT[bi]])
                pbk, bbk = bank()
                pkb = pbk[:, :].bitcast(BF16)
                tp(pkb[:, 0:P], k0[:, cs], ident_b[:, :], [bk[0]], [bbk])
                tp(pkb[:, P:2 * P], k1[:, cs], ident_b[:, :], [bk[1]], [bbk])
                act(ktok[bi], pkb[:, 0:256], AF.Identity, [bbk, b_cst], [b_kt[bi]], scale=gk[:, h:h + 1])

            def ret_s2(blk, hl):
                h = 2 * hp + hl
                cs = slice(blk * P, (blk + 1) * P)
                bi = hl * 2 + blk % 2
                vv = v3[:, blk, hl * 256:(hl + 1) * 256]
                pbd, bbd = bank()
                mm(pbd[:, 0:256], ktok[bi][:, 0:P], vv, True, True, [b_kt[bi], b_vp[blk]], [bbd])
                mm(pbd[:, 256:512], ktok[bi][:, P:2 * P], vv, True, True, [b_kt[bi], b_vp[blk]], [bbd])
                if full:
                    qd0, qd1 = qkb[hl * 4 + 2], qkb[hl * 4 + 3]
                    bq = b_qk[hl * 4:hl * 4 + 4]
                    pbo, bbo = bank()
                    mm(pbo[:, 0:256], sTb[bi][:, 0:P], vv, True, False, [b_sT[bi], b_vp[blk]], [bbo])
                    mm(pbo[:, 0:256], qd0[:, cs], Srb[:, h * 512:h * 512 + 256], False, False,
                       [bq[2], b_Srb[h]], [bbo])
                    mm(pbo[:, 0:256], qd1[:, cs], Srb[:, h * 512 + 256:h * 512 + 512], False, True,
                       [bq[3], b_Srb[h]], [bbo])
                sr = Sr[:, h * 512:(h + 1) * 512]
                stt(sr, sr, g128[h], pbd[:, :], ALU.mult, ALU.add, [b_Sr[h], bbd], [b_Sr[h]])
                act(Srb[:, h * 512:(h + 1) * 512], sr, AF.Copy, [b_Sr[h]], [b_Srb[h]])
                if full:
                    head_norm(pbo, bbo, hl, blk, True, hl)

            for step in range(NB + 1):
                if step < NB:
                    for hl in range(2):
                        ret_s1(step, hl)
                if step >= 1:
                    for hl in range(2):
                        ret_s2(step - 1, hl)
            if full:
                opair_to_oT(hp * 4)

        if full:
            for blk in ([NB - 1] if mode == "halo" else range(NB)):
                dma("sp", Y[:, blk * D:(blk + 1) * D], x_d[tok0 + blk * P: tok0 + (blk + 1) * P, :], [], [b_Y[blk]], b_Y[blk])
        for hp in range(2):
            qps = {}
            for which in (("qg",) if full else ()) + ("kg",):
                for hl in range(2):
                    qps[(which, hl)] = fm_group(ti, which, xT, b_xT)
            for hl in range(2):
                h = 2 * hp + hl
                pbB, bbB = bank()
                for blk in range(NB):
                    mm(pbB[:, blk * P:(blk + 1) * P], sp3[:, blk, h * P:(h + 1) * P], Lmat, True, True,
                       [b_sp[blk], b_cst], [bbB])
                Ep, En = Eb[hl * 2], Eb[hl * 2 + 1]
                act(Ep, pbB[:, :], AF.Exp, [bbB], [b_E[hl * 2]])
                act(En, pbB[:, :], AF.Exp, [bbB], [b_E[hl * 2 + 1]], scale=-1.0)
                i0 = hl * 4
                qP, qN, kP, kN = qkb[i0], qkb[i0 + 1], qkb[i0 + 2], qkb[i0 + 3]
                sc = 128.0 ** -0.5
                if full:
                    pq, bq_ = qps[("qg", hl)]
                    stt(qP, pq[:, :], sc, Ep, ALU.mult, ALU.mult, [bq_, b_E[hl * 2]], [b_qk[i0]])
                    stt(qN, pq[:, :], sc, En, ALU.mult, ALU.mult, [bq_, b_E[hl * 2 + 1]], [b_qk[i0 + 1]])
                pk, bk_ = qps[("kg", hl)]
                if full:
                    tt("dve", kP, pk[:, :], Ep, ALU.mult, [bk_, b_E[hl * 2]], [b_qk[i0 + 2]])
                tt("dve", kN, pk[:, :], En, ALU.mult, [bk_, b_E[hl * 2 + 1]], [b_qk[i0 + 3]])
            vz_groups(ti, "vg", "zg", 1024 + hp * 512, full)
            v3 = x3(vpair, NB)

            def gla_s1(blk, hl):
                i0 = hl * 4
                qP, qN, kP, kN = qkb[i0], qkb[i0 + 1], qkb[i0 + 2], qkb[i0 + 3]
                bqP, bqN, bkP, bkN = b_qk[i0:i0 + 4]
                cs = slice(blk * P, (blk + 1) * P)
                bi = hl * 2 + blk % 2
                if full:
                    pbs, bbs = bank()
                    mm(pbs[:, 0:P], kN[:, cs], qP[:, cs], True, True, [bkN, bqP], [bbs])
                    mm(pbs[:, P:2 * P], kP[:, cs], qN[:, cs], True, True, [bkP, bqN], [bbs])
                    tt("dve", sTb[bi], pbs[:, 0:256], maskG, ALU.mult, [bbs, b_cst], [b_sT[bi]])
                pbk, bbk = bank()
                pkb = pbk[:, :].bitcast(BF16)
                tp(pkb[:, 0:P], kN[:, cs], ident_b[:, :], [bkN], [bbk])
                act(ktok[bi][:, 0:P], pkb[:, 0:P], AF.Copy, [bbk], [b_kt[bi]])

            def gla_s2(blk, hl):
                h = 2 * hp + hl
                i0 = hl * 4
                qP = qkb[i0]
                bqP = b_qk[i0]
                Ep = Eb[hl * 2]
                vv = v3[:, blk, hl * 256:(hl + 1) * 256]
                cs = slice(blk * P, (blk + 1) * P)
                bi = hl * 2 + blk % 2
                sg_ = Sg[:, h * 256:(h + 1) * 256]
                e127 = Ep[:, blk * P + 127: blk * P + 128]
                pbd, bbd = bank()
                mm(pbd[:, 0:256], ktok[bi][:, 0:P], vv, True, True, [b_kt[bi], b_vp[blk]], [bbd])
                if full:
                    pbo, bbo = bank()
                    mm(pbo[:, 0:256], sTb[bi][:, 0:P], vv, True, False, [b_sT[bi], b_vp[blk]], [bbo])
                    mm(pbo[:, 0:256], sTb[bi][:, P:2 * P], vv, False, False, [b_sT[bi], b_vp[blk]], [bbo])
                    mm(pbo[:, 0:256], qP[:, cs], Sgb[:, h * 256:(h + 1) * 256], False, True,
                       [bqP, b_Sgb[h]], [bbo])
                ts("dve", S1[hl], sg_, e127, None, ALU.mult, None, [b_Sg[h], b_E[hl * 2]], [b_S1[hl]])
                stt(sg_, pbd[:, 0:256], e127, S1[hl], ALU.mult, ALU.add, [bbd, b_E[hl * 2], b_S1[hl], b_Sg[h]],
                    [b_Sg[h]])
                act(Sgb[:, h * 256:(h + 1) * 256], sg_, AF.Copy, [b_Sg[h]], [b_Sgb[h]])
                if full:
                    head_norm(pbo, bbo, hl, blk, False, hl)

            for step in range(NB + 1):
                if step < NB:
                    for hl in range(2):
                        gla_s1(step, hl)
                if step >= 1:
                    for hl in range(2):
                        gla_s2(step - 1, hl)
            if full:
                opair_to_oT(8 + hp * 4)
        if not full:
            return

        dma("sp", rep[:, :], rep_d[1], [], [b_rep], b_rep)
        dma("sp", rep2[:, :], rep_d[2], [], [b_rep2], b_rep2)
        oT3 = x3(oT, 16)
        blksB = [NB - 1] if mode == "halo" else list(range(NB))
        for cg in range(4):
            def lhs(k, blk):
                return oT3[:, k, blk * P:(blk + 1) * P], b_oT[k]
            bks = tm_group(ti, "wo", 16, lhs, blksB)
            for blk in blksB:
                yb = Y[:, blk * D + cg * 512: blk * D + (cg + 1) * 512]
                stt(yb, yb, ALPHA, bks[blk][0][:, :], ALU.mult, ALU.add, [b_Y[blk], bks[blk][1]], [b_Y[blk]])
        S.barrier()
        for blk in blksB:
            layer_norm1(blk)
        transposes_x1(blksB)
        prefetch_x(ti + 1)
        deferred = []
        if mode == "full":
            for blk in blksB:
                for cgi in range(4):
                    yc = Y[:, blk * D + cgi * 512: blk * D + (cgi + 1) * 512]
                    deferred.append((yc, rep[:, cgi * 512:(cgi + 1) * 512], ALU.mult, b_rep, blk))
                for cgi in range(4):
                    yc = Y[:, blk * D + cgi * 512: blk * D + (cgi + 1) * 512]
                    deferred.append((yc, rep2[:, cgi * 512:(cgi + 1) * 512], ALU.add, b_rep2, blk))

        if ti == first_full:
            ts("pool", carry[:, :], carry[:, :], flag[:, 0:1], None, ALU.mult, None, [b_carry, b_cst], [b_carry])
        a3 = x3(actT, NFF)
        wc3 = x3(wconv[:, :], 88)
        c3 = x3(carry[:, :], 88)
        if mode == "halo":
            for j in range(NFF):
                for half_, tag in ((0, "uv"), (1, "ug")):
                    ch = j + half_ * NFF
                    pb, bb = fm_group(ti, tag, x1T, b_x1T, c0=384, c1=512)
                    act(c3[:, ch, :], pb[:, 126:128], AF.Copy, [bb], [b_carry])
            return
        for j in range(NFF):
            if deferred:
                yc, rr, op_, brr, blk_ = deferred.pop(0)
                tt("pool", yc, yc, rr, op_, [b_Y[blk_], brr], [b_Y[blk_]])
            res = []
            for half_, tag in ((0, "uv"), (1, "ug")):
                ch = j + half_ * NFF
                pb, bb = fm_group(ti, tag, x1T, b_x1T)
                wi = (2 * j + half_) % 4
                u, bu = ue[wi], b_ue[wi]
                a_, ba = acc[wi], b_acc[wi]
                S.op("pool", (lambda u, ch: (lambda e: e.tensor_copy(u[:, 0:2], c3[:, ch, :])))(u, ch),
                     [b_carry], [bu])
                act(u[:, 2:514], pb[:, :], AF.Copy, [bb], [bu])
                S.op("pool", (lambda u, ch: (lambda e: e.tensor_copy(c3[:, ch, :], u[:, 512:514])))(u, ch),
                     [bu], [b_carry])
                act(a_, pb[:, :], AF.Identity, [bb, b_cst], [ba], scale=wc3[:, ch, 2:3], bias=wc3[:, ch, 3:4])
                stt(a_, u[:, 1:513], wc3[:, ch, 1:2], a_, ALU.mult, ALU.add, [bu, ba, b_cst], [ba])
                stt(a_, u[:, 0:512], wc3[:, ch, 0:1], a_, ALU.mult, ALU.add, [bu, ba, b_cst], [ba])
                res.append((a_, ba))
            if mode == "full":
                s_, bs_ = sgb[j % 2], b_sg[j % 2]
                act(s_, res[1][0], AF.Silu, [res[1][1]], [bs_])
                tt("dve", a3[:, j, :], s_, res[0][0], ALU.mult, [bs_, res[0][1]], [b_actT[j]])
        if mode != "full":
            return

        dma("sp", rep[:, :], rep_d[3], [], [b_rep], b_rep)
        dma("sp", rep2[:, :], rep_d[4], [], [b_rep2], b_rep2)
        for cg in range(4):
            def lhs(k, blk):
                return a3[:, k, blk * P:(blk + 1) * P], b_actT[k]
            bks = tm_group(ti, "wd", NFF, lhs)
            for blk in range(NB):
                yb = Y[:, blk * D + cg * 512: blk * D + (cg + 1) * 512]
                stt(yb, yb, ALPHA, bks[blk][0][:, :], ALU.mult, ALU.add, [b_Y[blk], bks[blk][1]], [b_Y[blk]])
        ot = out_tile[0]
        out_tile[0] += 1
        S.barrier()
        for blk in range(NB):
            layer_norm2(blk)
            dma("sp", out_d[ot * T + blk * P: ot * T + (blk + 1) * P, :], Y[:, blk * D:(blk + 1) * D],
                [b_Y[blk]], [], b_Y[blk])

    prefetch_x(0)
    for ti, m in enumerate(modes):
        do_tile(ti, m)

    S.emit(nc, final_waits=b_Y)
    es.close()
    return nc


def make_shared_inputs(w_in, w_gla_gate, b_gla_gate, g_ret, g_gla, w_out, ln1_g, ln1_b,
                       w_up, w_conv, b_conv, w_down, ln2_g, ln2_b):
    C = host_consts()
    w_in = np.asarray(w_in)[0]
    sh = {}
    sh["wp"] = build_pieces(w_in, np.asarray(w_out)[0], np.asarray(w_up)[0], np.asarray(w_down)[0])
    sh["agw"] = np.ascontiguousarray(
        w_in[:, AG:AG + 16].reshape(16, P, 16).transpose(1, 0, 2).reshape(P, 256))
    cst = np.zeros((P, 1796), np.float32)
    cst[:, 0:128] = C["ident"]
    cst[:, 128:640] = C["maskR"]
    cst[:, 640:896] = C["maskG"]
    cst[:, 896:1024] = C["Lmat"]
    cst[:, 1024:1536] = C["gq"]
    cst[:, 1536:1540] = C["gk"]
    cst[:, 1540:1796] = -0.5
    sh["cst"] = cst
    rep = np.empty((5, P, 2048), np.float32)
    rep[0] = np.concatenate([np.asarray(g_ret)[0], np.asarray(g_gla)[0]])[None, :]
    rep[1] = np.asarray(ln1_g)[0][None, :]
    rep[2] = np.asarray(ln1_b)[0][None, :]
    rep[3] = np.asarray(ln2_g)[0][None, :]
    rep[4] = np.asarray(ln2_b)[0][None, :]
    sh["rep"] = rep
    wc = np.concatenate([np.asarray(w_conv)[0], np.asarray(b_conv)], axis=0)
    sh["wconv"] = np.ascontiguousarray(wc.reshape(4, 88, P).transpose(2, 1, 0).reshape(P, 88 * 4))
    sh["wg"] = np.ascontiguousarray(np.asarray(w_gla_gate)[0])
    sh["lncol"] = np.ascontiguousarray(np.concatenate(
        [np.asarray(ln1_g)[0].reshape(16, P).T, np.asarray(ln1_b)[0].reshape(16, P).T], axis=1))
    bg = np.ones((1, 640), np.float32)
    bg[0, 0:512] = np.asarray(b_gla_gate)[0]
    sh["bg"] = bg
    return sh


_CACHE = {}


def kernel(x, w_in, w_gla_gate, b_gla_gate, g_ret, g_gla, w_out, ln1_g, ln1_b,
           w_up, w_conv, b_conv, w_down, ln2_g, ln2_b):
    x = np.asarray(x)
    Bn, Sn, _ = x.shape
    half = Sn // 2
    npre = half // T
    nmain = half // T
    modes = ["state"] * (npre - 1) + ["halo"] + ["full"] * nmain
    sh = make_shared_inputs(w_in, w_gla_gate, b_gla_gate, g_ret, g_gla, w_out, ln1_g, ln1_b,
                            w_up, w_conv, b_conv, w_down, ln2_g, ln2_b)
    key = tuple(modes)
    if key not in _CACHE:
        _CACHE[key] = build_program(modes)
    nc = _CACHE[key]
    in_maps = []
    for c in range(2 * Bn):
        b, hf = c // 2, c % 2
        if hf == 0:
            xs = np.concatenate([np.zeros((half, D), np.float32), x[b, :half]], axis=0)
            pos = np.concatenate([np.zeros(half), np.arange(half)])
        else:
            xs = x[b]
            pos = np.arange(Sn)
        cs, sn = rope_tables(pos)
        m = dict(sh)
        m["x"] = np.ascontiguousarray(xs)
        m["cos"] = cs
        m["sin"] = sn
        m["flag"] = np.full((P, 1), float(hf), np.float32)
        in_maps.append(m)
    res = run_bass_kernel_spmd(nc, in_maps, core_ids=list(range(2 * Bn)))
    out = np.empty((Bn, Sn, D), np.float32)
    for c in range(2 * Bn):
        b, hf = c // 2, c % 2
        out[b, hf * half:(hf + 1) * half] = res.results[c]["out"]
    return out
```

```python
from contextlib import ExitStack

import numpy as np
import concourse.bass as bass
import concourse.mybir as mybir
from concourse.bass_utils import run_bass_kernel_spmd

F32 = mybir.dt.float32
BF16 = mybir.dt.bfloat16
AF = mybir.ActivationFunctionType
ALU = mybir.AluOpType

P = 128
D = 2048
T = 512
NB = 4
DFF = 5632
NFF = DFF // P
ALPHA = 2.0 ** 0.25
EPS = 1e-5
NSLOT = 6
WINDOW = 6
QR, KR, VR, ZR, QG, KG, VG, ZG, AG = 0, 1024, 2048, 3072, 4096, 4608, 5120, 6144, 7168


class Buf:
    __slots__ = ("name", "w", "r", "semval", "const")

    def __init__(self, name, const=False):
        self.name = name
        self.w = None
        self.r = []
        self.semval = 0
        self.const = const


class Op:
    __slots__ = ("eng", "fn", "deps", "signal", "ticket", "dbuf", "dval", "idx")


class Sched:
    ENGS = ("pe", "act", "dve", "pool", "sp")

    def __init__(self):
        self.q = {e: [] for e in self.ENGS}
        self.dma_since_barrier = []
        self.pending_barrier = {e: [] for e in self.ENGS}
        self.dma_bufs = []

    def op(self, eng, fn, reads=(), writes=(), dbuf=None, extra=(), track=True):
        o = Op()
        o.eng = eng
        o.fn = fn
        o.signal = False
        o.ticket = 0
        o.dbuf = dbuf
        o.dval = 0
        o.idx = len(self.q[eng])
        if dbuf is not None:
            if dbuf.semval == 0 and dbuf not in self.dma_bufs:
                self.dma_bufs.append(dbuf)
            dbuf.semval += 16
            o.dval = dbuf.semval
        deps = []
        for b in reads:
            if b.w is not None:
                deps.append(b.w)
        for b in writes:
            if b.w is not None:
                deps.append(b.w)
            deps.extend(b.r)
        deps.extend(extra)
        if self.pending_barrier[eng]:
            deps.extend(self.pending_barrier[eng])
            self.pending_barrier[eng] = []
        o.deps = []
        seen = set()
        for d in deps:
            if d is o or id(d) in seen:
                continue
            seen.add(id(d))
            if d.dbuf is None and d.eng == eng:
                if eng == "pe":
                    continue
                if o.idx - d.idx > WINDOW:
                    continue
            if d.dbuf is None:
                d.signal = True
            o.deps.append(d)
        for b in reads:
            if not b.const:
                b.r.append(o)
        for b in writes:
            b.w = o
            b.r = []
        self.q[eng].append(o)
        if dbuf is not None and track:
            self.dma_since_barrier.append(o)
        return o

    def barrier(self):
        lasts = []
        for e in ("pe", "act", "dve", "pool"):
            for o in reversed(self.q[e]):
                if o.dbuf is None:
                    lasts.append(o)
                    break
        lasts.extend(self.dma_since_barrier)
        self.dma_since_barrier = []
        for e in ("act", "dve", "pool", "sp"):
            self.pending_barrier[e] = self.pending_barrier[e] + list(lasts)

    def emit(self, nc, final_waits):
        for e in self.ENGS:
            c = 0
            for o in self.q[e]:
                if o.dbuf is None and o.signal:
                    c += 1
                    o.ticket = c
        with ExitStack() as es:
            esem = {e: es.enter_context(nc.semaphore("s_" + e)) for e in ("pe", "act", "dve", "pool", "sp")}
            dsem = {}
            for b in self.dma_bufs:
                dsem[id(b)] = es.enter_context(nc.semaphore("d_" + b.name))
            block = es.enter_context(nc.Block())

            def run(ename, eng):
                seen = {}
                for o in self.q[ename]:
                    need = {}
                    for d in o.deps:
                        if d.dbuf is not None:
                            sem, val, key = dsem[id(d.dbuf)], d.dval, id(d.dbuf)
                        else:
                            sem, val, key = esem[d.eng], d.ticket, d.eng
                        if key not in need or need[key][1] < val:
                            need[key] = (sem, val)
                    for key, (sem, val) in need.items():
                        if seen.get(key, 0) < val:
                            eng.wait_ge(sem, val)
                            seen[key] = val
                    ins = o.fn(eng)
                    if o.dbuf is not None:
                        ins.then_inc(dsem[id(o.dbuf)], 16)
                    elif o.signal:
                        ins.then_inc(esem[ename], 1)
                if ename == "sp":
                    for b in final_waits:
                        if seen.get(id(b), 0) < b.semval:
                            eng.wait_ge(dsem[id(b)], b.semval)

            @block.tensor
            def _(e):
                run("pe", e)

            @block.scalar
            def _(e):
                run("act", e)

            @block.vector
            def _(e):
                run("dve", e)

            @block.gpsimd
            def _(e):
                run("pool", e)

            @block.sync
            def _(e):
                run("sp", e)


def _fm_piece(W, col0):
    K = W.shape[0] // P
    return W[:, col0:col0 + P].reshape(K, P, P).transpose(1, 0, 2).reshape(P, K * P)


def _tm_piece(W, kq, col0):
    return W[kq * 512:(kq + 1) * 512, col0:col0 + 512].reshape(4, P, 512).transpose(1, 0, 2).reshape(P, 2048)


def piece_plan():
    pl = []
    for hp in range(2):
        for h in (2 * hp, 2 * hp + 1):
            for c in range(2):
                pl.append(("qr", "fm_in", QR + h * 256 + c * P))
        for h in (2 * hp, 2 * hp + 1):
            for c in range(2):
                pl.append(("kr", "fm_in", KR + h * 256 + c * P))
        for kq in range(4):
            pl.append(("vr", "tm_in", (kq, VR + hp * 512)))
        for kq in range(4):
            pl.append(("zr", "tm_in", (kq, ZR + hp * 512)))
    for hp in range(2):
        for h in (2 * hp, 2 * hp + 1):
            pl.append(("qg", "fm_in", QG + h * P))
        for h in (2 * hp, 2 * hp + 1):
            pl.append(("kg", "fm_in", KG + h * P))
        for kq in range(4):
            pl.append(("vg", "tm_in", (kq, VG + hp * 512)))
        for kq in range(4):
            pl.append(("zg", "tm_in", (kq, ZG + hp * 512)))
    for cg in range(4):
        for kq in range(4):
            pl.append(("wo", "tm_out", (kq, cg * 512)))
    for j in range(NFF):
        pl.append(("uv", "fm_up", j * P))
        pl.append(("ug", "fm_up", DFF + j * P))
    for cg in range(4):
        for kq in range(NFF // 4):
            pl.append(("wd", "tm_down", (kq, cg * 512)))
    return pl


PLAN = piece_plan()
NPIECE = len(PLAN)
STATE_TAGS = ("kr", "vr", "kg", "vg")


def build_pieces(w_in, w_out, w_up, w_down):
    out = np.empty((NPIECE, P, 2048), np.float32)
    for i, (tag, kind, a) in enumerate(PLAN):
        if kind == "fm_in":
            out[i] = _fm_piece(w_in, a)
        elif kind == "tm_in":
            out[i] = _tm_piece(w_in, a[0], a[1])
        elif kind == "tm_out":
            out[i] = _tm_piece(w_out, a[0], a[1])
        elif kind == "fm_up":
            out[i] = _fm_piece(w_up, a)
        else:
            out[i] = _tm_piece(w_down, a[0], a[1])
    return out


def host_consts():
    gam = 1.0 - 2.0 ** (-5.0 - np.arange(4, dtype=np.float64))
    idx = np.arange(P)
    c = {}
    c["ident"] = np.eye(P, dtype=np.float32)
    mr = np.zeros((P, 4, P), np.float64)
    for h in range(4):
        dist = np.abs(idx[None, :] - idx[:, None])
        ok = (idx[:, None] // 64) <= (idx[None, :] // 64)
        mr[:, h, :] = np.where(ok, gam[h] ** dist / 16.0, 0.0)
    c["maskR"] = mr.astype(np.float32).reshape(P, 4 * P)
    mc = (idx[:, None] <= idx[None, :]).astype(np.float32)
    ma = ((idx[:, None] > idx[None, :]) & ((idx[:, None] // 64) == (idx[None, :] // 64))).astype(np.float32)
    c["maskG"] = np.concatenate([mc, ma], axis=1)
    c["Lmat"] = np.where(idx[:, None] <= idx[None, :], -1.0 / 16.0, 0.0).astype(np.float32)
    gq = np.zeros((P, 4, P), np.float64)
    for h in range(4):
        gq[:, h, :] = (gam[h] ** (idx + 1.0))[None, :]
    c["gq"] = gq.astype(np.float32).reshape(P, 4 * P)
    gk = np.zeros((P, 4), np.float64)
    for h in range(4):
        gk[:, h] = gam[h] ** (127.0 - idx) / 16.0
    c["gk"] = gk.astype(np.float32)
    c["g128"] = [float(g ** 128.0) for g in gam]
    return c


def rope_tables(pos):
    half = 128
    inv_freq = (np.float32(10000.0) ** (-(np.arange(half, dtype=np.float32)) / np.float32(half))).astype(np.float32)
    ang = (pos.astype(np.float32)[None, :] * inv_freq[:, None]).astype(np.float32)
    a64 = ang.astype(np.float64)
    return np.cos(a64).astype(np.float32), np.sin(a64).astype(np.float32)


def build_program(modes, dbg=None):
    NT = len(modes)
    NTOK = NT * T
    n_out_tiles = sum(1 for m in modes if m == "full")
    first_full = modes.index("full")
    C = host_consts()
    g128 = C["g128"]

    nc = bass.Bass("TRN2", target_bir_lowering=False)
    x_d = nc.dram_tensor("x", [NTOK, D], F32, kind="ExternalInput").ap()
    cos_d = nc.dram_tensor("cos", [P, NTOK], F32, kind="ExternalInput").ap()
    sin_d = nc.dram_tensor("sin", [P, NTOK], F32, kind="ExternalInput").ap()
    wp_d = nc.dram_tensor("wp", [NPIECE, P, 2048], F32, kind="ExternalInput").ap()
    agw_d = nc.dram_tensor("agw", [P, 256], F32, kind="ExternalInput").ap()
    cst_d = nc.dram_tensor("cst", [P, 1796], F32, kind="ExternalInput").ap()
    rep_d = nc.dram_tensor("rep", [5, P, 2048], F32, kind="ExternalInput").ap()
    wconv_d = nc.dram_tensor("wconv", [P, 88 * 4], F32, kind="ExternalInput").ap()
    wg_d = nc.dram_tensor("wg", [16, 512], F32, kind="ExternalInput").ap()
    bg_d = nc.dram_tensor("bg", [1, 640], F32, kind="ExternalInput").ap()
    flag_d = nc.dram_tensor("flag", [P, 1], F32, kind="ExternalInput").ap()
    lncol_d = nc.dram_tensor("lncol", [P, 32], F32, kind="ExternalInput").ap()
    out_d = nc.dram_tensor("out", [n_out_tiles * T, D], F32, kind="ExternalOutput").ap()
    wb_d = nc.dram_tensor("wbf", [NPIECE, P, 2048], BF16, kind="Internal").ap()
    dbg_d = None
    if dbg is not None:
        dbg_d = nc.dram_tensor("dbg", list(dbg), F32, kind="ExternalOutput").ap()

    S = Sched()
    es = ExitStack()

    def sb(name, shape, dt):
        return es.enter_context(nc.sbuf_tensor("s_" + name, shape, dt))

    cst = sb("cst", [P, 1796], F32)
    ident_f = cst[:, 0:128]
    maskR = cst[:, 128:640]
    maskG = cst[:, 640:896]
    Lmat = cst[:, 896:1024]
    gq = cst[:, 1024:1536]
    gk = cst[:, 1536:1540]
    neghalf = cst[:, 1540:1796]
    ident_b = sb("identb", [P, P], BF16)
    rep = sb("rep", [P, 2048], F32)
    rep2 = sb("rep2", [P, 2048], F32)
    wconv = sb("wconv", [P, 88 * 4], F32)
    wg = sb("wg", [16, 512], F32)
    bg = sb("bg", [1, 640], F32)
    agw_f = sb("agwf", [P, 256], F32)
    agw = sb("agw", [P, 256], BF16)
    flag = sb("flag", [P, 1], F32)
    lncol = sb("lncol", [P, 32], F32)
    Sr = sb("Sr", [P, 4 * 512], F32)
    Srb = sb("Srb", [P, 4 * 512], BF16)
    Sg = sb("Sg", [P, 4 * 256], F32)
    Sgb = sb("Sgb", [P, 4 * 256], BF16)
    carry = sb("carry", [P, 88 * 2], F32)
    Y = sb("Y", [P, NB * D], F32)
    wring = sb("wring", [P, NSLOT * 2048], BF16)
    small = sb("small", [P, 64], F32)
    UB = 101 * 1024
    U = sb("U", [P, UB // 2], BF16)

    class Carver:
        def __init__(self):
            self.off = 0

        def take(self, nbytes, dt):
            a = self.off
            self.off += (nbytes + 31) // 32 * 32
            assert self.off <= UB, (self.off, UB)
            ap = U[:, a // 2:(a + nbytes) // 2]
            return ap if dt == BF16 else ap.bitcast(F32)

    cv = Carver()
    x1T = cv.take(16 * 512 * 2, BF16)
    actT = cv.take(NFF * 512 * 2, BF16)
    ue = [cv.take(514 * 4, F32) for _ in range(4)]
    acc = [cv.take(512 * 4, F32) for _ in range(4)]
    sgb = [cv.take(512 * 4, F32) for _ in range(2)]
    bd_end = cv.off
    cv = Carver()
    xT = cv.take(16 * 512 * 2, BF16)
    cosb = cv.take(512 * 4, F32)
    sinb = cv.take(512 * 4, F32)
    qkb = [cv.take(512 * 2, BF16) for _ in range(12)]
    rt = [cv.take(512 * 4, F32) for _ in range(3)]
    vpair = cv.take(NB * 512 * 2, BF16)
    gz = cv.take(NB * 512 * 4, F32)
    spb = cv.take(NB * 512 * 4, F32)
    agT = rt[1]
    Eb = [cv.take(512 * 4, F32) for _ in range(4)]
    e1 = rt[2]
    sTb = [cv.take(256 * 2, BF16) for _ in range(4)]
    ktok = [cv.take(256 * 2, BF16) for _ in range(4)]
    onb = [cv.take(256 * 4, F32) for _ in range(4)]
    S1 = [cv.take(256 * 4, F32) for _ in range(2)]
    opairs = [cv.take(NB * 512 * 2, BF16) for _ in range(2)]
    cv.off = max(cv.off, bd_end)
    oT = cv.take(16 * 512 * 2, BF16)

    ps = [es.enter_context(nc.psum_tensor("ps%d" % i, [P, 512], F32)) for i in range(8)]

    def B(name, const=False):
        return Buf(name, const)

    b_cst = B("cst", True)
    b_rep = B("rep")
    b_rep2 = B("rep2")
    b_ps = [B("ps%d" % i) for i in range(8)]
    b_slot = [B("slot%d" % i) for i in range(NSLOT)]
    b_Y = [B("Y%d" % i) for i in range(NB)]
    b_xT = B("xT")
    b_x1T = B("x1T")
    b_oT = [B("oT%d" % i) for i in range(16)]
    b_actT = [B("actT%d" % i) for i in range(NFF)]
    b_rope = B("rope")
    b_qk = [B("qk%d" % i) for i in range(12)]
    b_rt = [B("rt%d" % i) for i in range(3)]
    b_vp = [B("vp%d" % i) for i in range(NB)]
    b_gz = [B("gz%d" % i) for i in range(NB)]
    b_sp = [B("sp%d" % i) for i in range(NB)]
    b_agT = b_rt[1]
    b_E = [B("E%d" % i) for i in range(4)]
    b_e1 = b_rt[2]
    b_sT = [B("sT%d" % i) for i in range(4)]
    b_kt = [B("kt%d" % i) for i in range(4)]
    b_on = [B("on%d" % i) for i in range(4)]
    b_S1 = [B("S1%d" % i) for i in range(2)]
    b_ops = [[B("op%d_%d" % (j, i)) for i in range(NB)] for j in range(2)]
    b_Sr = [B("Sr%d" % i) for i in range(4)]
    b_Srb = [B("Srb%d" % i) for i in range(4)]
    b_Sg = [B("Sg%d" % i) for i in range(4)]
    b_Sgb = [B("Sgb%d" % i) for i in range(4)]
    b_carry = B("carry")
    b_ue = [B("ue%d" % i) for i in range(4)]
    b_acc = [B("acc%d" % i) for i in range(4)]
    b_sg = [B("sg%d" % i) for i in range(2)]
    b_small = [B("small%d" % i) for i in range(8)]
    b_wb = B("wbf")
    b_xb = [B("xb%d" % i) for i in range(NB)]
    b_dbg = B("dbg")

    psn = [0]

    def bank():
        i = psn[0] % 8
        psn[0] += 1
        return ps[i], b_ps[i]

    def banks4():
        while psn[0] % 4 != 0:
            psn[0] += 1
        return [bank() for _ in range(4)]

    def mm(out, lhsT, rhs, start, stop, reads, writes):
        S.op("pe", lambda e: e.matmul(out, lhsT, rhs, start=start, stop=stop), reads, writes)

    def tp(out, in_, ident, reads, writes):
        S.op("pe", lambda e: e.transpose(out, in_, ident), reads + [b_cst], writes)

    def act(out, in_, func, reads, writes, scale=1.0, bias=0.0):
        S.op("act", lambda e: e.activation(out, in_, func, bias=bias, scale=scale), reads, writes)

    def tt(eng, out, a, b, op, reads, writes):
        S.op(eng, lambda e: e.tensor_tensor(out, a, b, op), reads, writes)

    def ts(eng, out, a, s1, s2, op0, op1, reads, writes):
        if s2 is None:
            S.op(eng, lambda e: e.tensor_scalar(out, a, s1, None, op0), reads, writes)
        else:
            S.op(eng, lambda e: e.tensor_scalar(out, a, s1, s2, op0, op1), reads, writes)

    def stt(out, a, s, b, op0, op1, reads, writes):
        S.op("dve", lambda e: e.scalar_tensor_tensor(out, a, s, b, op0, op1), reads, writes)

    def cp(eng, out, in_, reads, writes):
        S.op(eng, lambda e: e.tensor_copy(out, in_), reads, writes)

    def dma(eng, out, in_, reads, writes, dbuf):
        return S.op(eng, lambda e: e.dma_start(out=out, in_=in_), reads, writes, dbuf)

    b_cl = B("cload")
    for (dst, src) in ((cst[:, :], cst_d), (wconv[:, :], wconv_d), (wg[:, :], wg_d), (bg[:, :], bg_d),
                       (agw_f[:, :], agw_d), (flag[:, :], flag_d), (lncol[:, :], lncol_d)):
        dma("sp", dst, src, [], [b_cst], b_cl)
    S.op("dve", lambda e: e.tensor_copy(ident_b[:, :], ident_f), [b_cst], [b_cst])
    S.op("dve", lambda e: e.tensor_copy(agw[:, :], agw_f[:, :]), [b_cst], [b_cst])
    for t_, n_ in ((Sr, 2048), (Sg, 1024), (carry, 176)):
        S.op("pool", (lambda t_: (lambda e: e.memset(t_[:, :], 0.0)))(t_), [], [b_cst])
    for t_ in (Srb, Sgb):
        S.op("pool", (lambda t_: (lambda e: e.memset(t_[:, :], 0.0)))(t_), [], [b_cst])
    for blk in range(NB):
        S.op("pool", (lambda blk: (lambda e: e.dma_start(
            out=x3(oT, NB)[:, blk, :], in_=x_d[blk * P:(blk + 1) * P, :])))(blk),
            [], [b_oT[4 * blk + i] for i in range(4)], b_xb[blk])
    b_wb2 = B("wbf2")
    last_cast = {}
    order = [i for i in range(NPIECE) if PLAN[i][0] in STATE_TAGS] + [i for i in range(NPIECE) if PLAN[i][0] not in STATE_TAGS]
    for i in order:
        grp = 0 if PLAN[i][0] in STATE_TAGS else 1
        last_cast[grp] = S.op("pool", (lambda i: (lambda e: e.dma_start(out=wb_d[i], in_=wp_d[i])))(i), [], [],
                              b_wb if grp == 0 else b_wb2, track=False)
    S.barrier()

    stream = []
    for ti, m in enumerate(modes):
        for pi, (tag, kind, a) in enumerate(PLAN):
            if m == "state" and tag not in STATE_TAGS:
                continue
            if m == "halo" and tag == "wd":
                continue
            stream.append((ti, pi, tag))
    wpos = {"issued": 0, "used": 0}

    def issue_loads(upto):
        while wpos["issued"] < min(upto, len(stream)):
            k = wpos["issued"]
            _, pi, _ = stream[k]
            sl = k % NSLOT
            grp = 0 if PLAN[pi][0] in STATE_TAGS else 1
            S.op("sp", (lambda sl, pi: (lambda e: e.dma_start(out=wring[:, sl * 2048:(sl + 1) * 2048], in_=wb_d[pi])))(sl, pi),
                 [], [b_slot[sl]], b_slot[sl], extra=[last_cast[grp]])
            wpos["issued"] += 1

    def next_piece(ti, tag):
        k = wpos["used"]
        assert stream[k][0] == ti and stream[k][2] == tag, (stream[k], ti, tag)
        issue_loads(k + NSLOT)
        wpos["used"] += 1
        sl = k % NSLOT
        return wring[:, sl * 2048:(sl + 1) * 2048], b_slot[sl]

    def fm_group(ti, tag, src, b_src, M=P, c0=0, c1=512):
        w, bw = next_piece(ti, tag)
        pb, bb = bank()
        for k in range(16):
            mm(pb[0:M, 0:c1 - c0], w[:, k * P:k * P + M], src[:, k * 512 + c0:k * 512 + c1], k == 0, k == 15,
               [bw, b_src], [bb])
        return pb, bb

    def tm_group(ti, tag, nk, lhs_fn, blks=range(NB)):
        bks = banks4()
        for kq in range(nk // 4):
            w, bw = next_piece(ti, tag)
            for kk in range(4):
                k = kq * 4 + kk
                for blk in blks:
                    l, bl = lhs_fn(k, blk)
                    mm(bks[blk][0][:, :], l, w[:, kk * 512:(kk + 1) * 512], k == 0, k == nk - 1,
                       [bw, bl], [bks[blk][1]])
        return bks

    def x3(ap, a):
        return ap.rearrange("p (a b) -> p a b", a=a)

    out_tile = [0]

    def transposes_to(dstT, b_dst, blks=range(NB)):
        d3 = x3(dstT, 16)
        for blk in blks:
            for j in range(4):
                pb, bb = bank()
                for c in range(4):
                    fc = 4 * j + c
                    tp(pb[:, c * P:(c + 1) * P], Y[:, blk * D + fc * P: blk * D + (fc + 1) * P], ident_f,
                       [b_Y[blk]], [bb])
                eng_copy = "act" if (j % 2 == 0) else "dve"
                o_ap = d3[:, 4 * j:4 * j + 4, blk * P:(blk + 1) * P]
                i_ap = x3(pb[:, :], 4)
                if eng_copy == "act":
                    act(o_ap, i_ap, AF.Copy, [bb], [b_dst])
                else:
                    S.op("dve", (lambda o_ap, i_ap: (lambda e: e.tensor_copy(o_ap, i_ap)))(o_ap, i_ap), [bb], [b_dst])

    def ln_stats(blk):
        st = small[:, 0:24]
        mv = small[:, 24:26]
        rs = small[:, 26:27]
        yb = Y[:, blk * D:(blk + 1) * D]
        for c in range(4):
            S.op("dve", (lambda c: (lambda e: e.bn_stats(st[:, c * 6:(c + 1) * 6], yb[:, c * 512:(c + 1) * 512])))(c),
                 [b_Y[blk]], [b_small[0]])
        S.op("dve", lambda e: e.bn_aggr(mv, st), [b_small[0]], [b_small[1]])
        ts("pool", rs, mv[:, 1:2], EPS, None, ALU.add, None, [b_small[1]], [b_small[2]])
        tt("pool", rs, rs, neghalf[:, 0:1], ALU.pow, [b_small[2], b_cst], [b_small[2]])
        return yb, mv, rs

    xbf = x3(oT, NB)

    def prefetch_x(ti):
        if ti >= NT:
            return
        tok0 = ti * T
        for blk in range(NB):
            S.op("pool", (lambda blk, tok0: (lambda e: e.dma_start(
                out=xbf[:, blk, :], in_=x_d[tok0 + blk * P: tok0 + (blk + 1) * P, :])))(blk, tok0),
                [], [b_oT[4 * blk + i] for i in range(4)], b_xb[blk])

    def transposes_x():
        d3 = x3(xT, 16)
        for blk in range(NB):
            for j in range(2):
                pb, bb = bank()
                pbb = pb[:, :].bitcast(BF16)
                for c in range(8):
                    fc = 8 * j + c
                    tp(pbb[:, c * P:(c + 1) * P], xbf[:, blk, fc * P:(fc + 1) * P], ident_b[:, :],
                       [b_oT[4 * blk + fc // 4]], [bb])
                act(d3[:, 8 * j:8 * j + 8, blk * P:(blk + 1) * P], x3(pbb[:, 0:1024], 8), AF.Copy, [bb], [b_xT])

    def transposes_x1(blks):
        d3 = x3(x1T, 16)
        for blk in blks:
            for j in range(4):
                pb, bb = bank()
                for c in range(4):
                    fc = 4 * j + c
                    tp(pb[:, c * P:(c + 1) * P], Y[:, blk * D + fc * P: blk * D + (fc + 1) * P], ident_f,
                       [b_Y[blk]], [bb])
                for c in range(4):
                    fc = 4 * j + c
                    if c % 2 == 0:
                        act(d3[:, fc, blk * P:(blk + 1) * P], pb[:, c * P:(c + 1) * P], AF.Identity, [bb, b_cst],
                            [b_x1T], scale=lncol[:, fc:fc + 1], bias=lncol[:, 16 + fc:17 + fc])
                    else:
                        ts("dve", d3[:, fc, blk * P:(blk + 1) * P], pb[:, c * P:(c + 1) * P], lncol[:, fc:fc + 1],
                           lncol[:, 16 + fc:17 + fc], ALU.mult, ALU.add, [bb, b_cst], [b_x1T])

    def layer_norm1(blk):
        yb, mv, rs = ln_stats(blk)
        ts("dve", yb, yb, mv[:, 0:1], rs, ALU.subtract, ALU.mult, [b_Y[blk], b_small[1], b_small[2]], [b_Y[blk]])

    def layer_norm2(blk):
        yb, mv, rs = ln_stats(blk)
        stt(yb, yb, mv[:, 0:1], rep[:, :], ALU.subtract, ALU.mult, [b_Y[blk], b_small[1], b_rep], [b_Y[blk]])
        stt(yb, yb, rs, rep2[:, :], ALU.mult, ALU.add, [b_Y[blk], b_small[2], b_rep2], [b_Y[blk]])

    def layer_norm(blk, si):
        st = small[:, 0:24]
        mv = small[:, 24:26]
        rs = small[:, 26:27]
        yb = Y[:, blk * D:(blk + 1) * D]
        for c in range(4):
            S.op("dve", (lambda c: (lambda e: e.bn_stats(st[:, c * 6:(c + 1) * 6], yb[:, c * 512:(c + 1) * 512])))(c),
                 [b_Y[blk]], [b_small[0]])
        S.op("dve", lambda e: e.bn_aggr(mv, st), [b_small[0]], [b_small[1]])
        ts("pool", rs, mv[:, 1:2], EPS, None, ALU.add, None, [b_small[1]], [b_small[2]])
        tt("pool", rs, rs, neghalf[:, 0:1], ALU.pow, [b_small[2], b_cst], [b_small[2]])
        ts("dve", yb, yb, mv[:, 0:1], rs, ALU.subtract, ALU.mult, [b_Y[blk], b_small[1], b_small[2]], [b_Y[blk]])
        tt("dve", yb, yb, rep[:, :], ALU.mult, [b_Y[blk], b_rep], [b_Y[blk]])
        tt("pool", yb, yb, rep2[:, :], ALU.add, [b_Y[blk], b_rep2], [b_Y[blk]])

    pidx = [0]
    pending_o = []

    def flush_o():
        while pending_o:
            pi_, fc0_ = pending_o.pop(0)
            opair_to_oT(pi_, fc0_)

    def head_norm(pb, bb, hl, blk, center, si):
        st = small[:, 32 + si * 16: 32 + si * 16 + 6]
        mv = small[:, 32 + si * 16 + 6: 32 + si * 16 + 8]
        rs = small[:, 32 + si * 16 + 8: 32 + si * 16 + 9]
        m2 = small[:, 32 + si * 16 + 9: 32 + si * 16 + 10]
        bs = b_small[3 + si]
        oi = si * 2 + blk % 2
        on = onb[oi]
        act(on, pb[:, 0:256], AF.Copy, [bb], [b_on[oi]])
        S.op("dve", lambda e: e.bn_stats(st, on), [b_on[oi]], [bs])
        S.op("dve", lambda e: e.bn_aggr(mv, st), [bs], [bs])
        if center:
            ts("pool", rs, mv[:, 1:2], EPS, None, ALU.add, None, [bs], [bs])
        else:
            tt("pool", m2, mv[:, 0:1], mv[:, 0:1], ALU.mult, [bs], [bs])
            ts("pool", rs, m2, EPS, mv[:, 1:2], ALU.add, ALU.add, [bs], [bs])
        tt("pool", rs, rs, neghalf[:, 0:1], ALU.pow, [bs, b_cst], [bs])
        if center:
            ts("dve", on, on, mv[:, 0:1], rs, ALU.subtract, ALU.mult, [b_on[oi], bs], [b_on[oi]])
        else:
            ts("dve", on, on, rs, None, ALU.mult, None, [b_on[oi], bs], [b_on[oi]])
        g3 = x3(gz, NB)
        o3 = x3(opairs[pidx[0] % 2], NB)
        tt("pool", o3[:, blk, hl * 256:(hl + 1) * 256], on, g3[:, blk, hl * 256:(hl + 1) * 256], ALU.mult,
           [b_on[oi], b_gz[blk]], [b_ops[pidx[0] % 2][blk]])

    def vz_groups(ti, vtag, ztag, gcol, full):
        xT3 = x3(xT, 16)

        def lhs(k, blk):
            return xT3[:, k, blk * P:(blk + 1) * P], b_xT
        bks = tm_group(ti, vtag, 16, lhs)
        v3 = x3(vpair, NB)
        for blk in range(NB):
            if blk % 2 == 0:
                act(v3[:, blk, :], bks[blk][0][:, :], AF.Copy, [bks[blk][1]], [b_vp[blk]])
            else:
                cp("dve", v3[:, blk, :], bks[blk][0][:, :], [bks[blk][1]], [b_vp[blk]])
        if full:
            bks = tm_group(ti, ztag, 16, lhs)
            g3 = x3(gz, NB)
            for blk in range(NB):
                act(g3[:, blk, :], bks[blk][0][:, :], AF.Silu, [bks[blk][1]], [b_gz[blk]])
                tt("pool", g3[:, blk, :], g3[:, blk, :], rep[:, gcol:gcol + 512], ALU.mult,
                   [b_gz[blk], b_rep], [b_gz[blk]])

    def opair_to_oT(pi_, fc0):
        o3 = x3(opairs[pi_ % 2], NB)
        b_op = b_ops[pi_ % 2]
        oT3 = x3(oT, 16)
        for c in range(4):
            pb, bb = bank()
            pbb = pb[:, :].bitcast(BF16)
            for blk in range(NB):
                tp(pbb[:, blk * P:(blk + 1) * P], o3[:, blk, c * P:(c + 1) * P], ident_b[:, :], [b_op[blk]], [bb])
            if c % 2 == 0:
                act(oT3[:, fc0 + c, :], pbb[:, 0:512], AF.Copy, [bb], [b_oT[fc0 + c]])
            else:
                S.op("dve", (lambda c, pbb: (lambda e: e.tensor_copy(oT3[:, fc0 + c, :], pbb[:, 0:512])))(c, pbb),
                     [bb], [b_oT[fc0 + c]])

    def do_tile(ti, mode):
        full = mode != "state"
        tok0 = ti * T
        if ti == 0 or modes[ti - 1] != "full":
            S.barrier()
        dma("sp", cosb, cos_d[:, tok0:tok0 + T], [], [b_rope], b_rope)
        dma("sp", sinb, sin_d[:, tok0:tok0 + T], [], [b_rope], b_rope)
        transposes_x()
        if not full:
            prefetch_x(ti + 1)

        pb, bb = bank()
        for k in range(16):
            mm(pb[0:16, :], agw[:, k * 16:(k + 1) * 16], xT[:, k * 512:(k + 1) * 512], k == 0, k == 15,
               [b_cst, b_xT], [bb])
        act(agT[0:16, :], pb[0:16, :], AF.Copy, [bb], [b_agT])
        sp3 = x3(spb, NB)
        for blk in range(NB):
            pb, bb = bank()
            mm(pb[:, :], agT[0:16, blk * P:(blk + 1) * P], wg[:, :], True, False, [b_agT, b_cst], [bb])
            mm(pb[:, :], bg[0:1, 512:640], bg[0:1, 0:512], False, True, [b_cst], [bb])
            act(e1, pb[:, :], AF.Exp, [bb], [b_e1], scale=-1.0)
            act(sp3[:, blk, :], e1, AF.Ln, [b_e1], [b_sp[blk]], bias=1.0)

        for hp in range(2):
            qb = {}
            for which in (("qr",) if full else ()) + ("kr",):
                if which == "qr":
                    pass
                for hl in range(2):
                    h = 2 * hp + hl
                    pb0, bb0 = fm_group(ti, which, xT, b_xT)
                    pb1, bb1 = fm_group(ti, which, xT, b_xT)
                    x1_, x2_ = pb0[:, :], pb1[:, :]
                    if which == "qr":
                        i0 = hl * 4
                        q0, q1, qd0, qd1 = qkb[i0], qkb[i0 + 1], qkb[i0 + 2], qkb[i0 + 3]
                        bq = b_qk[i0:i0 + 4]
                        gqb = gq[:, h * P:(h + 1) * P].unsqueeze(1).broadcast_to([P, NB, P])
                        tt("dve", rt[0], x1_, cosb, ALU.mult, [bb0, b_rope], [b_rt[0]])
                        tt("dve", rt[1], x2_, sinb, ALU.mult, [bb1, b_rope], [b_rt[1]])
                        tt("dve", rt[2], rt[0], rt[1], ALU.subtract, [b_rt[0], b_rt[1]], [b_rt[2]])
                        act(q0, rt[2], AF.Copy, [b_rt[2]], [bq[0]])
                        tt("dve", x3(qd0, NB), x3(rt[2], NB), gqb, ALU.mult, [b_rt[2], b_cst], [bq[2]])
                        tt("dve", rt[0], x1_, sinb, ALU.mult, [bb0, b_rope], [b_rt[0]])
                        tt("dve", rt[1], x2_, cosb, ALU.mult, [bb1, b_rope], [b_rt[1]])
                        tt("dve", rt[2], rt[0], rt[1], ALU.add, [b_rt[0], b_rt[1]], [b_rt[2]])
                        act(q1, rt[2], AF.Copy, [b_rt[2]], [bq[1]])
                        tt("dve", x3(qd1, NB), x3(rt[2], NB), gqb, ALU.mult, [b_rt[2], b_cst], [bq[3]])
                    else:
                        i0 = 8 + hl * 2
                        k0, k1 = qkb[i0], qkb[i0 + 1]
                        tt("dve", rt[0], x1_, cosb, ALU.mult, [bb0, b_rope], [b_rt[0]])
                        tt("dve", rt[1], x2_, sinb, ALU.mult, [bb1, b_rope], [b_rt[1]])
                        tt("dve", k0, rt[0], rt[1], ALU.subtract, [b_rt[0], b_rt[1]], [b_qk[i0]])
                        tt("dve", rt[0], x1_, sinb, ALU.mult, [bb0, b_rope], [b_rt[0]])
                        tt("dve", rt[1], x2_, cosb, ALU.mult, [bb1, b_rope], [b_rt[1]])
                        tt("dve", k1, rt[0], rt[1], ALU.add, [b_rt[0], b_rt[1]], [b_qk[i0 + 1]])
            if full and hp == 0:
                dma("sp", rep[:, :], rep_d[0], [], [b_rep], b_rep)
            vz_groups(ti, "vr", "zr", hp * 512, full)
            flush_o()
            v3 = x3(vpair, NB)

            def ret_s1(blk, hl):
                h = 2 * hp + hl
                k0, k1 = qkb[8 + hl * 2], qkb[9 + hl * 2]
                bk = [b_qk[8 + hl * 2], b_qk[9 + hl * 2]]
                cs = slice(blk * P, (blk + 1) * P)
                bi = hl * 2 + blk % 2
                if full:
                    q0, q1 = qkb[hl * 4], qkb[hl * 4 + 1]
                    bq = b_qk[hl * 4:hl * 4 + 4]
                    pbs, bbs = bank()
                    mm(pbs[:, 0:P], k0[:, cs], q0[:, cs], True, False, [bk[0], bq[0]], [bbs])
                    mm(pbs[:, 0:P], k1[:, cs], q1[:, cs], False, True, [bk[1], bq[1]], [bbs])
                    tt("dve", sTb[bi][:, 0:P], pbs[:, 0:P], maskR[:, h * P:(h + 1) * P], ALU.mult,
                       [bbs, b_cst], [b_sT[bi]])
                pbk, bbk = bank()
                pkb = pbk[:, :].bitcast(BF16)
                tp(pkb[:, 0:P], k0[:, cs], ident_b[:, :], [bk[0]], [bbk])
                tp(pkb[:, P:2 * P], k1[:, cs], ident_b[:, :], [bk[1]], [bbk])
                act(ktok[bi], pkb[:, 0:256], AF.Identity, [bbk, b_cst], [b_kt[bi]], scale=gk[:, h:h + 1])

            def ret_s2(blk, hl):
                h = 2 * hp + hl
                cs = slice(blk * P, (blk + 1) * P)
                bi = hl * 2 + blk % 2
                vv = v3[:, blk, hl * 256:(hl + 1) * 256]
                pbd, bbd = bank()
                mm(pbd[:, 0:256], ktok[bi][:, 0:P], vv, True, True, [b_kt[bi], b_vp[blk]], [bbd])
                mm(pbd[:, 256:512], ktok[bi][:, P:2 * P], vv, True, True, [b_kt[bi], b_vp[blk]], [bbd])
                if full:
                    qd0, qd1 = qkb[hl * 4 + 2], qkb[hl * 4 + 3]
                    bq = b_qk[hl * 4:hl * 4 + 4]
                    pbo, bbo = bank()
                    mm(pbo[:, 0:256], sTb[bi][:, 0:P], vv, True, False, [b_sT[bi], b_vp[blk]], [bbo])
                    mm(pbo[:, 0:256], qd0[:, cs], Srb[:, h * 512:h * 512 + 256], False, False,
                       [bq[2], b_Srb[h]], [bbo])
                    mm(pbo[:, 0:256], qd1[:, cs], Srb[:, h * 512 + 256:h * 512 + 512], False, True,
                       [bq[3], b_Srb[h]], [bbo])
                sr = Sr[:, h * 512:(h + 1) * 512]
                stt(sr, sr, g128[h], pbd[:, :], ALU.mult, ALU.add, [b_Sr[h], bbd], [b_Sr[h]])
                act(Srb[:, h * 512:(h + 1) * 512], sr, AF.Copy, [b_Sr[h]], [b_Srb[h]])
                if full:
                    head_norm(pbo, bbo, hl, blk, True, hl)

            for step in range(NB + 1):
                if step < NB:
                    for hl in range(2):
                        ret_s1(step, hl)
                if step >= 1:
                    for hl in range(2):
                        ret_s2(step - 1, hl)
            if full:
                pending_o.append((pidx[0], hp * 4))
                pidx[0] += 1

        if full:
            for blk in ([NB - 1] if mode == "halo" else range(NB)):
                dma("sp", Y[:, blk * D:(blk + 1) * D], x_d[tok0 + blk * P: tok0 + (blk + 1) * P, :], [], [b_Y[blk]], b_Y[blk])
        for hp in range(2):
            qps = {}
            for which in (("qg",) if full else ()) + ("kg",):
                for hl in range(2):
                    qps[(which, hl)] = fm_group(ti, which, xT, b_xT)
            for hl in range(2):
                h = 2 * hp + hl
                pbB, bbB = bank()
                for blk in range(NB):
                    mm(pbB[:, blk * P:(blk + 1) * P], sp3[:, blk, h * P:(h + 1) * P], Lmat, True, True,
                       [b_sp[blk], b_cst], [bbB])
                Ep, En = Eb[hl * 2], Eb[hl * 2 + 1]
                act(Ep, pbB[:, :], AF.Exp, [bbB], [b_E[hl * 2]])
                act(En, pbB[:, :], AF.Exp, [bbB], [b_E[hl * 2 + 1]], scale=-1.0)
                i0 = hl * 4
                qP, qN, kP, kN = qkb[i0], qkb[i0 + 1], qkb[i0 + 2], qkb[i0 + 3]
                sc = 128.0 ** -0.5
                if full:
                    pq, bq_ = qps[("qg", hl)]
                    stt(qP, pq[:, :], sc, Ep, ALU.mult, ALU.mult, [bq_, b_E[hl * 2]], [b_qk[i0]])
                    stt(qN, pq[:, :], sc, En, ALU.mult, ALU.mult, [bq_, b_E[hl * 2 + 1]], [b_qk[i0 + 1]])
                pk, bk_ = qps[("kg", hl)]
                if full:
                    tt("dve", kP, pk[:, :], Ep, ALU.mult, [bk_, b_E[hl * 2]], [b_qk[i0 + 2]])
                tt("dve", kN, pk[:, :], En, ALU.mult, [bk_, b_E[hl * 2 + 1]], [b_qk[i0 + 3]])
            vz_groups(ti, "vg", "zg", 1024 + hp * 512, full)
            flush_o()
            v3 = x3(vpair, NB)

            def gla_s1(blk, hl):
                i0 = hl * 4
                qP, qN, kP, kN = qkb[i0], qkb[i0 + 1], qkb[i0 + 2], qkb[i0 + 3]
                bqP, bqN, bkP, bkN = b_qk[i0:i0 + 4]
                cs = slice(blk * P, (blk + 1) * P)
                bi = hl * 2 + blk % 2
                if full:
                    pbs, bbs = bank()
                    mm(pbs[:, 0:P], kN[:, cs], qP[:, cs], True, True, [bkN, bqP], [bbs])
                    mm(pbs[:, P:2 * P], kP[:, cs], qN[:, cs], True, True, [bkP, bqN], [bbs])
                    tt("dve", sTb[bi], pbs[:, 0:256], maskG, ALU.mult, [bbs, b_cst], [b_sT[bi]])
                pbk, bbk = bank()
                pkb = pbk[:, :].bitcast(BF16)
                tp(pkb[:, 0:P], kN[:, cs], ident_b[:, :], [bkN], [bbk])
                act(ktok[bi][:, 0:P], pkb[:, 0:P], AF.Copy, [bbk], [b_kt[bi]])

            def gla_s2(blk, hl):
                h = 2 * hp + hl
                i0 = hl * 4
                qP = qkb[i0]
                bqP = b_qk[i0]
                Ep = Eb[hl * 2]
                vv = v3[:, blk, hl * 256:(hl + 1) * 256]
                cs = slice(blk * P, (blk + 1) * P)
                bi = hl * 2 + blk % 2
                sg_ = Sg[:, h * 256:(h + 1) * 256]
                e127 = Ep[:, blk * P + 127: blk * P + 128]
                pbd, bbd = bank()
                mm(pbd[:, 0:256], ktok[bi][:, 0:P], vv, True, True, [b_kt[bi], b_vp[blk]], [bbd])
                if full:
                    pbo, bbo = bank()
                    mm(pbo[:, 0:256], sTb[bi][:, 0:P], vv, True, False, [b_sT[bi], b_vp[blk]], [bbo])
                    mm(pbo[:, 0:256], sTb[bi][:, P:2 * P], vv, False, False, [b_sT[bi], b_vp[blk]], [bbo])
                    mm(pbo[:, 0:256], qP[:, cs], Sgb[:, h * 256:(h + 1) * 256], False, True,
                       [bqP, b_Sgb[h]], [bbo])
                ts("dve", S1[hl], sg_, e127, None, ALU.mult, None, [b_Sg[h], b_E[hl * 2]], [b_S1[hl]])
                stt(sg_, pbd[:, 0:256], e127, S1[hl], ALU.mult, ALU.add, [bbd, b_E[hl * 2], b_S1[hl], b_Sg[h]],
                    [b_Sg[h]])
                act(Sgb[:, h * 256:(h + 1) * 256], sg_, AF.Copy, [b_Sg[h]], [b_Sgb[h]])
                if full:
                    head_norm(pbo, bbo, hl, blk, False, hl)

            for step in range(NB + 1):
                if step < NB:
                    for hl in range(2):
                        gla_s1(step, hl)
                if step >= 1:
                    for hl in range(2):
                        gla_s2(step - 1, hl)
            if full:
                pending_o.append((pidx[0], 8 + hp * 4))
                pidx[0] += 1
        if not full:
            return
        flush_o()

        dma("sp", rep[:, :], rep_d[1], [], [b_rep], b_rep)
        dma("sp", rep2[:, :], rep_d[2], [], [b_rep2], b_rep2)
        oT3 = x3(oT, 16)
        blksB = [NB - 1] if mode == "halo" else list(range(NB))
        for cg in range(4):
            def lhs(k, blk):
                return oT3[:, k, blk * P:(blk + 1) * P], b_oT[k]
            bks = tm_group(ti, "wo", 16, lhs, blksB)
            for blk in blksB:
                yb = Y[:, blk * D + cg * 512: blk * D + (cg + 1) * 512]
                stt(yb, yb, ALPHA, bks[blk][0][:, :], ALU.mult, ALU.add, [b_Y[blk], bks[blk][1]], [b_Y[blk]])
        S.barrier()
        for blk in blksB:
            layer_norm1(blk)
        transposes_x1(blksB)
        prefetch_x(ti + 1)
        deferred = []
        if mode == "full":
            for blk in blksB:
                for cgi in range(4):
                    yc = Y[:, blk * D + cgi * 512: blk * D + (cgi + 1) * 512]
                    deferred.append((yc, rep[:, cgi * 512:(cgi + 1) * 512], ALU.mult, b_rep, blk))
                for cgi in range(4):
                    yc = Y[:, blk * D + cgi * 512: blk * D + (cgi + 1) * 512]
                    deferred.append((yc, rep2[:, cgi * 512:(cgi + 1) * 512], ALU.add, b_rep2, blk))

        if ti == first_full:
            ts("pool", carry[:, :], carry[:, :], flag[:, 0:1], None, ALU.mult, None, [b_carry, b_cst], [b_carry])
        a3 = x3(actT, NFF)
        wc3 = x3(wconv[:, :], 88)
        c3 = x3(carry[:, :], 88)
        if mode == "halo":
            for j in range(NFF):
                for half_, tag in ((0, "uv"), (1, "ug")):
                    ch = j + half_ * NFF
                    pb, bb = fm_group(ti, tag, x1T, b_x1T, c0=384, c1=512)
                    act(c3[:, ch, :], pb[:, 126:128], AF.Copy, [bb], [b_carry])
            return
        for j in range(NFF):
            if deferred:
                yc, rr, op_, brr, blk_ = deferred.pop(0)
                tt("pool", yc, yc, rr, op_, [b_Y[blk_], brr], [b_Y[blk_]])
            res = []
            for half_, tag in ((0, "uv"), (1, "ug")):
                ch = j + half_ * NFF
                pb, bb = fm_group(ti, tag, x1T, b_x1T)
                wi = (2 * j + half_) % 4
                u, bu = ue[wi], b_ue[wi]
                a_, ba = acc[wi], b_acc[wi]
                S.op("pool", (lambda u, ch: (lambda e: e.tensor_copy(u[:, 0:2], c3[:, ch, :])))(u, ch),
                     [b_carry], [bu])
                act(u[:, 2:514], pb[:, :], AF.Copy, [bb], [bu])
                S.op("pool", (lambda u, ch: (lambda e: e.tensor_copy(c3[:, ch, :], u[:, 512:514])))(u, ch),
                     [bu], [b_carry])
                act(a_, pb[:, :], AF.Identity, [bb, b_cst], [ba], scale=wc3[:, ch, 2:3], bias=wc3[:, ch, 3:4])
                stt(a_, u[:, 1:513], wc3[:, ch, 1:2], a_, ALU.mult, ALU.add, [bu, ba, b_cst], [ba])
                stt(a_, u[:, 0:512], wc3[:, ch, 0:1], a_, ALU.mult, ALU.add, [bu, ba, b_cst], [ba])
                res.append((a_, ba))
            if mode == "full":
                s_, bs_ = sgb[j % 2], b_sg[j % 2]
                act(s_, res[1][0], AF.Silu, [res[1][1]], [bs_])
                tt("dve", a3[:, j, :], s_, res[0][0], ALU.mult, [bs_, res[0][1]], [b_actT[j]])
        if mode != "full":
            return

        dma("sp", rep[:, :], rep_d[3], [], [b_rep], b_rep)
        dma("sp", rep2[:, :], rep_d[4], [], [b_rep2], b_rep2)
        for cg in range(4):
            def lhs(k, blk):
                return a3[:, k, blk * P:(blk + 1) * P], b_actT[k]
            bks = tm_group(ti, "wd", NFF, lhs)
            for blk in range(NB):
                yb = Y[:, blk * D + cg * 512: blk * D + (cg + 1) * 512]
                stt(yb, yb, ALPHA, bks[blk][0][:, :], ALU.mult, ALU.add, [b_Y[blk], bks[blk][1]], [b_Y[blk]])
        ot = out_tile[0]
        out_tile[0] += 1
        S.barrier()
        for blk in range(NB):
            layer_norm2(blk)
            dma("sp", out_d[ot * T + blk * P: ot * T + (blk + 1) * P, :], Y[:, blk * D:(blk + 1) * D],
                [b_Y[blk]], [], b_Y[blk])

    for ti, m in enumerate(modes):
        do_tile(ti, m)

    S.emit(nc, final_waits=b_Y)
    es.close()
    return nc


def make_shared_inputs(w_in, w_gla_gate, b_gla_gate, g_ret, g_gla, w_out, ln1_g, ln1_b,
                       w_up, w_conv, b_conv, w_down, ln2_g, ln2_b):
    C = host_consts()
    w_in = np.asarray(w_in)[0]
    sh = {}
    sh["wp"] = build_pieces(w_in, np.asarray(w_out)[0], np.asarray(w_up)[0], np.asarray(w_down)[0])
    sh["agw"] = np.ascontiguousarray(
        w_in[:, AG:AG + 16].reshape(16, P, 16).transpose(1, 0, 2).reshape(P, 256))
    cst = np.zeros((P, 1796), np.float32)
    cst[:, 0:128] = C["ident"]
    cst[:, 128:640] = C["maskR"]
    cst[:, 640:896] = C["maskG"]
    cst[:, 896:1024] = C["Lmat"]
    cst[:, 1024:1536] = C["gq"]
    cst[:, 1536:1540] = C["gk"]
    cst[:, 1540:1796] = -0.5
    sh["cst"] = cst
    rep = np.empty((5, P, 2048), np.float32)
    rep[0] = np.concatenate([np.asarray(g_ret)[0], np.asarray(g_gla)[0]])[None, :]
    rep[1] = np.asarray(ln1_g)[0][None, :]
    rep[2] = np.asarray(ln1_b)[0][None, :]
    rep[3] = np.asarray(ln2_g)[0][None, :]
    rep[4] = np.asarray(ln2_b)[0][None, :]
    sh["rep"] = rep
    wc = np.concatenate([np.asarray(w_conv)[0], np.asarray(b_conv)], axis=0)
    sh["wconv"] = np.ascontiguousarray(wc.reshape(4, 88, P).transpose(2, 1, 0).reshape(P, 88 * 4))
    sh["wg"] = np.ascontiguousarray(np.asarray(w_gla_gate)[0])
    sh["lncol"] = np.ascontiguousarray(np.concatenate(
        [np.asarray(ln1_g)[0].reshape(16, P).T, np.asarray(ln1_b)[0].reshape(16, P).T], axis=1))
    bg = np.ones((1, 640), np.float32)
    bg[0, 0:512] = np.asarray(b_gla_gate)[0]
    sh["bg"] = bg
    return sh


_CACHE = {}


def kernel(x, w_in, w_gla_gate, b_gla_gate, g_ret, g_gla, w_out, ln1_g, ln1_b,
           w_up, w_conv, b_conv, w_down, ln2_g, ln2_b):
    x = np.asarray(x)
    Bn, Sn, _ = x.shape
    half = Sn // 2
    npre = half // T
    nmain = half // T
    modes = ["state"] * (npre - 1) + ["halo"] + ["full"] * nmain
    sh = make_shared_inputs(w_in, w_gla_gate, b_gla_gate, g_ret, g_gla, w_out, ln1_g, ln1_b,
                            w_up, w_conv, b_conv, w_down, ln2_g, ln2_b)
    key = tuple(modes)
    if key not in _CACHE:
        _CACHE[key] = build_program(modes)
    nc = _CACHE[key]
    in_maps = []
    for c in range(2 * Bn):
        b, hf = c // 2, c % 2
        if hf == 0:
            xs = np.concatenate([np.zeros((half, D), np.float32), x[b, :half]], axis=0)
            pos = np.concatenate([np.zeros(half), np.arange(half)])
        else:
            xs = x[b]
            pos = np.arange(Sn)
        cs, sn = rope_tables(pos)
        m = dict(sh)
        m["x"] = np.ascontiguousarray(xs)
        m["cos"] = cs
        m["sin"] = sn
        m["flag"] = np.full((P, 1), float(hf), np.float32)
        in_maps.append(m)
    res = run_bass_kernel_spmd(nc, in_maps, core_ids=list(range(2 * Bn)))
    out = np.empty((Bn, Sn, D), np.float32)
    for c in range(2 * Bn):
        b, hf = c // 2, c % 2
        out[b, hf * half:(hf + 1) * half] = res.results[c]["out"]
    return out
```

```python
from contextlib import ExitStack

import numpy as np
import concourse.bass as bass
import concourse.mybir as mybir
from concourse.bass_utils import run_bass_kernel_spmd

F32 = mybir.dt.float32
BF16 = mybir.dt.bfloat16
AF = mybir.ActivationFunctionType
ALU = mybir.AluOpType

P = 128
D = 2048
T = 512
NB = 4
DFF = 5632
NFF = DFF // P
ALPHA = 2.0 ** 0.25
EPS = 1e-5
NSLOT = 6
WINDOW = 6
QR, KR, VR, ZR, QG, KG, VG, ZG, AG = 0, 1024, 2048, 3072, 4096, 4608, 5120, 6144, 7168


class Buf:
    __slots__ = ("name", "w", "r", "semval", "const")

    def __init__(self, name, const=False):
        self.name = name
        self.w = None
        self.r = []
        self.semval = 0
        self.const = const


class Op:
    __slots__ = ("eng", "fn", "deps", "signal", "ticket", "dbuf", "dval", "idx")


class Sched:
    ENGS = ("pe", "act", "dve", "pool", "sp")

    def __init__(self):
        self.q = {e: [] for e in self.ENGS}
        self.dma_since_barrier = []
        self.pending_barrier = {e: [] for e in self.ENGS}
        self.dma_bufs = []

    def op(self, eng, fn, reads=(), writes=(), dbuf=None, extra=(), track=True):
        o = Op()
        o.eng = eng
        o.fn = fn
        o.signal = False
        o.ticket = 0
        o.dbuf = dbuf
        o.dval = 0
        o.idx = len(self.q[eng])
        if dbuf is not None:
            if dbuf.semval == 0 and dbuf not in self.dma_bufs:
                self.dma_bufs.append(dbuf)
            dbuf.semval += 16
            o.dval = dbuf.semval
        deps = []
        for b in reads:
            if b.w is not None:
                deps.append(b.w)
        for b in writes:
            if b.w is not None:
                deps.append(b.w)
            deps.extend(b.r)
        deps.extend(extra)
        if self.pending_barrier[eng]:
            deps.extend(self.pending_barrier[eng])
            self.pending_barrier[eng] = []
        o.deps = []
        seen = set()
        for d in deps:
            if d is o or id(d) in seen:
                continue
            seen.add(id(d))
            if d.dbuf is None and d.eng == eng:
                if eng == "pe":
                    continue
                if o.idx - d.idx > WINDOW:
                    continue
            if d.dbuf is None:
                d.signal = True
            o.deps.append(d)
        for b in reads:
            if not b.const:
                b.r.append(o)
        for b in writes:
            b.w = o
            b.r = []
        self.q[eng].append(o)
        if dbuf is not None and track:
            self.dma_since_barrier.append(o)
        return o

    def barrier(self):
        lasts = []
        for e in ("pe", "act", "dve", "pool"):
            for o in reversed(self.q[e]):
                if o.dbuf is None:
                    lasts.append(o)
                    break
        lasts.extend(self.dma_since_barrier)
        self.dma_since_barrier = []
        for e in ("act", "dve", "pool", "sp"):
            self.pending_barrier[e] = self.pending_barrier[e] + list(lasts)

    def emit(self, nc, final_waits):
        for e in self.ENGS:
            c = 0
            for o in self.q[e]:
                if o.dbuf is None and o.signal:
                    c += 1
                    o.ticket = c
        with ExitStack() as es:
            esem = {e: es.enter_context(nc.semaphore("s_" + e)) for e in ("pe", "act", "dve", "pool", "sp")}
            dsem = {}
            for b in self.dma_bufs:
                dsem[id(b)] = es.enter_context(nc.semaphore("d_" + b.name))
            block = es.enter_context(nc.Block())

            def run(ename, eng):
                seen = {}
                for o in self.q[ename]:
                    need = {}
                    for d in o.deps:
                        if d.dbuf is not None:
                            sem, val, key = dsem[id(d.dbuf)], d.dval, id(d.dbuf)
                        else:
                            sem, val, key = esem[d.eng], d.ticket, d.eng
                        if key not in need or need[key][1] < val:
                            need[key] = (sem, val)
                    for key, (sem, val) in need.items():
                        if seen.get(key, 0) < val:
                            eng.wait_ge(sem, val)
                            seen[key] = val
                    ins = o.fn(eng)
                    if o.dbuf is not None:
                        ins.then_inc(dsem[id(o.dbuf)], 16)
                    elif o.signal:
                        ins.then_inc(esem[ename], 1)
                if ename == "sp":
                    for b in final_waits:
                        if seen.get(id(b), 0) < b.semval:
                            eng.wait_ge(dsem[id(b)], b.semval)

            @block.tensor
            def _(e):
                run("pe", e)

            @block.scalar
            def _(e):
                run("act", e)

            @block.vector
            def _(e):
                run("dve", e)

            @block.gpsimd
            def _(e):
                run("pool", e)

            @block.sync
            def _(e):
                run("sp", e)


def _fm_piece(W, col0):
    K = W.shape[0] // P
    return W[:, col0:col0 + P].reshape(K, P, P).transpose(1, 0, 2).reshape(P, K * P)


def _tm_piece(W, kq, col0):
    return W[kq * 512:(kq + 1) * 512, col0:col0 + 512].reshape(4, P, 512).transpose(1, 0, 2).reshape(P, 2048)


def piece_plan():
    pl = []
    for hp in range(2):
        for h in (2 * hp, 2 * hp + 1):
            for c in range(2):
                pl.append(("qr", "fm_in", QR + h * 256 + c * P))
        for h in (2 * hp, 2 * hp + 1):
            for c in range(2):
                pl.append(("kr", "fm_in", KR + h * 256 + c * P))
        for kq in range(4):
            pl.append(("vr", "tm_in", (kq, VR + hp * 512)))
        for kq in range(4):
            pl.append(("zr", "tm_in", (kq, ZR + hp * 512)))
    for hp in range(2):
        for h in (2 * hp, 2 * hp + 1):
            pl.append(("qg", "fm_in", QG + h * P))
        for h in (2 * hp, 2 * hp + 1):
            pl.append(("kg", "fm_in", KG + h * P))
        for kq in range(4):
            pl.append(("vg", "tm_in", (kq, VG + hp * 512)))
        for kq in range(4):
            pl.append(("zg", "tm_in", (kq, ZG + hp * 512)))
    for cg in range(4):
        for kq in range(4):
            pl.append(("wo", "tm_out", (kq, cg * 512)))
    for j in range(NFF):
        pl.append(("uv", "fm_up", j * P))
        pl.append(("ug", "fm_up", DFF + j * P))
    for cg in range(4):
        for kq in range(NFF // 4):
            pl.append(("wd", "tm_down", (kq, cg * 512)))
    return pl


PLAN = piece_plan()
NPIECE = len(PLAN)
STATE_TAGS = ("kr", "vr", "kg", "vg")


def build_pieces(w_in, w_out, w_up, w_down):
    out = np.empty((NPIECE, P, 2048), np.float32)
    for i, (tag, kind, a) in enumerate(PLAN):
        if kind == "fm_in":
            out[i] = _fm_piece(w_in, a)
        elif kind == "tm_in":
            out[i] = _tm_piece(w_in, a[0], a[1])
        elif kind == "tm_out":
            out[i] = _tm_piece(w_out, a[0], a[1])
        elif kind == "fm_up":
            out[i] = _fm_piece(w_up, a)
        else:
            out[i] = _tm_piece(w_down, a[0], a[1])
    return out


def host_consts():
    gam = 1.0 - 2.0 ** (-5.0 - np.arange(4, dtype=np.float64))
    idx = np.arange(P)
    c = {}
    c["ident"] = np.eye(P, dtype=np.float32)
    mr = np.zeros((P, 4, P), np.float64)
    for h in range(4):
        dist = np.abs(idx[None, :] - idx[:, None])
        ok = (idx[:, None] // 64) <= (idx[None, :] // 64)
        mr[:, h, :] = np.where(ok, gam[h] ** dist / 16.0, 0.0)
    c["maskR"] = mr.astype(np.float32).reshape(P, 4 * P)
    mc = (idx[:, None] <= idx[None, :]).astype(np.float32)
    ma = ((idx[:, None] > idx[None, :]) & ((idx[:, None] // 64) == (idx[None, :] // 64))).astype(np.float32)
    c["maskG"] = np.concatenate([mc, ma], axis=1)
    c["Lmat"] = np.where(idx[:, None] <= idx[None, :], -1.0 / 16.0, 0.0).astype(np.float32)
    gq = np.zeros((P, 4, P), np.float64)
    for h in range(4):
        gq[:, h, :] = (gam[h] ** (idx + 1.0))[None, :]
    c["gq"] = gq.astype(np.float32).reshape(P, 4 * P)
    gk = np.zeros((P, 4), np.float64)
    for h in range(4):
        gk[:, h] = gam[h] ** (127.0 - idx) / 16.0
    c["gk"] = gk.astype(np.float32)
    c["g128"] = [float(g ** 128.0) for g in gam]
    return c


def rope_tables(pos):
    half = 128
    inv_freq = (np.float32(10000.0) ** (-(np.arange(half, dtype=np.float32)) / np.float32(half))).astype(np.float32)
    ang = (pos.astype(np.float32)[None, :] * inv_freq[:, None]).astype(np.float32)
    a64 = ang.astype(np.float64)
    return np.cos(a64).astype(np.float32), np.sin(a64).astype(np.float32)


def build_program(modes, dbg=None):
    NT = len(modes)
    NTOK = NT * T
    n_out_tiles = sum(1 for m in modes if m == "full")
    first_full = modes.index("full")
    C = host_consts()
    g128 = C["g128"]

    nc = bass.Bass("TRN2", target_bir_lowering=False)
    x_d = nc.dram_tensor("x", [NTOK, D], F32, kind="ExternalInput").ap()
    cos_d = nc.dram_tensor("cos", [P, NTOK], F32, kind="ExternalInput").ap()
    sin_d = nc.dram_tensor("sin", [P, NTOK], F32, kind="ExternalInput").ap()
    wp_d = nc.dram_tensor("wp", [NPIECE, P, 2048], F32, kind="ExternalInput").ap()
    agw_d = nc.dram_tensor("agw", [P, 256], F32, kind="ExternalInput").ap()
    cst_d = nc.dram_tensor("cst", [P, 1796], F32, kind="ExternalInput").ap()
    rep_d = nc.dram_tensor("rep", [5, P, 2048], F32, kind="ExternalInput").ap()
    wconv_d = nc.dram_tensor("wconv", [P, 88 * 4], F32, kind="ExternalInput").ap()
    wg_d = nc.dram_tensor("wg", [16, 512], F32, kind="ExternalInput").ap()
    bg_d = nc.dram_tensor("bg", [1, 640], F32, kind="ExternalInput").ap()
    flag_d = nc.dram_tensor("flag", [P, 1], F32, kind="ExternalInput").ap()
    lncol_d = nc.dram_tensor("lncol", [P, 32], F32, kind="ExternalInput").ap()
    out_d = nc.dram_tensor("out", [n_out_tiles * T, D], F32, kind="ExternalOutput").ap()
    wb_d = nc.dram_tensor("wbf", [NPIECE, P, 2048], BF16, kind="Internal").ap()
    dbg_d = None
    if dbg is not None:
        dbg_d = nc.dram_tensor("dbg", list(dbg), F32, kind="ExternalOutput").ap()

    S = Sched()
    es = ExitStack()

    def sb(name, shape, dt):
        return es.enter_context(nc.sbuf_tensor("s_" + name, shape, dt))

    cst = sb("cst", [P, 1796], F32)
    ident_f = cst[:, 0:128]
    maskR = cst[:, 128:640]
    maskG = cst[:, 640:896]
    Lmat = cst[:, 896:1024]
    gq = cst[:, 1024:1536]
    gk = cst[:, 1536:1540]
    neghalf = cst[:, 1540:1796]
    ident_b = sb("identb", [P, P], BF16)
    rep = sb("rep", [P, 2048], F32)
    rep2 = sb("rep2", [P, 2048], F32)
    wconv = sb("wconv", [P, 88 * 4], F32)
    wg = sb("wg", [16, 512], F32)
    bg = sb("bg", [1, 640], F32)
    agw_f = sb("agwf", [P, 256], F32)
    agw = sb("agw", [P, 256], BF16)
    flag = sb("flag", [P, 1], F32)
    lncol = sb("lncol", [P, 32], F32)
    Sr = sb("Sr", [P, 4 * 512], F32)
    Srb = sb("Srb", [P, 4 * 512], BF16)
    Sg = sb("Sg", [P, 4 * 256], F32)
    Sgb = sb("Sgb", [P, 4 * 256], BF16)
    carry = sb("carry", [P, 88 * 2], F32)
    Y = sb("Y", [P, NB * D], F32)
    wring = sb("wring", [P, NSLOT * 2048], BF16)
    small = sb("small", [P, 64], F32)
    UB = 99 * 1024
    U = sb("U", [P, UB // 2], BF16)

    class Carver:
        def __init__(self):
            self.off = 0

        def take(self, nbytes, dt):
            a = self.off
            self.off += (nbytes + 31) // 32 * 32
            assert self.off <= UB, (self.off, UB)
            ap = U[:, a // 2:(a + nbytes) // 2]
            return ap if dt == BF16 else ap.bitcast(F32)

    cv = Carver()
    x1T = cv.take(16 * 512 * 2, BF16)
    actT = cv.take(NFF * 512 * 2, BF16)
    ue = [cv.take(514 * 4, F32) for _ in range(4)]
    acc = [cv.take(512 * 4, F32) for _ in range(4)]
    sgb = [cv.take(512 * 4, F32) for _ in range(2)]
    bd_end = cv.off
    cv = Carver()
    xT = cv.take(16 * 512 * 2, BF16)
    cosb = cv.take(512 * 4, F32)
    sinb = cv.take(512 * 4, F32)
    qkb = [cv.take(512 * 2, BF16) for _ in range(12)]
    rt = [cv.take(512 * 4, F32) for _ in range(3)]
    vpair = cv.take(NB * 512 * 2, BF16)
    gz = cv.take(NB * 512 * 4, F32)
    spb = cv.take(NB * 512 * 4, F32)
    agT = cv.take(512 * 4, F32)
    Eb = [cv.take(512 * 4, F32) for _ in range(4)]
    e1 = cv.take(512 * 4, F32)
    sTb = [cv.take(256 * 2, BF16) for _ in range(4)]
    ktok = [cv.take(256 * 2, BF16) for _ in range(4)]
    onb = [cv.take(256 * 4, F32) for _ in range(2)]
    S1 = [cv.take(256 * 4, F32) for _ in range(2)]
    opair = cv.take(NB * 512 * 2, BF16)
    cv.off = max(cv.off, bd_end)
    oT = cv.take(16 * 512 * 2, BF16)

    ps = [es.enter_context(nc.psum_tensor("ps%d" % i, [P, 512], F32)) for i in range(8)]

    def B(name, const=False):
        return Buf(name, const)

    b_cst = B("cst", True)
    b_rep = B("rep")
    b_rep2 = B("rep2")
    b_ps = [B("ps%d" % i) for i in range(8)]
    b_slot = [B("slot%d" % i) for i in range(NSLOT)]
    b_Y = [B("Y%d" % i) for i in range(NB)]
    b_xT = B("xT")
    b_x1T = B("x1T")
    b_oT = [B("oT%d" % i) for i in range(16)]
    b_actT = [B("actT%d" % i) for i in range(NFF)]
    b_rope = B("rope")
    b_qk = [B("qk%d" % i) for i in range(12)]
    b_rt = [B("rt%d" % i) for i in range(3)]
    b_vp = [B("vp%d" % i) for i in range(NB)]
    b_gz = [B("gz%d" % i) for i in range(NB)]
    b_sp = [B("sp%d" % i) for i in range(NB)]
    b_agT = B("agT")
    b_E = [B("E%d" % i) for i in range(4)]
    b_e1 = B("e1")
    b_sT = [B("sT%d" % i) for i in range(4)]
    b_kt = [B("kt%d" % i) for i in range(4)]
    b_on = [B("on%d" % i) for i in range(2)]
    b_S1 = [B("S1%d" % i) for i in range(2)]
    b_op = [B("op%d" % i) for i in range(NB)]
    b_Sr = [B("Sr%d" % i) for i in range(4)]
    b_Srb = [B("Srb%d" % i) for i in range(4)]
    b_Sg = [B("Sg%d" % i) for i in range(4)]
    b_Sgb = [B("Sgb%d" % i) for i in range(4)]
    b_carry = B("carry")
    b_ue = [B("ue%d" % i) for i in range(4)]
    b_acc = [B("acc%d" % i) for i in range(4)]
    b_sg = [B("sg%d" % i) for i in range(2)]
    b_small = [B("small%d" % i) for i in range(8)]
    b_wb = B("wbf")
    b_xb = [B("xb%d" % i) for i in range(NB)]
    b_dbg = B("dbg")

    psn = [0]

    def bank():
        i = psn[0] % 8
        psn[0] += 1
        return ps[i], b_ps[i]

    def banks4():
        while psn[0] % 4 != 0:
            psn[0] += 1
        return [bank() for _ in range(4)]

    def mm(out, lhsT, rhs, start, stop, reads, writes):
        S.op("pe", lambda e: e.matmul(out, lhsT, rhs, start=start, stop=stop), reads, writes)

    def tp(out, in_, ident, reads, writes):
        S.op("pe", lambda e: e.transpose(out, in_, ident), reads + [b_cst], writes)

    def act(out, in_, func, reads, writes, scale=1.0, bias=0.0):
        S.op("act", lambda e: e.activation(out, in_, func, bias=bias, scale=scale), reads, writes)

    def tt(eng, out, a, b, op, reads, writes):
        S.op(eng, lambda e: e.tensor_tensor(out, a, b, op), reads, writes)

    def ts(eng, out, a, s1, s2, op0, op1, reads, writes):
        if s2 is None:
            S.op(eng, lambda e: e.tensor_scalar(out, a, s1, None, op0), reads, writes)
        else:
            S.op(eng, lambda e: e.tensor_scalar(out, a, s1, s2, op0, op1), reads, writes)

    def stt(out, a, s, b, op0, op1, reads, writes):
        S.op("dve", lambda e: e.scalar_tensor_tensor(out, a, s, b, op0, op1), reads, writes)

    def cp(eng, out, in_, reads, writes):
        S.op(eng, lambda e: e.tensor_copy(out, in_), reads, writes)

    def dma(eng, out, in_, reads, writes, dbuf):
        return S.op(eng, lambda e: e.dma_start(out=out, in_=in_), reads, writes, dbuf)

    b_cl = B("cload")
    for (dst, src) in ((cst[:, :], cst_d), (wconv[:, :], wconv_d), (wg[:, :], wg_d), (bg[:, :], bg_d),
                       (agw_f[:, :], agw_d), (flag[:, :], flag_d), (lncol[:, :], lncol_d)):
        dma("sp", dst, src, [], [b_cst], b_cl)
    S.op("dve", lambda e: e.tensor_copy(ident_b[:, :], ident_f), [b_cst], [b_cst])
    S.op("dve", lambda e: e.tensor_copy(agw[:, :], agw_f[:, :]), [b_cst], [b_cst])
    for t_, n_ in ((Sr, 2048), (Sg, 1024), (carry, 176)):
        S.op("pool", (lambda t_: (lambda e: e.memset(t_[:, :], 0.0)))(t_), [], [b_cst])
    for t_ in (Srb, Sgb):
        S.op("pool", (lambda t_: (lambda e: e.memset(t_[:, :], 0.0)))(t_), [], [b_cst])
    for blk in range(NB):
        S.op("pool", (lambda blk: (lambda e: e.dma_start(
            out=x3(oT, NB)[:, blk, :], in_=x_d[blk * P:(blk + 1) * P, :])))(blk),
            [], [b_oT[4 * blk + i] for i in range(4)], b_xb[blk])
    b_wb2 = B("wbf2")
    last_cast = {}
    cast_state = [i for i in range(NPIECE) if PLAN[i][0] in STATE_TAGS]
    cast_rest = [i for i in range(NPIECE) if PLAN[i][0] not in STATE_TAGS]
    n_state_tiles = sum(1 for m_ in modes if m_ == "state")

    def emit_casts(lst, grp):
        for i in lst:
            last_cast[grp] = S.op("pool", (lambda i: (lambda e: e.dma_start(out=wb_d[i], in_=wp_d[i])))(i), [], [],
                                  b_wb if grp == 0 else b_wb2, track=False)

    emit_casts(cast_state, 0)
    if n_state_tiles == 0:
        emit_casts(cast_rest, 1)
        cast_rest = []
    cast_chunk = (len(cast_rest) + max(n_state_tiles, 1) - 1) // max(n_state_tiles, 1)
    S.barrier()

    stream = []
    for ti, m in enumerate(modes):
        for pi, (tag, kind, a) in enumerate(PLAN):
            if m == "state" and tag not in STATE_TAGS:
                continue
            if m == "halo" and tag == "wd":
                continue
            stream.append((ti, pi, tag))
    wpos = {"issued": 0, "used": 0}

    def issue_loads(upto):
        while wpos["issued"] < min(upto, len(stream)):
            k = wpos["issued"]
            _, pi, _ = stream[k]
            sl = k % NSLOT
            grp = 0 if PLAN[pi][0] in STATE_TAGS else 1
            S.op("sp", (lambda sl, pi: (lambda e: e.dma_start(out=wring[:, sl * 2048:(sl + 1) * 2048], in_=wb_d[pi])))(sl, pi),
                 [], [b_slot[sl]], b_slot[sl], extra=[last_cast[grp]])
            wpos["issued"] += 1

    def next_piece(ti, tag):
        k = wpos["used"]
        assert stream[k][0] == ti and stream[k][2] == tag, (stream[k], ti, tag)
        issue_loads(k + NSLOT)
        wpos["used"] += 1
        sl = k % NSLOT
        return wring[:, sl * 2048:(sl + 1) * 2048], b_slot[sl]

    def fm_group(ti, tag, src, b_src, M=P, c0=0, c1=512):
        w, bw = next_piece(ti, tag)
        pb, bb = bank()
        for k in range(16):
            mm(pb[0:M, 0:c1 - c0], w[:, k * P:k * P + M], src[:, k * 512 + c0:k * 512 + c1], k == 0, k == 15,
               [bw, b_src], [bb])
        return pb, bb

    def tm_group(ti, tag, nk, lhs_fn, blks=range(NB)):
        bks = banks4()
        for kq in range(nk // 4):
            w, bw = next_piece(ti, tag)
            for kk in range(4):
                k = kq * 4 + kk
                for blk in blks:
                    l, bl = lhs_fn(k, blk)
                    mm(bks[blk][0][:, :], l, w[:, kk * 512:(kk + 1) * 512], k == 0, k == nk - 1,
                       [bw, bl], [bks[blk][1]])
        return bks

    def x3(ap, a):
        return ap.rearrange("p (a b) -> p a b", a=a)

    out_tile = [0]

    def transposes_to(dstT, b_dst, blks=range(NB)):
        d3 = x3(dstT, 16)
        for blk in blks:
            for j in range(4):
                pb, bb = bank()
                for c in range(4):
                    fc = 4 * j + c
                    tp(pb[:, c * P:(c + 1) * P], Y[:, blk * D + fc * P: blk * D + (fc + 1) * P], ident_f,
                       [b_Y[blk]], [bb])
                eng_copy = "act" if (j % 2 == 0) else "dve"
                o_ap = d3[:, 4 * j:4 * j + 4, blk * P:(blk + 1) * P]
                i_ap = x3(pb[:, :], 4)
                if eng_copy == "act":
                    act(o_ap, i_ap, AF.Copy, [bb], [b_dst])
                else:
                    S.op("dve", (lambda o_ap, i_ap: (lambda e: e.tensor_copy(o_ap, i_ap)))(o_ap, i_ap), [bb], [b_dst])

    def ln_stats(blk):
        st = small[:, 0:24]
        mv = small[:, 24:26]
        rs = small[:, 26:27]
        yb = Y[:, blk * D:(blk + 1) * D]
        for c in range(4):
            S.op("dve", (lambda c: (lambda e: e.bn_stats(st[:, c * 6:(c + 1) * 6], yb[:, c * 512:(c + 1) * 512])))(c),
                 [b_Y[blk]], [b_small[0]])
        S.op("dve", lambda e: e.bn_aggr(mv, st), [b_small[0]], [b_small[1]])
        ts("pool", rs, mv[:, 1:2], EPS, None, ALU.add, None, [b_small[1]], [b_small[2]])
        tt("pool", rs, rs, neghalf[:, 0:1], ALU.pow, [b_small[2], b_cst], [b_small[2]])
        return yb, mv, rs

    xbf = x3(oT, NB)

    def prefetch_x(ti):
        if ti >= NT:
            return
        tok0 = ti * T
        for blk in range(NB):
            S.op("pool", (lambda blk, tok0: (lambda e: e.dma_start(
                out=xbf[:, blk, :], in_=x_d[tok0 + blk * P: tok0 + (blk + 1) * P, :])))(blk, tok0),
                [], [b_oT[4 * blk + i] for i in range(4)], b_xb[blk])

    def transposes_x():
        d3 = x3(xT, 16)
        for blk in range(NB):
            for j in range(2):
                pb, bb = bank()
                pbb = pb[:, :].bitcast(BF16)
                for c in range(8):
                    fc = 8 * j + c
                    tp(pbb[:, c * P:(c + 1) * P], xbf[:, blk, fc * P:(fc + 1) * P], ident_b[:, :],
                       [b_oT[4 * blk + fc // 4]], [bb])
                act(d3[:, 8 * j:8 * j + 8, blk * P:(blk + 1) * P], x3(pbb[:, 0:1024], 8), AF.Copy, [bb], [b_xT])

    def transposes_x1(blks):
        d3 = x3(x1T, 16)
        for blk in blks:
            for j in range(4):
                pb, bb = bank()
                for c in range(4):
                    fc = 4 * j + c
                    tp(pb[:, c * P:(c + 1) * P], Y[:, blk * D + fc * P: blk * D + (fc + 1) * P], ident_f,
                       [b_Y[blk]], [bb])
                for c in range(4):
                    fc = 4 * j + c
                    act(d3[:, fc, blk * P:(blk + 1) * P], pb[:, c * P:(c + 1) * P], AF.Identity, [bb, b_cst], [b_x1T],
                        scale=lncol[:, fc:fc + 1], bias=lncol[:, 16 + fc:17 + fc])

    def layer_norm1(blk):
        yb, mv, rs = ln_stats(blk)
        ts("dve", yb, yb, mv[:, 0:1], rs, ALU.subtract, ALU.mult, [b_Y[blk], b_small[1], b_small[2]], [b_Y[blk]])

    def layer_norm2(blk):
        yb, mv, rs = ln_stats(blk)
        stt(yb, yb, mv[:, 0:1], rep[:, :], ALU.subtract, ALU.mult, [b_Y[blk], b_small[1], b_rep], [b_Y[blk]])
        stt(yb, yb, rs, rep2[:, :], ALU.mult, ALU.add, [b_Y[blk], b_small[2], b_rep2], [b_Y[blk]])

    def layer_norm(blk, si):
        st = small[:, 0:24]
        mv = small[:, 24:26]
        rs = small[:, 26:27]
        yb = Y[:, blk * D:(blk + 1) * D]
        for c in range(4):
            S.op("dve", (lambda c: (lambda e: e.bn_stats(st[:, c * 6:(c + 1) * 6], yb[:, c * 512:(c + 1) * 512])))(c),
                 [b_Y[blk]], [b_small[0]])
        S.op("dve", lambda e: e.bn_aggr(mv, st), [b_small[0]], [b_small[1]])
        ts("pool", rs, mv[:, 1:2], EPS, None, ALU.add, None, [b_small[1]], [b_small[2]])
        tt("pool", rs, rs, neghalf[:, 0:1], ALU.pow, [b_small[2], b_cst], [b_small[2]])
        ts("dve", yb, yb, mv[:, 0:1], rs, ALU.subtract, ALU.mult, [b_Y[blk], b_small[1], b_small[2]], [b_Y[blk]])
        tt("dve", yb, yb, rep[:, :], ALU.mult, [b_Y[blk], b_rep], [b_Y[blk]])
        tt("pool", yb, yb, rep2[:, :], ALU.add, [b_Y[blk], b_rep2], [b_Y[blk]])

    def head_norm(pb, bb, hl, blk, center, si):
        st = small[:, 32 + si * 16: 32 + si * 16 + 6]
        mv = small[:, 32 + si * 16 + 6: 32 + si * 16 + 8]
        rs = small[:, 32 + si * 16 + 8: 32 + si * 16 + 9]
        m2 = small[:, 32 + si * 16 + 9: 32 + si * 16 + 10]
        bs = b_small[3 + si]
        S.op("dve", lambda e: e.bn_stats(st, pb[:, 0:256]), [bb], [bs])
        S.op("dve", lambda e: e.bn_aggr(mv, st), [bs], [bs])
        if center:
            ts("pool", rs, mv[:, 1:2], EPS, None, ALU.add, None, [bs], [bs])
        else:
            tt("pool", m2, mv[:, 0:1], mv[:, 0:1], ALU.mult, [bs], [bs])
            ts("pool", rs, m2, EPS, mv[:, 1:2], ALU.add, ALU.add, [bs], [bs])
        tt("pool", rs, rs, neghalf[:, 0:1], ALU.pow, [bs, b_cst], [bs])
        on = onb[si]
        if center:
            ts("dve", on, pb[:, 0:256], mv[:, 0:1], rs, ALU.subtract, ALU.mult, [bb, bs], [b_on[si]])
        else:
            ts("dve", on, pb[:, 0:256], rs, None, ALU.mult, None, [bb, bs], [b_on[si]])
        g3 = x3(gz, NB)
        o3 = x3(opair, NB)
        tt("pool", o3[:, blk, hl * 256:(hl + 1) * 256], on, g3[:, blk, hl * 256:(hl + 1) * 256], ALU.mult,
           [b_on[si], b_gz[blk]], [b_op[blk]])

    def vz_groups(ti, vtag, ztag, gcol, full):
        xT3 = x3(xT, 16)

        def lhs(k, blk):
            return xT3[:, k, blk * P:(blk + 1) * P], b_xT
        bks = tm_group(ti, vtag, 16, lhs)
        v3 = x3(vpair, NB)
        for blk in range(NB):
            if blk % 2 == 0:
                act(v3[:, blk, :], bks[blk][0][:, :], AF.Copy, [bks[blk][1]], [b_vp[blk]])
            else:
                cp("dve", v3[:, blk, :], bks[blk][0][:, :], [bks[blk][1]], [b_vp[blk]])
        if full:
            bks = tm_group(ti, ztag, 16, lhs)
            g3 = x3(gz, NB)
            for blk in range(NB):
                act(g3[:, blk, :], bks[blk][0][:, :], AF.Silu, [bks[blk][1]], [b_gz[blk]])
                tt("pool", g3[:, blk, :], g3[:, blk, :], rep[:, gcol:gcol + 512], ALU.mult,
                   [b_gz[blk], b_rep], [b_gz[blk]])

    def opair_to_oT(fc0):
        o3 = x3(opair, NB)
        oT3 = x3(oT, 16)
        for c in range(4):
            pb, bb = bank()
            pbb = pb[:, :].bitcast(BF16)
            for blk in range(NB):
                tp(pbb[:, blk * P:(blk + 1) * P], o3[:, blk, c * P:(c + 1) * P], ident_b[:, :], [b_op[blk]], [bb])
            if c % 2 == 0:
                act(oT3[:, fc0 + c, :], pbb[:, 0:512], AF.Copy, [bb], [b_oT[fc0 + c]])
            else:
                S.op("dve", (lambda c, pbb: (lambda e: e.tensor_copy(oT3[:, fc0 + c, :], pbb[:, 0:512])))(c, pbb),
                     [bb], [b_oT[fc0 + c]])

    def do_tile(ti, mode):
        full = mode != "state"
        tok0 = ti * T
        if ti == 0 or modes[ti - 1] != "full":
            S.barrier()
        dma("sp", cosb, cos_d[:, tok0:tok0 + T], [], [b_rope], b_rope)
        dma("sp", sinb, sin_d[:, tok0:tok0 + T], [], [b_rope], b_rope)
        transposes_x()
        if not full:
            prefetch_x(ti + 1)
            emit_casts(cast_rest[:cast_chunk], 1)
            del cast_rest[:cast_chunk]

        pb, bb = bank()
        for k in range(16):
            mm(pb[0:16, :], agw[:, k * 16:(k + 1) * 16], xT[:, k * 512:(k + 1) * 512], k == 0, k == 15,
               [b_cst, b_xT], [bb])
        act(agT[0:16, :], pb[0:16, :], AF.Copy, [bb], [b_agT])
        sp3 = x3(spb, NB)
        for blk in range(NB):
            pb, bb = bank()
            mm(pb[:, :], agT[0:16, blk * P:(blk + 1) * P], wg[:, :], True, False, [b_agT, b_cst], [bb])
            mm(pb[:, :], bg[0:1, 512:640], bg[0:1, 0:512], False, True, [b_cst], [bb])
            act(e1, pb[:, :], AF.Exp, [bb], [b_e1], scale=-1.0)
            act(sp3[:, blk, :], e1, AF.Ln, [b_e1], [b_sp[blk]], bias=1.0)

        for hp in range(2):
            qb = {}
            for which in (("qr",) if full else ()) + ("kr",):
                if which == "qr":
                    pass
                for hl in range(2):
                    h = 2 * hp + hl
                    pb0, bb0 = fm_group(ti, which, xT, b_xT)
                    pb1, bb1 = fm_group(ti, which, xT, b_xT)
                    x1_, x2_ = pb0[:, :], pb1[:, :]
                    if which == "qr":
                        i0 = hl * 4
                        q0, q1, qd0, qd1 = qkb[i0], qkb[i0 + 1], qkb[i0 + 2], qkb[i0 + 3]
                        bq = b_qk[i0:i0 + 4]
                        gqb = gq[:, h * P:(h + 1) * P].unsqueeze(1).broadcast_to([P, NB, P])
                        tt("dve", rt[0], x1_, cosb, ALU.mult, [bb0, b_rope], [b_rt[0]])
                        tt("dve", rt[1], x2_, sinb, ALU.mult, [bb1, b_rope], [b_rt[1]])
                        tt("dve", rt[2], rt[0], rt[1], ALU.subtract, [b_rt[0], b_rt[1]], [b_rt[2]])
                        act(q0, rt[2], AF.Copy, [b_rt[2]], [bq[0]])
                        tt("dve", x3(qd0, NB), x3(rt[2], NB), gqb, ALU.mult, [b_rt[2], b_cst], [bq[2]])
                        tt("dve", rt[0], x1_, sinb, ALU.mult, [bb0, b_rope], [b_rt[0]])
                        tt("dve", rt[1], x2_, cosb, ALU.mult, [bb1, b_rope], [b_rt[1]])
                        tt("dve", rt[2], rt[0], rt[1], ALU.add, [b_rt[0], b_rt[1]], [b_rt[2]])
                        act(q1, rt[2], AF.Copy, [b_rt[2]], [bq[1]])
                        tt("dve", x3(qd1, NB), x3(rt[2], NB), gqb, ALU.mult, [b_rt[2], b_cst], [bq[3]])
                    else:
                        i0 = 8 + hl * 2
                        k0, k1 = qkb[i0], qkb[i0 + 1]
                        tt("dve", rt[0], x1_, cosb, ALU.mult, [bb0, b_rope], [b_rt[0]])
                        tt("dve", rt[1], x2_, sinb, ALU.mult, [bb1, b_rope], [b_rt[1]])
                        tt("dve", k0, rt[0], rt[1], ALU.subtract, [b_rt[0], b_rt[1]], [b_qk[i0]])
                        tt("dve", rt[0], x1_, sinb, ALU.mult, [bb0, b_rope], [b_rt[0]])
                        tt("dve", rt[1], x2_, cosb, ALU.mult, [bb1, b_rope], [b_rt[1]])
                        tt("dve", k1, rt[0], rt[1], ALU.add, [b_rt[0], b_rt[1]], [b_qk[i0 + 1]])
            if full and hp == 0:
                dma("sp", rep[:, :], rep_d[0], [], [b_rep], b_rep)
            vz_groups(ti, "vr", "zr", hp * 512, full)
            v3 = x3(vpair, NB)

            def ret_s1(blk, hl):
                h = 2 * hp + hl
                k0, k1 = qkb[8 + hl * 2], qkb[9 + hl * 2]
                bk = [b_qk[8 + hl * 2], b_qk[9 + hl * 2]]
                cs = slice(blk * P, (blk + 1) * P)
                bi = hl * 2 + blk % 2
                if full:
                    q0, q1 = qkb[hl * 4], qkb[hl * 4 + 1]
                    bq = b_qk[hl * 4:hl * 4 + 4]
                    pbs, bbs = bank()
                    mm(pbs[:, 0:P], k0[:, cs], q0[:, cs], True, False, [bk[0], bq[0]], [bbs])
                    mm(pbs[:, 0:P], k1[:, cs], q1[:, cs], False, True, [bk[1], bq[1]], [bbs])
                    tt("dve", sTb[bi][:, 0:P], pbs[:, 0:P], maskR[:, h * P:(h + 1) * P], ALU.mult,
                       [bbs, b_cst], [b_sT[bi]])
                pbk, bbk = bank()
                pkb = pbk[:, :].bitcast(BF16)
                tp(pkb[:, 0:P], k0[:, cs], ident_b[:, :], [bk[0]], [bbk])
                tp(pkb[:, P:2 * P], k1[:, cs], ident_b[:, :], [bk[1]], [bbk])
                act(ktok[bi], pkb[:, 0:256], AF.Identity, [bbk, b_cst], [b_kt[bi]], scale=gk[:, h:h + 1])

            def ret_s2(blk, hl):
                h = 2 * hp + hl
                cs = slice(blk * P, (blk + 1) * P)
                bi = hl * 2 + blk % 2
                vv = v3[:, blk, hl * 256:(hl + 1) * 256]
                pbd, bbd = bank()
                mm(pbd[:, 0:256], ktok[bi][:, 0:P], vv, True, True, [b_kt[bi], b_vp[blk]], [bbd])
                mm(pbd[:, 256:512], ktok[bi][:, P:2 * P], vv, True, True, [b_kt[bi], b_vp[blk]], [bbd])
                if full:
                    qd0, qd1 = qkb[hl * 4 + 2], qkb[hl * 4 + 3]
                    bq = b_qk[hl * 4:hl * 4 + 4]
                    pbo, bbo = bank()
                    mm(pbo[:, 0:256], sTb[bi][:, 0:P], vv, True, False, [b_sT[bi], b_vp[blk]], [bbo])
                    mm(pbo[:, 0:256], qd0[:, cs], Srb[:, h * 512:h * 512 + 256], False, False,
                       [bq[2], b_Srb[h]], [bbo])
                    mm(pbo[:, 0:256], qd1[:, cs], Srb[:, h * 512 + 256:h * 512 + 512], False, True,
                       [bq[3], b_Srb[h]], [bbo])
                sr = Sr[:, h * 512:(h + 1) * 512]
                stt(sr, sr, g128[h], pbd[:, :], ALU.mult, ALU.add, [b_Sr[h], bbd], [b_Sr[h]])
                act(Srb[:, h * 512:(h + 1) * 512], sr, AF.Copy, [b_Sr[h]], [b_Srb[h]])
                if full:
                    head_norm(pbo, bbo, hl, blk, True, hl)

            for step in range(NB + 1):
                if step < NB:
                    for hl in range(2):
                        ret_s1(step, hl)
                if step >= 1:
                    for hl in range(2):
                        ret_s2(step - 1, hl)
            if full:
                opair_to_oT(hp * 4)

        if full:
            for blk in ([NB - 1] if mode == "halo" else range(NB)):
                dma("sp", Y[:, blk * D:(blk + 1) * D], x_d[tok0 + blk * P: tok0 + (blk + 1) * P, :], [], [b_Y[blk]], b_Y[blk])
        for hp in range(2):
            qps = {}
            for which in (("qg",) if full else ()) + ("kg",):
                for hl in range(2):
                    qps[(which, hl)] = fm_group(ti, which, xT, b_xT)
            for hl in range(2):
                h = 2 * hp + hl
                pbB, bbB = bank()
                for blk in range(NB):
                    mm(pbB[:, blk * P:(blk + 1) * P], sp3[:, blk, h * P:(h + 1) * P], Lmat, True, True,
                       [b_sp[blk], b_cst], [bbB])
                Ep, En = Eb[hl * 2], Eb[hl * 2 + 1]
                act(Ep, pbB[:, :], AF.Exp, [bbB], [b_E[hl * 2]])
                act(En, pbB[:, :], AF.Exp, [bbB], [b_E[hl * 2 + 1]], scale=-1.0)
                i0 = hl * 4
                qP, qN, kP, kN = qkb[i0], qkb[i0 + 1], qkb[i0 + 2], qkb[i0 + 3]
                sc = 128.0 ** -0.5
                if full:
                    pq, bq_ = qps[("qg", hl)]
                    stt(qP, pq[:, :], sc, Ep, ALU.mult, ALU.mult, [bq_, b_E[hl * 2]], [b_qk[i0]])
                    stt(qN, pq[:, :], sc, En, ALU.mult, ALU.mult, [bq_, b_E[hl * 2 + 1]], [b_qk[i0 + 1]])
                pk, bk_ = qps[("kg", hl)]
                if full:
                    tt("dve", kP, pk[:, :], Ep, ALU.mult, [bk_, b_E[hl * 2]], [b_qk[i0 + 2]])
                tt("dve", kN, pk[:, :], En, ALU.mult, [bk_, b_E[hl * 2 + 1]], [b_qk[i0 + 3]])
            vz_groups(ti, "vg", "zg", 1024 + hp * 512, full)
            v3 = x3(vpair, NB)

            def gla_s1(blk, hl):
                i0 = hl * 4
                qP, qN, kP, kN = qkb[i0], qkb[i0 + 1], qkb[i0 + 2], qkb[i0 + 3]
                bqP, bqN, bkP, bkN = b_qk[i0:i0 + 4]
                cs = slice(blk * P, (blk + 1) * P)
                bi = hl * 2 + blk % 2
                if full:
                    pbs, bbs = bank()
                    mm(pbs[:, 0:P], kN[:, cs], qP[:, cs], True, True, [bkN, bqP], [bbs])
                    mm(pbs[:, P:2 * P], kP[:, cs], qN[:, cs], True, True, [bkP, bqN], [bbs])
                    tt("dve", sTb[bi], pbs[:, 0:256], maskG, ALU.mult, [bbs, b_cst], [b_sT[bi]])
                pbk, bbk = bank()
                pkb = pbk[:, :].bitcast(BF16)
                tp(pkb[:, 0:P], kN[:, cs], ident_b[:, :], [bkN], [bbk])
                act(ktok[bi][:, 0:P], pkb[:, 0:P], AF.Copy, [bbk], [b_kt[bi]])

            def gla_s2(blk, hl):
                h = 2 * hp + hl
                i0 = hl * 4
                qP = qkb[i0]
                bqP = b_qk[i0]
                Ep = Eb[hl * 2]
                vv = v3[:, blk, hl * 256:(hl + 1) * 256]
                cs = slice(blk * P, (blk + 1) * P)
                bi = hl * 2 + blk % 2
                sg_ = Sg[:, h * 256:(h + 1) * 256]
                e127 = Ep[:, blk * P + 127: blk * P + 128]
                pbd, bbd = bank()
                mm(pbd[:, 0:256], ktok[bi][:, 0:P], vv, True, True, [b_kt[bi], b_vp[blk]], [bbd])
                if full:
                    pbo, bbo = bank()
                    mm(pbo[:, 0:256], sTb[bi][:, 0:P], vv, True, False, [b_sT[bi], b_vp[blk]], [bbo])
                    mm(pbo[:, 0:256], sTb[bi][:, P:2 * P], vv, False, False, [b_sT[bi], b_vp[blk]], [bbo])
                    mm(pbo[:, 0:256], qP[:, cs], Sgb[:, h * 256:(h + 1) * 256], False, True,
                       [bqP, b_Sgb[h]], [bbo])
                ts("dve", S1[hl], sg_, e127, None, ALU.mult, None, [b_Sg[h], b_E[hl * 2]], [b_S1[hl]])
                stt(sg_, pbd[:, 0:256], e127, S1[hl], ALU.mult, ALU.add, [bbd, b_E[hl * 2], b_S1[hl], b_Sg[h]],
                    [b_Sg[h]])
                act(Sgb[:, h * 256:(h + 1) * 256], sg_, AF.Copy, [b_Sg[h]], [b_Sgb[h]])
                if full:
                    head_norm(pbo, bbo, hl, blk, False, hl)

            for step in range(NB + 1):
                if step < NB:
                    for hl in range(2):
                        gla_s1(step, hl)
                if step >= 1:
                    for hl in range(2):
                        gla_s2(step - 1, hl)
            if full:
                opair_to_oT(8 + hp * 4)
        if not full:
            return

        dma("sp", rep[:, :], rep_d[1], [], [b_rep], b_rep)
        dma("sp", rep2[:, :], rep_d[2], [], [b_rep2], b_rep2)
        oT3 = x3(oT, 16)
        blksB = [NB - 1] if mode == "halo" else list(range(NB))
        for cg in range(4):
            def lhs(k, blk):
                return oT3[:, k, blk * P:(blk + 1) * P], b_oT[k]
            bks = tm_group(ti, "wo", 16, lhs, blksB)
            for blk in blksB:
                yb = Y[:, blk * D + cg * 512: blk * D + (cg + 1) * 512]
                stt(yb, yb, ALPHA, bks[blk][0][:, :], ALU.mult, ALU.add, [b_Y[blk], bks[blk][1]], [b_Y[blk]])
        S.barrier()
        for blk in blksB:
            layer_norm1(blk)
        transposes_x1(blksB)
        prefetch_x(ti + 1)
        deferred = []
        if mode == "full":
            for blk in blksB:
                for cgi in range(4):
                    yc = Y[:, blk * D + cgi * 512: blk * D + (cgi + 1) * 512]
                    deferred.append((yc, rep[:, cgi * 512:(cgi + 1) * 512], ALU.mult, b_rep, blk))
                for cgi in range(4):
                    yc = Y[:, blk * D + cgi * 512: blk * D + (cgi + 1) * 512]
                    deferred.append((yc, rep2[:, cgi * 512:(cgi + 1) * 512], ALU.add, b_rep2, blk))

        if ti == first_full:
            ts("pool", carry[:, :], carry[:, :], flag[:, 0:1], None, ALU.mult, None, [b_carry, b_cst], [b_carry])
        a3 = x3(actT, NFF)
        wc3 = x3(wconv[:, :], 88)
        c3 = x3(carry[:, :], 88)
        if mode == "halo":
            for j in range(NFF):
                for half_, tag in ((0, "uv"), (1, "ug")):
                    ch = j + half_ * NFF
                    pb, bb = fm_group(ti, tag, x1T, b_x1T, c0=384, c1=512)
                    act(c3[:, ch, :], pb[:, 126:128], AF.Copy, [bb], [b_carry])
            return
        for j in range(NFF):
            if deferred:
                yc, rr, op_, brr, blk_ = deferred.pop(0)
                tt("pool", yc, yc, rr, op_, [b_Y[blk_], brr], [b_Y[blk_]])
            res = []
            for half_, tag in ((0, "uv"), (1, "ug")):
                ch = j + half_ * NFF
                pb, bb = fm_group(ti, tag, x1T, b_x1T)
                wi = (2 * j + half_) % 4
                u, bu = ue[wi], b_ue[wi]
                a_, ba = acc[wi], b_acc[wi]
                S.op("pool", (lambda u, ch: (lambda e: e.tensor_copy(u[:, 0:2], c3[:, ch, :])))(u, ch),
                     [b_carry], [bu])
                act(u[:, 2:514], pb[:, :], AF.Copy, [bb], [bu])
                S.op("pool", (lambda u, ch: (lambda e: e.tensor_copy(c3[:, ch, :], u[:, 512:514])))(u, ch),
                     [bu], [b_carry])
                act(a_, pb[:, :], AF.Identity, [bb, b_cst], [ba], scale=wc3[:, ch, 2:3], bias=wc3[:, ch, 3:4])
                stt(a_, u[:, 1:513], wc3[:, ch, 1:2], a_, ALU.mult, ALU.add, [bu, ba, b_cst], [ba])
                stt(a_, u[:, 0:512], wc3[:, ch, 0:1], a_, ALU.mult, ALU.add, [bu, ba, b_cst], [ba])
                res.append((a_, ba))
            if mode == "full":
                s_, bs_ = sgb[j % 2], b_sg[j % 2]
                act(s_, res[1][0], AF.Silu, [res[1][1]], [bs_])
                tt("dve", a3[:, j, :], s_, res[0][0], ALU.mult, [bs_, res[0][1]], [b_actT[j]])
        if mode != "full":
            return

        dma("sp", rep[:, :], rep_d[3], [], [b_rep], b_rep)
        dma("sp", rep2[:, :], rep_d[4], [], [b_rep2], b_rep2)
        for cg in range(4):
            def lhs(k, blk):
                return a3[:, k, blk * P:(blk + 1) * P], b_actT[k]
            bks = tm_group(ti, "wd", NFF, lhs)
            for blk in range(NB):
                yb = Y[:, blk * D + cg * 512: blk * D + (cg + 1) * 512]
                stt(yb, yb, ALPHA, bks[blk][0][:, :], ALU.mult, ALU.add, [b_Y[blk], bks[blk][1]], [b_Y[blk]])
        ot = out_tile[0]
        out_tile[0] += 1
        S.barrier()
        for blk in range(NB):
            layer_norm2(blk)
            dma("sp", out_d[ot * T + blk * P: ot * T + (blk + 1) * P, :], Y[:, blk * D:(blk + 1) * D],
                [b_Y[blk]], [], b_Y[blk])

    for ti, m in enumerate(modes):
        do_tile(ti, m)

    S.emit(nc, final_waits=b_Y)
    es.close()
    return nc


def make_shared_inputs(w_in, w_gla_gate, b_gla_gate, g_ret, g_gla, w_out, ln1_g, ln1_b,
                       w_up, w_conv, b_conv, w_down, ln2_g, ln2_b):
    C = host_consts()
    w_in = np.asarray(w_in)[0]
    sh = {}
    sh["wp"] = build_pieces(w_in, np.asarray(w_out)[0], np.asarray(w_up)[0], np.asarray(w_down)[0])
    sh["agw"] = np.ascontiguousarray(
        w_in[:, AG:AG + 16].reshape(16, P, 16).transpose(1, 0, 2).reshape(P, 256))
    cst = np.zeros((P, 1796), np.float32)
    cst[:, 0:128] = C["ident"]
    cst[:, 128:640] = C["maskR"]
    cst[:, 640:896] = C["maskG"]
    cst[:, 896:1024] = C["Lmat"]
    cst[:, 1024:1536] = C["gq"]
    cst[:, 1536:1540] = C["gk"]
    cst[:, 1540:1796] = -0.5
    sh["cst"] = cst
    rep = np.empty((5, P, 2048), np.float32)
    rep[0] = np.concatenate([np.asarray(g_ret)[0], np.asarray(g_gla)[0]])[None, :]
    rep[1] = np.asarray(ln1_g)[0][None, :]
    rep[2] = np.asarray(ln1_b)[0][None, :]
    rep[3] = np.asarray(ln2_g)[0][None, :]
    rep[4] = np.asarray(ln2_b)[0][None, :]
    sh["rep"] = rep
    wc = np.concatenate([np.asarray(w_conv)[0], np.asarray(b_conv)], axis=0)
    sh["wconv"] = np.ascontiguousarray(wc.reshape(4, 88, P).transpose(2, 1, 0).reshape(P, 88 * 4))
    sh["wg"] = np.ascontiguousarray(np.asarray(w_gla_gate)[0])
    sh["lncol"] = np.ascontiguousarray(np.concatenate(
        [np.asarray(ln1_g)[0].reshape(16, P).T, np.asarray(ln1_b)[0].reshape(16, P).T], axis=1))
    bg = np.ones((1, 640), np.float32)
    bg[0, 0:512] = np.asarray(b_gla_gate)[0]
    sh["bg"] = bg
    return sh


_CACHE = {}


def kernel(x, w_in, w_gla_gate, b_gla_gate, g_ret, g_gla, w_out, ln1_g, ln1_b,
           w_up, w_conv, b_conv, w_down, ln2_g, ln2_b):
    x = np.asarray(x)
    Bn, Sn, _ = x.shape
    half = Sn // 2
    npre = half // T
    nmain = half // T
    modes = ["state"] * (npre - 1) + ["halo"] + ["full"] * nmain
    sh = make_shared_inputs(w_in, w_gla_gate, b_gla_gate, g_ret, g_gla, w_out, ln1_g, ln1_b,
                            w_up, w_conv, b_conv, w_down, ln2_g, ln2_b)
    key = tuple(modes)
    if key not in _CACHE:
        _CACHE[key] = build_program(modes)
    nc = _CACHE[key]
    in_maps = []
    for c in range(2 * Bn):
        b, hf = c // 2, c % 2
        if hf == 0:
            xs = np.concatenate([np.zeros((half, D), np.float32), x[b, :half]], axis=0)
            pos = np.concatenate([np.zeros(half), np.arange(half)])
        else:
            xs = x[b]
            pos = np.arange(Sn)
        cs, sn = rope_tables(pos)
        m = dict(sh)
        m["x"] = np.ascontiguousarray(xs)
        m["cos"] = cs
        m["sin"] = sn
        m["flag"] = np.full((P, 1), float(hf), np.float32)
        in_maps.append(m)
    res = run_bass_kernel_spmd(nc, in_maps, core_ids=list(range(2 * Bn)))
    out = np.empty((Bn, Sn, D), np.float32)
    for c in range(2 * Bn):
        b, hf = c // 2, c % 2
        out[b, hf * half:(hf + 1) * half] = res.results[c]["out"]
    return out
```
